# Optimizing a Trainium2 kernel written in Bass

```python
import math
import jax, jax.numpy as jnp
from jax import lax
import numpy as np

D_MODEL = 1024
BATCH = 16
SEQ = 2048
DEPTH = 2

CTX_LEN = 256
GRID_W = 64
EPS = 1e-6
N_EVEN = (DEPTH + 1) // 2
N_ODD = DEPTH // 2
N_MOD = 6

HG_HEADS = 4
HG_DK = 128
HG_DV = 128
HG_WK = HG_HEADS * HG_DK
HG_WV = HG_HEADS * HG_DV
HG_CHUNK = 64
RG_W = 512
RG_BLOCKS = 8
RG_BW = RG_W // RG_BLOCKS
RG_C = 8.0
CONV_W = 4
EVEN_SPLITS = (HG_WK, HG_WK + HG_WV, 2 * HG_WK + HG_WV, 3 * HG_WK + HG_WV, 3 * HG_WK + 2 * HG_WV, 3 * HG_WK + 2 * HG_WV + RG_W)
EVEN_IN = 3 * HG_WK + 2 * HG_WV + 2 * RG_W
EVEN_OUT = HG_WV + RG_W

ATT_HQ = 12
ATT_HKV = 4
ATT_G = ATT_HQ // ATT_HKV
ATT_DH = 64
ATT_WQ = ATT_HQ * ATT_DH
ATT_WKV = ATT_HKV * ATT_DH
WINDOW = 128
ATT_BLOCK = 128
ROPE_BASE = 10000.0
FN_GROUPS = 4
FN_DIM = 64
FN_W = FN_GROUPS * FN_DIM
ODD_SPLITS = (ATT_WQ, ATT_WQ + ATT_WKV, ATT_WQ + 2 * ATT_WKV)
ODD_IN = ATT_WQ + 2 * ATT_WKV + FN_W
ODD_OUT = ATT_WQ + FN_W

PK_HEADS = 8
PK_NKEYS = 128
PK_EXPERTS = PK_NKEYS * PK_NKEYS
PK_DK = 256
PK_DKH = PK_DK // 2
PK_TOPK = 16
PK_CHUNK = 128

kernel_name = 'hybrid_hgrn2_rglru_swa_fnet_peer_block'


def rms_norm(x, g):
    xf = x.astype(jnp.float32)
    y = xf * lax.rsqrt(jnp.mean(xf * xf, axis=-1, keepdims=True) + EPS)
    return (y * g.astype(jnp.float32)).astype(x.dtype)


def modulate(h, shift, scale):
    return h * (1.0 + scale) + shift


def flip(a):
    return jnp.flip(a, axis=1)


def centred_dwconv(x, w, b):
    left = CONV_W // 2
    y = lax.conv_general_dilated(x, w.astype(x.dtype)[:, None, :], (1,), [(left, CONV_W - 1 - left)],
                                 dimension_numbers=('NWC', 'WIO', 'NWC'), feature_group_count=x.shape[-1])
    return y + b.astype(x.dtype)


def hgrn_gate(f_raw, lb):
    bsz, t_len, _ = f_raw.shape
    f = lb + (1.0 - lb) * jax.nn.sigmoid(f_raw.astype(jnp.float32))
    shp = (bsz, t_len, HG_HEADS, HG_DK)
    return (1.0 - f).reshape(shp), jnp.log(f).reshape(shp)


def hgrn_chunk_scan(q, k, v, log_f, s0):
    bsz, t_len, n_h, _ = q.shape
    n_chunks = t_len // HG_CHUNK

    def to_chunks(a):
        return a.reshape(bsz, n_chunks, HG_CHUNK, n_h, a.shape[-1]).transpose(1, 0, 3, 2, 4)

    lower = jnp.tril(jnp.ones((HG_CHUNK, HG_CHUNK), dtype=bool))[:, :, None]

    def step(state, inp):
        qc, kc, vc, gc = inp
        b = jnp.cumsum(gc, axis=2)
        o = jnp.einsum('bhtd,bhdv->bhtv', qc * jnp.exp(b), state)
        rel = jnp.where(lower, b[:, :, :, None, :] - b[:, :, None, :, :], -jnp.inf)
        att = jnp.einsum('bhtd,bhsd,bhtsd->bhts', qc, kc, jnp.exp(rel))
        o = o + jnp.einsum('bhts,bhsv->bhtv', att, vc)
        b_end = b[:, :, -1]
        state = jnp.exp(b_end)[..., None] * state + jnp.einsum(
            'bhsd,bhsv->bhdv', kc * jnp.exp(b_end[:, :, None] - b), vc)
        return state, o

    s_end, o = lax.scan(step, s0, (to_chunks(q), to_chunks(k), to_chunks(v), to_chunks(log_f)))
    return o.transpose(1, 0, 3, 2, 4).reshape(bsz, t_len, n_h, v.shape[-1]), s_end


def hgrn_bidir(q, v, k_f, lf_f, k_b, lf_b, s_f, s_b):
    o_f, s_f = hgrn_chunk_scan(q, k_f, v, lf_f, s_f)
    o_b, s_b = hgrn_chunk_scan(flip(q), flip(k_b), flip(v), flip(lf_b), s_b)
    return o_f + flip(o_b), s_f, s_b


def block_diag(x, w):
    bsz, t_len, _ = x.shape
    y = jnp.einsum('btnc,ncd->btnd', x.reshape(bsz, t_len, RG_BLOCKS, RG_BW), w)
    return y.reshape(bsz, t_len, RG_W)


def rglru_gates(xc, wa, ba, wx, bx, lam):
    r = jax.nn.sigmoid(block_diag(xc, wa) + ba)
    i = jax.nn.sigmoid(block_diag(xc, wx) + bx)
    log_a = -RG_C * r * jax.nn.softplus(-lam.astype(jnp.float32))
    u = jnp.sqrt(-jnp.expm1(2.0 * log_a)) * (i * xc)
    return log_a, u


def linear_scan(log_a, u, h0):
    a = jnp.exp(log_a)
    u = u.at[:, 0].add(a[:, 0] * h0)

    def combine(l, r):
        return l[0] * r[0], r[0] * l[1] + r[1]

    _, h = lax.associative_scan(combine, (a, u), axis=1)
    return h, h[:, -1]


def rglru_bidir(g_f, g_b, h_f, h_b):
    y_f, h_f = linear_scan(g_f[0], g_f[1], h_f)
    y_b, h_b = linear_scan(flip(g_b[0]), flip(g_b[1]), h_b)
    return y_f + flip(y_b), h_f, h_b


def even_mixer(hl, hc, w_in, w_out, lb, hg_gain, conv_w, conv_b, wa, ba, wx, bx, lam):
    def prepare(h):
        bsz, t_len, _ = h.shape
        q, i, f_fw, f_bw, g, xr, gate = jnp.split(h @ w_in, EVEN_SPLITS, axis=-1)
        q = jax.nn.silu(q.astype(jnp.float32)).reshape(bsz, t_len, HG_HEADS, HG_DK)
        v = i.astype(jnp.float32).reshape(bsz, t_len, HG_HEADS, HG_DV)
        k_f, lf_f = hgrn_gate(f_fw, lb[0])
        k_b, lf_b = hgrn_gate(f_bw, lb[1])
        xc = centred_dwconv(xr, conv_w, conv_b).astype(jnp.float32)
        g_f = rglru_gates(xc, wa[0], ba[0], wx[0], bx[0], lam[0])
        g_b = rglru_gates(xc, wa[1], ba[1], wx[1], bx[1], lam[1])
        return (q, v, k_f, lf_f, k_b, lf_b), (g_f, g_b), (g, gate)

    def merge(o_hg, h_rg, g, gate):
        bsz, t_len, _ = g.shape
        y_a = rms_norm(o_hg, hg_gain) * jax.nn.silu(g.astype(jnp.float32)).reshape(bsz, t_len, HG_HEADS, HG_DV)
        y_b = jax.nn.gelu(gate.astype(jnp.float32)) * h_rg
        y = jnp.concatenate([y_a.reshape(bsz, t_len, HG_WV), y_b], axis=-1)
        return y.astype(g.dtype) @ w_out

    hg_c, rg_c, gt_c = prepare(hc)
    hg_l, rg_l, gt_l = prepare(hl)
    bsz = hl.shape[0]
    s0 = jnp.zeros((bsz, HG_HEADS, HG_DK, HG_DV), jnp.float32)
    h0 = jnp.zeros((bsz, RG_W), jnp.float32)
    o_c, s_f, s_b = hgrn_bidir(*hg_c, s0, s0)
    r_c, h_f, h_b = rglru_bidir(*rg_c, h0, h0)
    o_l, _, _ = hgrn_bidir(*hg_l, s_f, s_b)
    r_l, _, _ = rglru_bidir(*rg_l, h_f, h_b)
    return merge(o_l, r_l, *gt_l), merge(o_c, r_c, *gt_c)


def axial_rope(t_len):
    rows = t_len // GRID_W
    row = jnp.repeat(jnp.arange(rows, dtype=jnp.float32), GRID_W)
    col = jnp.tile(jnp.arange(GRID_W, dtype=jnp.float32), rows)
    n_freq = ATT_DH // 4
    inv = ROPE_BASE ** (-jnp.arange(n_freq, dtype=jnp.float32) / n_freq)
    ang = jnp.concatenate([row[:, None] * inv, col[:, None] * inv], axis=-1)
    return jnp.cos(ang), jnp.sin(ang)


def apply_rope(x, cos, sin):
    xf = x.astype(jnp.float32)
    x1, x2 = jnp.split(xf, 2, axis=-1)
    cs, sn = cos[None, :, None, :], sin[None, :, None, :]
    return jnp.concatenate([x1 * cs - x2 * sn, x2 * cs + x1 * sn], axis=-1).astype(x.dtype)


def sink_attention(q, k, v, sink, mask):
    s = jnp.einsum('bqhgd,bkhd->bhgqk', q, k).astype(jnp.float32) * (ATT_DH ** -0.5)
    if mask is not None:
        s = jnp.where(mask, s, -jnp.inf)
    sk = jnp.broadcast_to(sink.astype(jnp.float32)[None, :, :, None, None], s.shape[:-1] + (1,))
    p = jax.nn.softmax(jnp.concatenate([sk, s], axis=-1), axis=-1)[..., 1:]
    return jnp.einsum('bhgqk,bkhd->bqhgd', p.astype(v.dtype), v)


def windowed_attention(q, k, v, kc, vc, sink):
    bsz, t_len = q.shape[:2]
    n_blk = t_len // ATT_BLOCK
    n_ctx = kc.shape[1]
    pad = ((0, 0), (ATT_BLOCK, ATT_BLOCK), (0, 0), (0, 0))
    kp, vp = jnp.pad(k, pad), jnp.pad(v, pad)
    qb = q.reshape(bsz, n_blk, ATT_BLOCK, ATT_HKV, ATT_G, ATT_DH).transpose(1, 0, 2, 3, 4, 5)
    offs_q = jnp.arange(ATT_BLOCK)
    offs_k = jnp.arange(3 * ATT_BLOCK)
    ctx_mask = jnp.ones((ATT_BLOCK, n_ctx), dtype=bool)

    def block(args):
        n, qn = args
        kn = lax.dynamic_slice_in_dim(kp, n * ATT_BLOCK, 3 * ATT_BLOCK, axis=1)
        vn = lax.dynamic_slice_in_dim(vp, n * ATT_BLOCK, 3 * ATT_BLOCK, axis=1)
        qpos = n * ATT_BLOCK + offs_q
        kpos = (n - 1) * ATT_BLOCK + offs_k
        local = (jnp.abs(qpos[:, None] - kpos[None, :]) <= WINDOW) & (kpos >= 0)[None, :] & (kpos < t_len)[None, :]
        mask = jnp.concatenate([ctx_mask, local], axis=1)
        return sink_attention(qn, jnp.concatenate([kc, kn], axis=1), jnp.concatenate([vc, vn], axis=1), sink, mask)

    o = lax.map(block, (jnp.arange(n_blk), qb))
    return o.transpose(1, 0, 2, 3, 4, 5).reshape(bsz, t_len, ATT_WQ)


def fourier_mix(z):
    bsz, t_len, _ = z.shape
    zz = z.astype(jnp.float32).reshape(bsz, t_len, FN_GROUPS, FN_DIM)
    y = jnp.fft.fft2(zz, axes=(1, 3), norm='ortho').real
    return y.reshape(bsz, t_len, FN_W).astype(z.dtype)


def odd_mixer(hl, hc, w_in, w_out, q_gain, k_gain, sink, need_ctx):
    bsz, t_len, _ = hl.shape
    n_ctx = hc.shape[1]
    ql, kl, vl, fl = jnp.split(hl @ w_in, ODD_SPLITS, axis=-1)
    cos, sin = axial_rope(t_len)
    ql = apply_rope(rms_norm(ql.reshape(bsz, t_len, ATT_HQ, ATT_DH), q_gain), cos, sin)
    kl = apply_rope(rms_norm(kl.reshape(bsz, t_len, ATT_HKV, ATT_DH), k_gain), cos, sin)
    vl = vl.reshape(bsz, t_len, ATT_HKV, ATT_DH)
    kc, vc = jnp.split(hc @ w_in[:, ATT_WQ:ATT_WQ + 2 * ATT_WKV], 2, axis=-1)
    kc = rms_norm(kc.reshape(bsz, n_ctx, ATT_HKV, ATT_DH), k_gain)
    vc = vc.reshape(bsz, n_ctx, ATT_HKV, ATT_DH)
    sink_g = sink.reshape(ATT_HKV, ATT_G)
    a_l = windowed_attention(ql, kl, vl, kc, vc, sink_g)
    y_l = jnp.concatenate([a_l, fourier_mix(fl)], axis=-1) @ w_out
    if not need_ctx:
        return y_l, None
    qc = rms_norm((hc @ w_in[:, :ATT_WQ]).reshape(bsz, n_ctx, ATT_HKV, ATT_G, ATT_DH), q_gain)
    fc = hc @ w_in[:, ATT_WQ + 2 * ATT_WKV:]
    a_c = sink_attention(qc, kc, vc, sink_g, None).reshape(bsz, n_ctx, ATT_WQ)
    y_c = jnp.concatenate([a_c, fourier_mix(fc)], axis=-1) @ w_out
    return y_l, y_c


def peer(h, w_q, b_q, sub_keys, u, v):
    n_tok, d = h.shape

    def chunk(xc):
        q = (xc @ w_q + b_q).astype(jnp.float32).reshape(PK_CHUNK, PK_HEADS, 2, PK_DKH)
        s = jnp.einsum('thpd,hpkd->thpk', q, sub_keys.astype(jnp.float32))
        sv, si = lax.top_k(s, PK_TOPK)
        cand = (sv[:, :, 0, :, None] + sv[:, :, 1, None, :]).reshape(PK_CHUNK, PK_HEADS, PK_TOPK * PK_TOPK)
        cidx = (si[:, :, 0, :, None] * PK_NKEYS + si[:, :, 1, None, :]).reshape(PK_CHUNK, PK_HEADS, PK_TOPK * PK_TOPK)
        best, pos = lax.top_k(cand, PK_TOPK)
        eidx = jnp.take_along_axis(cidx, pos, axis=-1)
        gate = jax.nn.softmax(best, axis=-1)
        ue = jnp.take(u, eidx, axis=0)
        ve = jnp.take(v, eidx, axis=0)
        act = jax.nn.gelu(jnp.einsum('td,thkd->thk', xc, ue).astype(jnp.float32)) * gate
        return jnp.einsum('thk,thkd->td', act.astype(xc.dtype), ve)

    out = lax.map(chunk, h.reshape(n_tok // PK_CHUNK, PK_CHUNK, d))
    return out.reshape(n_tok, d)


def setup_inputs(seed: int = 0) -> dict:
    key = jax.random.key(seed)
    keys = iter(jax.random.split(key, 32))
    D = D_MODEL

    def nrm(shape, std):
        return std * jax.random.normal(next(keys), shape, jnp.float32)

    x = nrm((BATCH, SEQ, D), 1.0)
    c = nrm((BATCH, D), 1.0)
    ctx = nrm((BATCH, CTX_LEN, D), 1.0)
    c_ctx = nrm((D,), 1.0)
    w_mod = nrm((DEPTH, D, N_MOD * D), 0.5 * D ** -0.5)
    b_mod = nrm((DEPTH, N_MOD * D), 0.02)
    g_mix = 1.0 + nrm((DEPTH, D), 0.02)
    g_ffn = 1.0 + nrm((DEPTH, D), 0.02)
    e_w_in = nrm((N_EVEN, D, EVEN_IN), D ** -0.5)
    e_w_out = nrm((N_EVEN, EVEN_OUT, D), EVEN_OUT ** -0.5)
    hg_lb_logits = nrm((2, N_EVEN + 1, HG_WK), 0.1)
    hg_gain = 1.0 + nrm((N_EVEN, HG_DV), 0.02)
    rg_conv_w = nrm((N_EVEN, CONV_W, RG_W), CONV_W ** -0.5)
    rg_conv_b = nrm((N_EVEN, RG_W), 0.02)
    rg_wa = nrm((N_EVEN, 2, RG_BLOCKS, RG_BW, RG_BW), RG_BW ** -0.5)
    rg_ba = nrm((N_EVEN, 2, RG_W), 0.02)
    rg_wx = nrm((N_EVEN, 2, RG_BLOCKS, RG_BW, RG_BW), RG_BW ** -0.5)
    rg_bx = nrm((N_EVEN, 2, RG_W), 0.02)
    a0 = jax.random.uniform(next(keys), (N_EVEN, 2, RG_W), jnp.float32, 0.9, 0.999)
    rg_lambda = jnp.log(a0) - jnp.log1p(-a0)
    o_w_in = nrm((N_ODD, D, ODD_IN), D ** -0.5)
    o_w_out = nrm((N_ODD, ODD_OUT, D), ODD_OUT ** -0.5)
    q_gain = 1.0 + nrm((N_ODD, ATT_DH), 0.02)
    k_gain = 1.0 + nrm((N_ODD, ATT_DH), 0.02)
    sinks = nrm((N_ODD, ATT_HQ), 0.5)
    p_wq = nrm((DEPTH, D, PK_HEADS * PK_DK), D ** -0.5)
    p_bq = nrm((DEPTH, PK_HEADS * PK_DK), 0.02)
    p_keys = nrm((DEPTH, PK_HEADS, 2, PK_NKEYS, PK_DKH), PK_DKH ** -0.5)
    p_u = nrm((DEPTH, PK_EXPERTS, D), D ** -0.5)
    p_v = nrm((DEPTH, PK_EXPERTS, D), 0.5)
    return {'x': x, 'c': c, 'ctx': ctx, 'c_ctx': c_ctx, 'w_mod': w_mod, 'b_mod': b_mod,
            'g_mix': g_mix, 'g_ffn': g_ffn, 'e_w_in': e_w_in, 'e_w_out': e_w_out,
            'hg_lb_logits': hg_lb_logits, 'hg_gain': hg_gain, 'rg_conv_w': rg_conv_w, 'rg_conv_b': rg_conv_b,
            'rg_wa': rg_wa, 'rg_ba': rg_ba, 'rg_wx': rg_wx, 'rg_bx': rg_bx, 'rg_lambda': rg_lambda,
            'o_w_in': o_w_in, 'o_w_out': o_w_out, 'q_gain': q_gain, 'k_gain': k_gain, 'sinks': sinks,
            'p_wq': p_wq, 'p_bq': p_bq, 'p_keys': p_keys, 'p_u': p_u, 'p_v': p_v}


def reference(x, c, ctx, c_ctx, w_mod, b_mod, g_mix, g_ffn, e_w_in, e_w_out, hg_lb_logits, hg_gain,
              rg_conv_w, rg_conv_b, rg_wa, rg_ba, rg_wx, rg_bx, rg_lambda, o_w_in, o_w_out, q_gain, k_gain,
              sinks, p_wq, p_bq, p_keys, p_u, p_v):
    lb_all = jnp.cumsum(jax.nn.softmax(hg_lb_logits.astype(jnp.float32), axis=1), axis=1)
    lat, cx = x, ctx
    for l in range(DEPTH):
        last = l == DEPTH - 1
        j = l // 2
        m_lat = jnp.split((jax.nn.silu(c) @ w_mod[l] + b_mod[l])[:, None, :], N_MOD, axis=-1)
        m_ctx = jnp.split(jax.nn.silu(c_ctx) @ w_mod[l] + b_mod[l], N_MOD, axis=-1)
        hl = modulate(rms_norm(lat, g_mix[l]), m_lat[0], m_lat[1])
        hc = modulate(rms_norm(cx, g_mix[l]), m_ctx[0], m_ctx[1])
        if l % 2 == 0:
            yl, yc = even_mixer(hl, hc, e_w_in[j], e_w_out[j], lb_all[:, j], hg_gain[j], rg_conv_w[j], rg_conv_b[j],
                                rg_wa[j], rg_ba[j], rg_wx[j], rg_bx[j], rg_lambda[j])
        else:
            yl, yc = odd_mixer(hl, hc, o_w_in[j], o_w_out[j], q_gain[j], k_gain[j], sinks[j], not last)
        lat = lat + m_lat[2] * yl
        hl2 = modulate(rms_norm(lat, g_ffn[l]), m_lat[3], m_lat[4])
        n_lat = hl2.shape[0] * hl2.shape[1]
        if last:
            ff = peer(hl2.reshape(n_lat, hl2.shape[-1]), p_wq[l], p_bq[l], p_keys[l], p_u[l], p_v[l])
            lat = lat + m_lat[5] * ff.reshape(lat.shape)
        else:
            cx = cx + m_ctx[2] * yc
            hc2 = modulate(rms_norm(cx, g_ffn[l]), m_ctx[3], m_ctx[4])
            tokens = jnp.concatenate([hl2.reshape(n_lat, hl2.shape[-1]), hc2.reshape(-1, hc2.shape[-1])], axis=0)
            ff = peer(tokens, p_wq[l], p_bq[l], p_keys[l], p_u[l], p_v[l])
            lat = lat + m_lat[5] * ff[:n_lat].reshape(lat.shape)
            cx = cx + m_ctx[5] * ff[n_lat:].reshape(cx.shape)
    return lat
```

```python
import numpy as np
from contextlib import ExitStack
import concourse.bass as bass
import concourse.mybir as mybir
from concourse.bass_utils import run_bass_kernel_spmd

F32 = mybir.dt.float32
BF16 = mybir.dt.bfloat16
I32 = mybir.dt.int32
U32 = mybir.dt.uint32
ALU = mybir.AluOpType
AF = mybir.ActivationFunctionType
AX = mybir.AxisListType
AP = bass.AP

NDS = 24
EPOCH = 30000


class KB:
    def __init__(self, nc):
        self.nc = nc
        self.es = ExitStack()
        self.engs = {'pe': nc.tensor, 'dve': nc.vector, 'act': nc.scalar, 'pool': nc.gpsimd, 'sp': nc.sync}
        self.csem = {}
        self.ccnt = {}
        self.nsem = 0
        for e in self.engs:
            self._new_epoch(e)
        self.dsem = [self.es.enter_context(nc.semaphore(f"dma{i}")) for i in range(NDS)]
        self.dcount = [0] * NDS
        self.ndma = 0
        self.seen = {e: {} for e in self.engs}
        self.lastw = {}
        self.readers = {}
        self.stage_stack = []
        self.n_inst = 0
        self.n_wait = 0

    def _new_epoch(self, e):
        self.nsem += 1
        self.csem[e] = (f"c{e}{self.nsem}", self.es.enter_context(self.nc.semaphore(f"c_{e}_{self.nsem}")))
        self.ccnt[e] = 0

    def sb(self, name, shape, dtype, stack=None):
        st = stack if stack is not None else (self.stage_stack[-1] if self.stage_stack else self.es)
        self.uid = getattr(self, "uid", 0) + 1
        return st.enter_context(self.nc.sbuf_tensor(f"sb{self.uid}_{name}", list(shape), dtype))

    def ps(self, name, shape, dtype, stack=None):
        st = stack if stack is not None else (self.stage_stack[-1] if self.stage_stack else self.es)
        self.uid = getattr(self, "uid", 0) + 1
        return st.enter_context(self.nc.psum_tensor(f"ps{self.uid}_{name}", list(shape), dtype))

    def dram(self, name, shape, dtype, kind="Internal"):
        return self.nc.dram_tensor(name, list(shape), dtype, kind=kind)

    def _need(self, eng, tok, waits):
        if tok is None:
            return
        key, sem, cnt, teng = tok
        if teng == 'pe' and eng == 'pe':
            return
        if self.seen[eng].get(key, 0) >= cnt:
            return
        if key not in waits or waits[key][1] < cnt:
            waits[key] = (sem, cnt)

    def _collect(self, eng, reads, writes):
        waits = {}
        for r in reads:
            self._need(eng, self.lastw.get(r), waits)
        for w in writes:
            self._need(eng, self.lastw.get(w), waits)
            for t in self.readers.get(w, {}).values():
                self._need(eng, t, waits)
        for key, (sem, cnt) in waits.items():
            self.engs[eng].wait_ge(sem, cnt)
            self.seen[eng][key] = cnt
            self.n_wait += 1

    def _record(self, tok, reads, writes):
        for r in reads:
            self.readers.setdefault(r, {})[tok[0] if tok[3] == 'dma' else tok[3]] = tok
        for w in writes:
            self.lastw[w] = tok
            self.readers[w] = {}

    def op(self, eng, fn, reads=(), writes=()):
        self._collect(eng, reads, writes)
        inst = fn(self.engs[eng])
        if self.ccnt[eng] >= EPOCH:
            self._new_epoch(eng)
        key, sem = self.csem[eng]
        inst.then_inc(sem, 1)
        self.ccnt[eng] += 1
        tok = (key, sem, self.ccnt[eng], eng)
        self._record(tok, reads, writes)
        self.n_inst += 1
        return tok

    def dma(self, out, in_, reads=(), writes=(), q='sp', fn=None, **kw):
        i = self.ndma % NDS
        sem = self.dsem[i]
        self._collect(q, reads, writes)
        if self.dcount[i] > 0 and self.seen[q].get(f"d{i}", 0) < 16 * self.dcount[i]:
            self.engs[q].wait_ge(sem, 16 * self.dcount[i])
            self.seen[q][f"d{i}"] = 16 * self.dcount[i]
        if fn is not None:
            inst = fn(self.engs[q])
        else:
            inst = self.engs[q].dma_start(out=out, in_=in_, **kw)
        inst.then_inc(sem, 16)
        self.dcount[i] += 1
        self.ndma += 1
        tok = (f"d{i}", sem, 16 * self.dcount[i], 'dma')
        self._record(tok, reads, writes)
        self.n_inst += 1
        return tok

    def barrier(self):
        for e in self.engs:
            for f in self.engs:
                if f == e:
                    continue
                key, sem = self.csem[f]
                c = self.ccnt[f]
                if c > 0 and self.seen[e].get(key, 0) < c:
                    self.engs[e].wait_ge(sem, c)
                    self.seen[e][key] = c
            for i in range(NDS):
                c = 16 * self.dcount[i]
                if c > 0 and self.seen[e].get(f"d{i}", 0) < c:
                    self.engs[e].wait_ge(self.dsem[i], c)
                    self.seen[e][f"d{i}"] = c
        self.lastw = {}
        self.readers = {}

    def final_wait(self, toks, eng='sp'):
        for tok in toks:
            key, sem, cnt, _ = tok
            self.engs[eng].wait_ge(sem, cnt)

    class _Stage:
        def __init__(self, kb):
            self.kb = kb

        def __enter__(self):
            st = ExitStack()
            self.kb.stage_stack.append(st)
            return st

        def __exit__(self, *a):
            self.kb.barrier()
            st = self.kb.stage_stack.pop()
            st.close()

    def stage(self):
        return KB._Stage(self)

    def close(self):
        self.es.close()


def bc(ap, shape_pairs, offset_add=0):
    return AP(ap.tensor, ap.offset + offset_add, [list(p) for p in shape_pairs])


D = 1024
NB = 2
TC = 256
TL = 2048
TS = TC + TL
NT = NB * TS
NTILE = NT // 128
EPS = 1e-6


def tile_info(t):
    b = t // (TS // 128)
    r = t % (TS // 128)
    return b, r < (TC // 128)


def mod_col(t):
    b, isc = tile_info(t)
    return 2 if isc else b


class Ctx:
    def dump(self, kb, name, ap, key, dtype=F32):
        if name not in self.dbg or hasattr(self, "_d_" + name):
            return
        t = kb.nc.dram_tensor(name, list(ap.shape), dtype, kind="ExternalOutput").ap()
        setattr(self, "_d_" + name, t)
        kb.dma(t, ap, reads=[key], writes=["dbg_" + name])


def stage_mod(kb, g):
    with kb.stage() as st:
        cT = kb.sb("m_cT", [128, 8, 3], F32)
        sc = kb.sb("m_sc", [128, 8, 3], F32)
        kb.dma(cT[:], g.cT[:], writes=["m_cT"])
        kb.op('act', lambda e: e.activation(out=sc[:], in_=cT[:], func=AF.Silu), reads=["m_cT"], writes=["m_sc"])
        wb = [kb.sb(f"m_w{i}", [128, 8, 512], F32) for i in range(2)]
        bb = [kb.sb(f"m_b{i}", [3, 512], F32) for i in range(2)]
        ob = [kb.sb(f"m_o{i}", [3, 512], F32) for i in range(2)]
        pss = [kb.ps(f"m_ps{i}", [3, 512], F32) for i in range(2)]
        it = 0
        for l in range(2):
            for cb in range(12):
                i = it % 2
                it += 1
                kb.dma(wb[i][:], g.w_mod[l, :, cb * 512:(cb + 1) * 512].rearrange("(k p) n -> p k n", p=128),
                       writes=[f"m_w{i}"])
                kb.dma(bb[i][:], bc(g.b_mod[l, cb * 512:(cb + 1) * 512], [[0, 3], [1, 512]]), writes=[f"m_b{i}"])
                for k in range(8):
                    kb.op('pe', lambda e, k=k, i=i: e.matmul(pss[i][:], lhsT=sc[:, k, :], rhs=wb[i][:, k, :],
                                                             start=(k == 0), stop=(k == 7)),
                          reads=["m_sc", f"m_w{i}"], writes=[f"m_ps{i}"])
                kb.op('dve', lambda e, i=i: e.tensor_tensor(out=ob[i][:], in0=pss[i][:], in1=bb[i][:], op=ALU.add),
                      reads=[f"m_ps{i}", f"m_b{i}"], writes=[f"m_o{i}"])
                kb.dma(g.mrow[l, :, cb * 512:(cb + 1) * 512], ob[i][:], reads=[f"m_o{i}"], writes=["mrow"])


def alloc_globals(kb, g):
    g.AB = kb.sb("AB", [128, 2 * 2 * 3 * 2 * 8], F32, stack=kb.es)
    g.lbc = kb.sb("lbc", [128, 2, 2, 4], F32, stack=kb.es)
    g.ident = kb.sb("ident", [128, 128], BF16, stack=kb.es)
    g.idf = kb.sb("idf", [128, 128], F32, stack=kb.es)


def stage_prep(kb, g):
    kb.dma(g.idf[:], g.identf[:], writes=["idf"])
    kb.op('dve', lambda e: e.tensor_copy(out=g.ident[:], in_=g.idf[:]), reads=["idf"], writes=["ident"])
    with kb.stage() as st:
        mcol = kb.sb("p_mcol", [128, 2, 3, 48], F32)
        gcol = kb.sb("p_gcol", [128, 2, 2, 8], F32)
        lg = kb.sb("p_lg", [128, 2, 2, 4], F32)
        dd = kb.sb("p_dd", [128, 2, 4], F32)
        for l in range(2):
            src = AP(g.mrow.tensor, g.mrow.offset + l * 3 * 6144, [[1, 128], [6144, 3], [128, 48]])
            kb.dma(mcol[:, l], src, reads=["mrow"], writes=["p_mcol"], allow_slow_non_contiguous=True)
        kb.dma(gcol[:], g.gcol[:], writes=["p_gcol"])
        kb.dma(lg[:], g.lbl[:], writes=["p_lg"])
        AB = g.AB[:].rearrange("p (l n c a k) -> p l n c a k", l=2, n=2, c=3, a=2)
        for l in range(2):
            for n in range(2):
                for c in range(3):
                    sh = mcol[:, l, c, (3 * n) * 8:(3 * n) * 8 + 8]
                    scl = mcol[:, l, c, (3 * n + 1) * 8:(3 * n + 1) * 8 + 8]
                    kb.op('dve', lambda e, l=l, n=n, c=c, scl=scl: e.scalar_tensor_tensor(
                        out=AB[:, l, n, c, 0, :], in0=scl, scalar=1.0, in1=gcol[:, l, n, :], op0=ALU.add, op1=ALU.mult),
                        reads=["p_mcol", "p_gcol"], writes=["AB"])
                    kb.op('dve', lambda e, l=l, n=n, c=c, sh=sh: e.tensor_copy(out=AB[:, l, n, c, 1, :], in_=sh),
                          reads=["p_mcol"], writes=["AB"])
        kb.op('dve', lambda e: e.tensor_tensor(out=dd[:], in0=lg[:, :, 0, :], in1=lg[:, :, 1, :], op=ALU.subtract),
              reads=["p_lg"], writes=["p_dd"])
        kb.op('act', lambda e: e.activation(out=g.lbc[:, 0], in_=dd[:], func=AF.Sigmoid), reads=["p_dd"], writes=["lbc"])
        kb.op('dve', lambda e: e.tensor_scalar(out=g.lbc[:, 1], in0=g.lbc[:, 0], scalar1=-1.0, scalar2=1.0,
                                               op0=ALU.mult, op1=ALU.add), reads=["lbc"], writes=["lbc"])


def load_weight_bf(kb, wbf, wsrc, K, N, tag, pieces=512):
    KC = K // 128
    stg = [kb.sb(f"{tag}_stg{i}", [128, KC, pieces], F32) for i in range(2)]
    for j, c0 in enumerate(range(0, N, pieces)):
        i = j % 2
        n = min(pieces, N - c0)
        kb.dma(stg[i][:, :, :n], wsrc[:, c0:c0 + n].rearrange("(k p) n -> p k n", p=128), writes=[f"{tag}_stg{i}"])
        kb.op('pool', lambda e, i=i, c0=c0, n=n: e.tensor_copy(out=wbf[:, :, c0:c0 + n], in_=stg[i][:, :, :n]),
              reads=[f"{tag}_stg{i}"], writes=[tag])


class NormT:
    def __init__(self, kb, g, tag, nx=2):
        self.kb, self.g, self.tag = kb, g, tag
        self.nx = nx
        self.xt = [kb.sb(f"{tag}_xt{i}", [128, D], F32) for i in range(nx)]
        self.xn = [kb.sb(f"{tag}_xn{i}", [128, D], BF16) for i in range(2)]
        self.junk = kb.sb(f"{tag}_junk", [128, D], BF16)
        self.ss = [kb.sb(f"{tag}_ss{i}", [128, 2], F32) for i in range(2)]
        self.tp = [kb.ps(f"{tag}_tp{i}", [128, 8, 128], BF16) for i in range(2)]
        self.n = 0

    def run(self, src_tile_ap, l, nrm, col, dst, dst_key, keep_x=None):
        kb, g, tag = self.kb, self.g, self.tag
        i = self.n % 2
        ix = self.n % self.nx
        self.n += 1
        xt, xn, ss, tp = self.xt[ix], self.xn[i], self.ss[i], self.tp[i]
        kx, kn, ks, kt = f"{tag}_xt{ix}", f"{tag}_xn{i}", f"{tag}_ss{i}", f"{tag}_tp{i}"
        kb.dma(xt[:], src_tile_ap, reads=["stream"], writes=[kx])
        kb.op('act', lambda e: e.activation(out=self.junk[:], in_=xt[:], func=AF.Square, accum_out=ss[:, 0:1]),
              reads=[kx], writes=[ks, f"{tag}_junk"])
        kb.op('dve', lambda e: e.tensor_scalar(out=ss[:, 1:2], in0=ss[:, 0:1], scalar1=1.0 / D, scalar2=EPS,
                                               op0=ALU.mult, op1=ALU.add), reads=[ks], writes=[ks])
        kb.op('act', lambda e: e.activation(out=ss[:, 1:2], in_=ss[:, 1:2], func=AF.Sqrt), reads=[ks], writes=[ks])
        kb.op('dve', lambda e: e.reciprocal(out=ss[:, 1:2], in_=ss[:, 1:2]), reads=[ks], writes=[ks])
        kb.op('dve', lambda e: e.tensor_scalar(out=xn[:], in0=xt[:], scalar1=ss[:, 1:2], scalar2=None, op0=ALU.mult),
              reads=[kx, ks], writes=[kn])
        for k in range(8):
            kb.op('pe', lambda e, k=k: e.transpose(out=tp[:, k, :], in_=xn[:, k * 128:(k + 1) * 128], identity=g.ident[:]),
                  reads=[kn, "ident"], writes=[kt])
        AB = g.AB[:].rearrange("p (l n c a k) -> p l n c a k", l=2, n=2, c=3, a=2)
        A = AB[:, l, nrm, col, 0, :]
        B = AB[:, l, nrm, col, 1, :]
        for k in range(8):
            kb.op('act', lambda e, k=k: e.activation(out=dst[:, k, :], in_=tp[:, k, :], func=AF.Identity,
                                                      scale=A[:, k:k + 1], bias=B[:, k:k + 1]),
                  reads=[kt, "AB"], writes=[dst_key])
        return xt, kx, ss, ks


E_FM = {}
for _i in range(4):
    E_FM[_i] = ("q", _i)
    E_FM[8 + _i] = ("ffw", 4 + _i)
    E_FM[12 + _i] = ("fbw", 8 + _i)
    E_FM[16 + _i] = ("g", 12 + _i)
    E_FM[20 + _i] = ("xr", 16 + _i)
    E_FM[24 + _i] = ("gate", 20 + _i)


def stage_e1(kb, g, l=0):
    with kb.stage() as st:
        wbf = kb.sb("e1_w", [128, 8, 3584], BF16)
        load_weight_bf(kb, wbf, g.e_w_in, D, 3584, "e1_w")
        nt = NormT(kb, g, "e1n")
        hT = [kb.sb(f"e1_hT{i}", [128, 8, 512], BF16) for i in range(2)]
        pss = [kb.ps(f"e1_ps{i}", [128, 512], F32) for i in range(4)]
        ev = [kb.sb(f"e1_ev{i}", [128, 512], F32) for i in range(4)]
        evv = [kb.sb(f"e1_evv{i}", [64, 512], BF16) for i in range(2)]
        nps = 0
        nev = 0
        nvv = 0
        def norm_block(blk):
            hi = blk % 2
            for tt in range(4):
                t = blk * 4 + tt
                nt.run(g.stream[t * 128:(t + 1) * 128, :], l, 0, mod_col(t), hT[hi][:, :, tt * 128:(tt + 1) * 128], f"e1_hT{hi}")
                bg_step(g)

        norm_block(0)
        for blk in range(NT // 512):
            hi = blk % 2
            if blk + 1 < NT // 512:
                norm_block(blk + 1)
            order = [0, 1, 2, 3, 16, 17, 18, 19, 8, 9, 10, 11, 12, 13, 14, 15, 20, 21, 22, 23, 24, 25, 26, 27]
            for cc in order:
                kind, fr = E_FM[cc]
                p = nps % 4
                nps += 1
                for k in range(8):
                    kb.op('pe', lambda e, k=k, p=p, cc=cc: e.matmul(pss[p][:], lhsT=wbf[:, k, cc * 128:(cc + 1) * 128],
                                                                   rhs=hT[hi][:, k, :], start=(k == 0), stop=(k == 7)),
                          reads=["e1_w", f"e1_hT{hi}"], writes=[f"e1_ps{p}"])
                v = nev % 4
                nev += 1
                if kind in ("q", "g"):
                    kb.op('act', lambda e, p=p, v=v: e.activation(out=ev[v][:], in_=pss[p][:], func=AF.Silu),
                          reads=[f"e1_ps{p}"], writes=[f"e1_ev{v}"])
                elif kind in ("ffw", "fbw"):
                    d = 0 if kind == "ffw" else 1
                    ch = cc % 4
                    kb.op('act', lambda e, p=p, v=v: e.activation(out=ev[v][:], in_=pss[p][:], func=AF.Sigmoid),
                          reads=[f"e1_ps{p}"], writes=[f"e1_ev{v}"])
                    kb.op('dve', lambda e, v=v, d=d, ch=ch: e.tensor_scalar(
                        out=ev[v][:], in0=ev[v][:], scalar1=g.lbc[:, 1, d, ch:ch + 1], scalar2=g.lbc[:, 0, d, ch:ch + 1],
                        op0=ALU.mult, op1=ALU.add), reads=[f"e1_ev{v}", "lbc"], writes=[f"e1_ev{v}"])
                else:
                    kb.op('dve', lambda e, p=p, v=v: e.tensor_copy(out=ev[v][:], in_=pss[p][:]),
                          reads=[f"e1_ps{p}"], writes=[f"e1_ev{v}"])
                kb.dma(g.fm[fr * 128:(fr + 1) * 128, blk * 512:(blk + 1) * 512], ev[v][:], reads=[f"e1_ev{v}"], writes=["fm"])
            for c8 in range(8):
                p = nps % 4
                nps += 1
                for k in range(8):
                    kb.op('pe', lambda e, k=k, p=p, c8=c8: e.matmul(pss[p][0:64, :], lhsT=hT[hi][:, k, c8 * 64:(c8 + 1) * 64],
                                                                   rhs=wbf[:, k, 512:1024], start=(k == 0), stop=(k == 7)),
                          reads=["e1_w", f"e1_hT{hi}"], writes=[f"e1_ps{p}"])
                v = nvv % 2
                nvv += 1
                kb.op('act', lambda e, p=p, v=v: e.activation(out=evv[v][:], in_=pss[p][0:64, :], func=AF.Copy),
                      reads=[f"e1_ps{p}"], writes=[f"e1_evv{v}"])
                r0 = blk * 512 + c8 * 64
                kb.dma(g.vtok[r0:r0 + 64, :], evv[v][:], reads=[f"e1_evv{v}"], writes=["vtok"])


NCH = TS // 64
NCC = TC // 64


def stage_e2(kb, g):
    with kb.stage() as st:
        W = TS
        q = kb.sb("h_q", [128, W], F32)
        f = [kb.sb(f"h_f{d}", [128, W], F32) for d in range(2)]
        lf = kb.sb("h_lf", [128, W], F32)
        a = kb.sb("h_a", [128, W], F32)
        E = kb.sb("h_E", [128, W], F32)
        omf = kb.sb("h_omf", [128, W], F32)
        tmp = kb.sb("h_tmp", [128, W], F32)
        rmask = kb.sb("h_rmask", [128, W], F32)
        qe = [kb.sb(f"h_qe{d}", [128, W], BF16) for d in range(2)]
        ke = [kb.sb(f"h_ke{d}", [128, W], BF16) for d in range(2)]
        kend = [kb.sb(f"h_kend{d}", [128, W], BF16) for d in range(2)]
        edec = [kb.sb(f"h_edec{d}", [128, NCH], F32) for d in range(2)]
        oT = [kb.sb(f"h_oT{d}", [128, W], F32) for d in range(2)]
        vt = kb.sb("h_vt", [64, NCH, 128], BF16)
        msk = kb.sb("h_msk", [64, 2, 64], F32)
        S32 = [kb.sb(f"h_S32{d}", [128, 128], F32) for d in range(2)]
        Sbf = [kb.sb(f"h_Sbf{d}", [128, 128], BF16) for d in range(2)]
        attsb = [[kb.sb(f"h_att{d}{i}", [64, 64], BF16) for i in range(2)] for d in range(2)]
        kendsb = [[kb.sb(f"h_ks{d}{i}", [64, 128], BF16) for i in range(2)] for d in range(2)]
        att_ps = [kb.ps(f"h_attps{i}", [128, 512], F32)[0:64, 0:64] for i in range(2)]
        kend_ps = [kb.ps(f"h_kps{i}", [128, 1024], BF16)[0:64, 0:128] for i in range(2)]
        o_ps = [kb.ps(f"h_ops{i}", [128, 512], F32)[:, 0:64] for i in range(2)]
        sup_ps = [kb.ps(f"h_sps{i}", [128, 512], F32)[:, 0:128] for i in range(2)]

        kb.dma(msk[:], g.hmask[:], writes=["h_msk"])
        kb.op('pool', lambda e: e.memset(rmask[:], 1.0), writes=["h_rmask"])
        r3 = rmask[:].rearrange("p (c t) -> p c t", t=64)
        kb.op('pool', lambda e: e.memset(r3[:, :, 0:1], 0.0), writes=["h_rmask"])

        def c3(t):
            return t[:].rearrange("p (c t) -> p c t", t=64)

        for b in range(NB):
            for h in range(4):
                c0 = b * TS
                kb.dma(q[:], g.fm[h * 128:(h + 1) * 128, c0:c0 + W], reads=["fm"], writes=["h_q"])
                for d in range(2):
                    kb.dma(f[d][:], g.fm[(4 + 4 * d + h) * 128:(5 + 4 * d + h) * 128, c0:c0 + W], reads=["fm"], writes=[f"h_f{d}"])
                vsrc = AP(g.vtok.tensor, g.vtok.offset + c0 * 512 + h * 128, [[512, 64], [64 * 512, NCH], [1, 128]])
                kb.dma(vt[:], vsrc, reads=["vtok"], writes=["h_vt"])
                for d in range(2):
                    fd = f[d]
                    kf = f"h_f{d}"
                    kb.op('act', lambda e: e.activation(out=lf[:], in_=fd[:], func=AF.Ln), reads=[kf], writes=["h_lf"])
                    kb.op('dve', lambda e: e.tensor_tensor_scan(out=a[:], data0=rmask[:], data1=lf[:], initial=0.0,
                                                                op0=ALU.mult, op1=ALU.add),
                          reads=["h_rmask", "h_lf"], writes=["h_a"])
                    a3 = c3(a)
                    if d == 1:
                        kb.op('dve', lambda e: e.tensor_tensor(out=tmp[:], in0=lf[:], in1=a[:], op=ALU.subtract),
                              reads=["h_lf", "h_a"], writes=["h_tmp"])
                        bend = a3[:, :, 63:64]
                        bend_b = bc(bend, [bend.ap[0], bend.ap[1], [0, 64]])
                        kb.op('dve', lambda e: e.tensor_tensor(out=c3(lf), in0=c3(tmp), in1=bend_b, op=ALU.add),
                              reads=["h_tmp", "h_a"], writes=["h_lf"])
                        kb.op('pool', lambda e: e.tensor_copy(out=a[:], in_=lf[:]), reads=["h_lf"], writes=["h_a"])
                    last = 63 if d == 0 else 0
                    alast = a3[:, :, last:last + 1]
                    alast_b = bc(alast, [alast.ap[0], alast.ap[1], [0, 64]])
                    kb.op('act', lambda e: e.activation(out=E[:], in_=a[:], func=AF.Exp), reads=["h_a"], writes=["h_E"])
                    kb.op('dve', lambda e: e.tensor_tensor(out=qe[d][:], in0=q[:], in1=E[:], op=ALU.mult),
                          reads=["h_q", "h_E"], writes=[f"h_qe{d}"])
                    E3 = c3(E)
                    kb.op('pool', lambda e: e.tensor_copy(out=edec[d][:], in_=E3[:, :, last]), reads=["h_E"], writes=[f"h_edec{d}"])
                    kb.op('dve', lambda e: e.tensor_scalar(out=omf[:], in0=fd[:], scalar1=-1.0, scalar2=1.0,
                                                           op0=ALU.mult, op1=ALU.add), reads=[kf], writes=["h_omf"])
                    kb.op('act', lambda e: e.activation(out=E[:], in_=a[:], func=AF.Exp, scale=-1.0), reads=["h_a"], writes=["h_E"])
                    kb.op('dve', lambda e: e.tensor_tensor(out=ke[d][:], in0=omf[:], in1=E[:], op=ALU.mult),
                          reads=["h_omf", "h_E"], writes=[f"h_ke{d}"])
                    kb.op('dve', lambda e: e.tensor_tensor(out=c3(tmp), in0=alast_b, in1=a3, op=ALU.subtract),
                          reads=["h_a"], writes=["h_tmp"])
                    kb.op('act', lambda e: e.activation(out=E[:], in_=tmp[:], func=AF.Exp), reads=["h_tmp"], writes=["h_E"])
                    kb.op('dve', lambda e: e.tensor_tensor(out=kend[d][:], in0=omf[:], in1=E[:], op=ALU.mult),
                          reads=["h_omf", "h_E"], writes=[f"h_kend{d}"])
                    if d == 0:
                        g.dump(kb, "d_a", a[:], "h_a")
                        g.dump(kb, "d_lf", lf[:], "h_lf")
                        g.dump(kb, "d_qe", qe[0][:], "h_qe0", BF16)
                        g.dump(kb, "d_ke", ke[0][:], "h_ke0", BF16)
                        g.dump(kb, "d_kend", kend[0][:], "h_kend0", BF16)
                        g.dump(kb, "d_edec", edec[0][:], "h_edec0")
                    kb.op('pool', lambda e: e.memset(S32[d][:], 0.0), writes=[f"h_S32{d}"])
                    kb.op('pool', lambda e: e.memset(Sbf[d][:], 0.0), writes=[f"h_Sbf{d}"])
                order = [list(range(NCH)), list(range(NCC - 1, -1, -1)) + list(range(NCH - 1, NCC - 1, -1))]
                for stp in range(NCH):
                    if stp % 9 == 4:
                        bg_step(g)
                    for d in range(2):
                        ci = order[d][stp]
                        i = stp % 2
                        cs = slice(ci * 64, (ci + 1) * 64)
                        kb.op('pe', lambda e: e.matmul(att_ps[i], lhsT=ke[d][:, cs], rhs=qe[d][:, cs], start=True, stop=True),
                              reads=[f"h_ke{d}", f"h_qe{d}"], writes=[f"h_attps{i}"])
                        kb.op('dve', lambda e: e.tensor_tensor(out=attsb[d][i][:], in0=att_ps[i], in1=msk[:, d, :], op=ALU.mult),
                              reads=[f"h_attps{i}", "h_msk"], writes=[f"h_att{d}{i}"])
                        if d == 0 and stp == 0:
                            g.dump(kb, "d_att", attsb[0][0][:], "h_att00", BF16)
                        kb.op('pe', lambda e: e.transpose(out=kend_ps[i], in_=kend[d][:, cs], identity=g.ident[:]),
                              reads=[f"h_kend{d}", "ident"], writes=[f"h_kps{i}"])
                        kb.op('act', lambda e: e.activation(out=kendsb[d][i][:], in_=kend_ps[i], func=AF.Copy),
                              reads=[f"h_kps{i}"], writes=[f"h_ks{d}{i}"])
                        kb.op('pe', lambda e: e.matmul(o_ps[i], lhsT=Sbf[d][:], rhs=qe[d][:, cs], start=True, stop=False),
                              reads=[f"h_Sbf{d}", f"h_qe{d}"], writes=[f"h_ops{i}"])
                        kb.op('pe', lambda e: e.matmul(o_ps[i], lhsT=vt[:, ci, :], rhs=attsb[d][i][:], start=False, stop=True),
                              reads=["h_vt", f"h_att{d}{i}"], writes=[f"h_ops{i}"])
                        kb.op('act', lambda e: e.activation(out=oT[d][:, cs], in_=o_ps[i], func=AF.Copy),
                              reads=[f"h_ops{i}"], writes=[f"h_oT{d}"])
                        kb.op('pe', lambda e: e.matmul(sup_ps[i], lhsT=kendsb[d][i][:], rhs=vt[:, ci, :], start=True, stop=True),
                              reads=[f"h_ks{d}{i}", "h_vt"], writes=[f"h_sps{i}"])
                        kb.op('dve', lambda e: e.scalar_tensor_tensor(out=S32[d][:], in0=S32[d][:], scalar=edec[d][:, ci:ci + 1],
                                                                     in1=sup_ps[i], op0=ALU.mult, op1=ALU.add),
                              reads=[f"h_S32{d}", f"h_edec{d}", f"h_sps{i}"], writes=[f"h_S32{d}"])
                        kb.op('act', lambda e: e.activation(out=Sbf[d][:], in_=S32[d][:], func=AF.Copy),
                              reads=[f"h_S32{d}"], writes=[f"h_Sbf{d}"])
                        if d == 0 and stp == 0:
                            g.dump(kb, "d_S0", S32[0][:], "h_S320")
                            g.dump(kb, "d_ks0", kendsb[0][0][:], "h_ks00", BF16)
                            g.dump(kb, "d_vt", vt[:], "h_vt", BF16)
                        if d == 0 and stp == 1:
                            g.dump(kb, "d_S1", S32[0][:], "h_S320")
                        if d == 0 and stp == 2:
                            g.dump(kb, "d_o01", oT[0][:, 0:192], "h_oT0")
                kb.op('pool', lambda e: e.tensor_tensor(out=oT[0][:], in0=oT[0][:], in1=oT[1][:], op=ALU.add),
                      reads=["h_oT0", "h_oT1"], writes=["h_oT0"])
                kb.dma(g.oT[h * 128:(h + 1) * 128, c0:c0 + W], oT[0][:], reads=["h_oT0"], writes=["oT"])


GELU_C = 1.5957691216057308
STAIR = [(0, 1, 16), (1, 1, 8), (2, 1, 5), (3, 1, 4), (4, 1, 3), (5, 3, 2), (8, 8, 1)]
NCAND = sum(ni * ln for _, ni, ln in STAIR)
VPE = True


def gelu_tanh(kb, out, x, tmp, kx, kt, ko, eng2='pool'):
    kb.op('act', lambda e: e.activation(out=out, in_=x, func=AF.Gelu_apprx_tanh), reads=[kx], writes=[ko])


def stage_e3(kb, g):
    with kb.stage() as st:
        W = TS
        wblk32 = kb.sb("r_w32", [128, 16, 128], F32)
        wblk = kb.sb("r_w", [128, 16, 128], BF16)
        sm = kb.sb("r_sm", [128, 44], F32)
        spc = kb.sb("r_spc", [128, 2, 2, 4], F32)
        kb.dma(wblk32[:], g.rg_blk[:], writes=["r_w32"])
        kb.op('pool', lambda e: e.tensor_copy(out=wblk[:], in_=wblk32[:]), reads=["r_w32"], writes=["r_w"])
        kb.dma(sm[:], g.rg_sm[:], writes=["r_sm"])
        smv = sm[:]
        cw = smv[:, 0:16].rearrange("p (j c) -> p j c", j=4)
        cb = smv[:, 16:20]
        ba = smv[:, 20:28].rearrange("p (d c) -> p d c", d=2)
        bx = smv[:, 28:36].rearrange("p (d c) -> p d c", d=2)
        lam = smv[:, 36:44].rearrange("p (d c) -> p d c", d=2)
        kb.op('act', lambda e: e.activation(out=spc[:, 0], in_=lam, func=AF.Exp, scale=-1.0), reads=["r_sm"], writes=["r_spc"])
        kb.op('act', lambda e: e.activation(out=spc[:, 0], in_=spc[:, 0], func=AF.Ln, bias=1.0), reads=["r_spc"], writes=["r_spc"])
        kb.op('dve', lambda e: e.tensor_scalar(out=spc[:, 1], in0=spc[:, 0], scalar1=-16.0, scalar2=None, op0=ALU.mult),
              reads=["r_spc"], writes=["r_spc"])
        kb.op('dve', lambda e: e.tensor_scalar(out=spc[:, 0], in0=spc[:, 0], scalar1=-8.0, scalar2=None, op0=ALU.mult),
              reads=["r_spc"], writes=["r_spc"])

        x = kb.sb("r_x", [128, W], F32)
        gt = kb.sb("r_gt", [128, W], F32)
        xc = kb.sb("r_xc", [128, W], F32)
        xcb = kb.sb("r_xcb", [128, W], BF16)
        r = kb.sb("r_r", [128, W], F32)
        ig = kb.sb("r_i", [128, W], F32)
        aa = kb.sb("r_a", [128, W], F32)
        u = kb.sb("r_u", [128, W], F32)
        hf = kb.sb("r_hf", [128, W], F32)
        hb = kb.sb("r_hb", [128, W], F32)
        tmp = kb.sb("r_tmp", [128, W], F32)
        yb = kb.sb("r_yb", [128, W], BF16)
        pss = [kb.ps(f"r_ps{i}", [128, 512], F32) for i in range(4)]
        nps = 0
        segs = [(0, TC), (TC, TS)]
        tblocks = [(i * 512, min(512, W - i * 512)) for i in range((W + 511) // 512)]
        for b in range(NB):
            for ct in range(4):
                c0 = b * TS
                kb.dma(x[:], g.fm[(16 + ct) * 128:(17 + ct) * 128, c0:c0 + W], reads=["fm"], writes=["r_x"])
                kb.dma(gt[:], g.fm[(20 + ct) * 128:(21 + ct) * 128, c0:c0 + W], reads=["fm"], writes=["r_gt"])
                kb.op('dve', lambda e: e.tensor_scalar(out=xc[:], in0=x[:], scalar1=cw[:, 2, ct:ct + 1], scalar2=cb[:, ct:ct + 1],
                                                       op0=ALU.mult, op1=ALU.add), reads=["r_x", "r_sm"], writes=["r_xc"])
                for j in (0, 1, 3):
                    sft = j - 2
                    for (s0, s1) in segs:
                        lo = max(s0, s0 - sft)
                        hi = min(s1, s1 - sft)
                        kb.op('dve', lambda e: e.scalar_tensor_tensor(out=xc[:, lo:hi], in0=x[:, lo + sft:hi + sft],
                                                                     scalar=cw[:, j, ct:ct + 1], in1=xc[:, lo:hi],
                                                                     op0=ALU.mult, op1=ALU.add),
                              reads=["r_x", "r_sm", "r_xc"], writes=["r_xc"])
                kb.op('pool', lambda e: e.tensor_copy(out=xcb[:], in_=xc[:]), reads=["r_xc"], writes=["r_xcb"])
                for d in range(2):
                    for ty, dst, bias, kd in ((0, r, ba, "r_r"), (1, ig, bx, "r_i")):
                        wi = (ct * 2 + ty) * 2 + d
                        for (t0, tn) in tblocks:
                            p = nps % 4
                            nps += 1
                            kb.op('pe', lambda e: e.matmul(pss[p][:, :tn], lhsT=wblk[:, wi, :], rhs=xcb[:, t0:t0 + tn], start=True, stop=True),
                                  reads=["r_w", "r_xcb"], writes=[f"r_ps{p}"])
                            kb.op('act', lambda e: e.activation(out=dst[:, t0:t0 + tn], in_=pss[p][:, :tn], func=AF.Sigmoid,
                                                                bias=bias[:, d, ct:ct + 1]),
                                  reads=[f"r_ps{p}", "r_sm"], writes=[kd])
                    kb.op('act', lambda e: e.activation(out=aa[:], in_=r[:], func=AF.Exp, scale=spc[:, 0, d, ct:ct + 1]),
                          reads=["r_r", "r_spc"], writes=["r_a"])
                    kb.op('act', lambda e: e.activation(out=tmp[:], in_=r[:], func=AF.Exp, scale=spc[:, 1, d, ct:ct + 1]),
                          reads=["r_r", "r_spc"], writes=["r_tmp"])
                    kb.op('dve', lambda e: e.tensor_scalar(out=tmp[:], in0=tmp[:], scalar1=-1.0, scalar2=1.0, op0=ALU.mult, op1=ALU.add),
                          reads=["r_tmp"], writes=["r_tmp"])
                    kb.op('act', lambda e: e.activation(out=tmp[:], in_=tmp[:], func=AF.Sqrt), reads=["r_tmp"], writes=["r_tmp"])
                    kb.op('pool', lambda e: e.tensor_tensor(out=u[:], in0=ig[:], in1=xc[:], op=ALU.mult), reads=["r_i", "r_xc"], writes=["r_u"])
                    kb.op('dve', lambda e: e.tensor_tensor(out=u[:], in0=u[:], in1=tmp[:], op=ALU.mult), reads=["r_u", "r_tmp"], writes=["r_u"])
                    if d == 0:
                        kb.op('dve', lambda e: e.tensor_tensor_scan(out=hf[:], data0=aa[:], data1=u[:], initial=0.0,
                                                                    op0=ALU.mult, op1=ALU.add), reads=["r_a", "r_u"], writes=["r_hf"])
                    else:
                        def rev(tl, s0, s1):
                            v = tl[:, s0:s1]
                            return bc(v, [v.ap[0], [-1, s1 - s0]], offset_add=(s1 - s0 - 1))
                        kb.op('dve', lambda e: e.tensor_tensor_scan(out=rev(hb, 0, TC), data0=rev(aa, 0, TC), data1=rev(u, 0, TC),
                                                                    initial=0.0, op0=ALU.mult, op1=ALU.add),
                              reads=["r_a", "r_u"], writes=["r_hb"])
                        kb.op('dve', lambda e: e.tensor_tensor_scan(out=rev(hb, TC, TS), data0=rev(aa, TC, TS), data1=rev(u, TC, TS),
                                                                    initial=hb[:, 0:1], op0=ALU.mult, op1=ALU.add),
                              reads=["r_a", "r_u", "r_hb"], writes=["r_hb"])
                kb.op('pool', lambda e: e.tensor_tensor(out=hf[:], in0=hf[:], in1=hb[:], op=ALU.add), reads=["r_hf", "r_hb"], writes=["r_hf"])
                gelu_tanh(kb, u[:], gt[:], tmp[:], "r_gt", "r_tmp", "r_u")
                kb.op('dve', lambda e: e.tensor_tensor(out=yb[:], in0=u[:], in1=hf[:], op=ALU.mult), reads=["r_u", "r_hf"], writes=["r_yb"])
                kb.dma(g.ybT[ct * 128:(ct + 1) * 128, c0:c0 + W], yb[:], reads=["r_yb"], writes=["ybT"])


def load_gate_rows(kb, g, l, idx, tag):
    mg = kb.sb(f"{tag}_mg", [128, 3, D], F32)
    for c in range(3):
        src = AP(g.mrow.tensor, g.mrow.offset + (l * 3 + c) * 6144 + idx * D, [[0, 128], [1, D]])
        kb.dma(mg[:, c, :], src, reads=["mrow"], writes=[f"{tag}_mg"])
    return mg


class OutProj:
    def __init__(self, kb, g, l, w_dram, gate_idx, tag):
        self.kb, self.g, self.tag = kb, g, tag
        self.w = kb.sb(f"{tag}_w", [128, 8, D], BF16)
        load_weight_bf(kb, self.w, w_dram, D, D, f"{tag}_w")
        self.mg = load_gate_rows(kb, g, l, gate_idx, tag)
        self.ps = [kb.ps(f"{tag}_ps{i}", [128, 512], F32) for i in range(2)]
        self.xt = [kb.sb(f"{tag}_xt{i}", [128, D], F32) for i in range(2)]
        self.n = 0

    def run(self, yT, ky, blk, src, dst, tiles=None):
        kb, tag = self.kb, self.tag
        for tt in range(4):
            t = blk * 4 + tt if tiles is None else tiles[tt]
            i = self.n % 2
            self.n += 1
            xt = self.xt[i]
            kb.dma(xt[:], src[t * 128:(t + 1) * 128, :], reads=["stream"], writes=[f"{tag}_xt{i}"])
            col = mod_col(t)
            for half in range(2):
                for k in range(8):
                    kb.op('pe', lambda e: e.matmul(self.ps[half][:], lhsT=yT[:, k, tt * 128:(tt + 1) * 128],
                                                   rhs=self.w[:, k, half * 512:(half + 1) * 512], start=(k == 0), stop=(k == 7)),
                          reads=[ky, f"{tag}_w"], writes=[f"{tag}_ps{half}"])
            tmpk = f"{tag}_tmp{i}"
            if not hasattr(self, "tmp"):
                self.tmp = [kb.sb(f"{tag}_tmp{j}", [128, D], F32) for j in range(2)]
            tmp = self.tmp[i]
            for half in range(2):
                hs = slice(half * 512, (half + 1) * 512)
                kb.op('dve', lambda e: e.tensor_tensor(out=tmp[:, hs], in0=self.ps[half][:], in1=self.mg[:, col, hs], op=ALU.mult),
                      reads=[f"{tag}_ps{half}", f"{tag}_mg"], writes=[tmpk])
            kb.op('pool', lambda e: e.tensor_tensor(out=tmp[:], in0=tmp[:], in1=xt[:], op=ALU.add),
                  reads=[tmpk, f"{tag}_xt{i}"], writes=[tmpk])
            kb.dma(dst[t * 128:(t + 1) * 128, :], tmp[:], reads=[tmpk], writes=["stream_out"])


def stage_e4(kb, g, l, src, dst):
    with kb.stage() as st:
        op = OutProj(kb, g, l, g.e_w_out, 2, "e4")
        ones = kb.sb("e4_ones", [128, 128], F32)
        kb.op('pool', lambda e: e.memset(ones[:], 1.0 / 128.0), writes=["e4_ones"])
        gain = kb.sb("e4_gain", [128, 1], F32)
        kb.dma(gain[:], g.hg_gain[:], writes=["e4_gain"])
        o_ = [kb.sb(f"e4_o{i}", [128, 4, 512], F32) for i in range(2)]
        gg_ = [kb.sb(f"e4_g{i}", [128, 4, 512], F32) for i in range(2)]
        sq_ = [kb.sb(f"e4_sq{i}", [128, 4, 512], F32) for i in range(2)]
        rs_ = [kb.sb(f"e4_rs{i}", [128, 512], F32) for i in range(2)]
        yT = [kb.sb(f"e4_yT{i}", [128, 8, 512], BF16) for i in range(2)]
        sps = [kb.ps(f"e4_sps{i}", [128, 512], F32) for i in range(2)]

        def merge(blk):
            bi = blk % 2
            o, gg, sq, rs = o_[bi], gg_[bi], sq_[bi], rs_[bi]
            KO, KG, KSQ, KRS = f"e4_o{bi}", f"e4_g{bi}", f"e4_sq{bi}", f"e4_rs{bi}"
            cs = slice(blk * 512, (blk + 1) * 512)
            kb.dma(o[:], g.oT[:, cs].rearrange("(h p) t -> p h t", p=128), reads=["oT"], writes=[KO])
            kb.dma(gg[:], g.fm[12 * 128:16 * 128, cs].rearrange("(h p) t -> p h t", p=128), reads=["fm"], writes=[KG])
            kb.dma(yT[bi][:, 4:8, :], g.ybT[:, cs].rearrange("(h p) t -> p h t", p=128), reads=["ybT"], writes=[f"e4_yT{bi}"])
            kb.op('act', lambda e: e.activation(out=sq[:], in_=o[:], func=AF.Square), reads=[KO], writes=[KSQ])
            for h in range(4):
                p = h % 2
                kb.op('pe', lambda e: e.matmul(sps[p][:], lhsT=ones[:], rhs=sq[:, h, :], start=True, stop=True),
                      reads=["e4_ones", KSQ], writes=[f"e4_sps{p}"])
                kb.op('dve', lambda e: e.tensor_scalar(out=rs[:], in0=sps[p][:], scalar1=EPS, scalar2=None, op0=ALU.add),
                      reads=[f"e4_sps{p}"], writes=[KRS])
                kb.op('act', lambda e: e.activation(out=rs[:], in_=rs[:], func=AF.Sqrt), reads=[KRS], writes=[KRS])
                kb.op('dve', lambda e: e.reciprocal(out=rs[:], in_=rs[:]), reads=[KRS], writes=[KRS])
                kb.op('dve', lambda e: e.tensor_tensor(out=rs[:], in0=rs[:], in1=o[:, h, :], op=ALU.mult),
                      reads=[KRS, KO], writes=[KRS])
                kb.op('dve', lambda e: e.scalar_tensor_tensor(out=yT[bi][:, h, :], in0=rs[:], scalar=gain[:, 0:1], in1=gg[:, h, :],
                                                             op0=ALU.mult, op1=ALU.mult),
                      reads=[KRS, "e4_gain", KG], writes=[f"e4_yT{bi}"])

        NBLK = NT // 512
        merge(0)
        for blk in range(NBLK):
            if blk + 1 < NBLK:
                merge(blk + 1)
            op.run(yT[blk % 2], f"e4_yT{blk % 2}", blk, src, dst)


class UVConv:
    R = 2

    def __init__(self, kb, g, l):
        self.kb, self.g, self.l = kb, g, l
        R = self.R
        self.ld = [[kb.sb(f"uv_ld{j}{i}", [128, R, D], F32) for i in range(2)] for j in range(2)]
        self.ob = [kb.sb(f"uv_ob{i}", [128, R, 2 * D], BF16) for i in range(2)]
        self.tabs = (g.p_u0, g.p_v0) if l == 0 else (g.p_u1, g.p_v1)
        self.dst = g.uv0 if l == 0 else g.uv1
        self.nit = 128 // R
        self.loaded = 0
        self.done = 0
        self._load()

    def _load(self):
        if self.loaded >= self.nit:
            return
        kb, R = self.kb, self.R
        i = self.loaded % 2
        r0 = self.loaded * R
        for j in range(2):
            srcp = AP(self.tabs[j].tensor, self.tabs[j].offset + r0 * D, [[128 * D, 128], [D, R], [1, D]])
            kb.dma(self.ld[j][i][:], srcp, writes=[f"uv_ld{j}{i}"])
        self.loaded += 1

    def step(self):
        if self.done >= self.nit:
            return
        kb, R = self.kb, self.R
        self._load()
        i = self.done % 2
        r0 = self.done * R
        for j in range(2):
            kb.op('pool', lambda e: e.tensor_copy(out=self.ob[i][:, :, j * D:(j + 1) * D], in_=self.ld[j][i][:]),
                  reads=[f"uv_ld{j}{i}"], writes=[f"uv_ob{i}"])
        dstp = AP(self.dst.tensor, self.dst.offset + r0 * 2 * D, [[128 * 2 * D, 128], [2 * D, R], [1, 2 * D]])
        kb.dma(dstp, self.ob[i][:], reads=[f"uv_ob{i}"], writes=["uv"])
        self.done += 1

    def finish(self):
        while self.done < self.nit:
            self.step()


def bg_step(g, n=1):
    bg = getattr(g, "bg", None)
    if bg is not None:
        for _ in range(n):
            bg.step()


def stage_peer(kb, g, l, src, dst, tiles, dst_row):
    GS = 8
    NGRP = 128 // GS
    with kb.stage() as st:
        wq = kb.sb("p_wq", [128, 8, 2048], BF16)
        with kb.stage() as st2:
            load_weight_bf(kb, wq, g.p_wq[l], D, 2048, "p_wq", pieces=256)
        uvt = g.uv0 if l == 0 else g.uv1
        keysT = kb.sb("p_keys", [128, 16, 128], F32)
        kb.dma(keysT[:], g.keysT[l], writes=["p_keys"])
        bq = kb.sb("p_bq", [128, 16], F32)
        kb.dma(bq[:], g.bqc[l], writes=["p_bq"])
        nt = NormT(kb, g, "pn", nx=2)
        hT = [kb.sb(f"p_hT{i}", [128, 8, 128], BF16) for i in range(2)]
        qT = kb.sb("p_qT", [128, 16, 128], F32)
        sc2 = kb.sb("p_sc2", [128, 128], F32)
        sv = kb.sb("p_sv", [128, 16, 16], F32)
        si = kb.sb("p_si", [128, 16, 16], U32)
        sif = kb.sb("p_sif", [128, 16, 16], F32)
        cand = kb.sb("p_cand", [128, 8, NCAND], F32)
        cand2 = kb.sb("p_cand2", [128, NCAND], F32)
        cidx = kb.sb("p_cidx", [128, 8, NCAND], F32)
        junk = kb.sb("p_junk", [128, NCAND], F32)
        best = kb.sb("p_best", [128, 8, 16], F32)
        eif = kb.sb("p_eif", [128, 128], F32)
        eidx = [kb.sb(f"p_eidx{i}", [128, 128], I32) for i in range(2)]
        gate = [kb.sb(f"p_gate{i}", [128, 8, 16], F32) for i in range(2)]
        gsum = kb.sb("p_gsum", [128, 8], F32)
        dots = [kb.sb(f"p_dots{i}", [128, 128], F32) for i in range(2)]
        actv = kb.sb("p_act", [128, 128], F32)
        gtmp = kb.sb("p_gtmp", [128, 128], F32)
        abrow = kb.sb("p_abrow", [128, 2, D], F32)
        m5 = [kb.sb(f"p_m5{i}", [128, D], F32) for i in range(2)]
        grow = kb.sb("p_grow", [128, D], F32)
        xtok = [kb.sb(f"p_xtok{i}", [128, D], F32) for i in range(2)]
        bigj = kb.sb("p_bigj", [128, D], BF16)
        acc = kb.sb("p_acc", [128, D], F32)
        NG = 18
        gb = [kb.sb(f"p_gb{i}", [128, 2 * D], BF16) for i in range(NG)]
        NDG = 8
        dg = [kb.sb(f"p_dg{i}", [128, 128], BF16) for i in range(NDG)]
        qps = kb.ps("p_qps", [128, 512], F32)
        scps = [kb.ps(f"p_scps{i}", [128, 512], F32) for i in range(2)]
        vps = [kb.ps(f"p_vps{i}", [128, 512], F32) for i in range(2)]
        src_g = AP(g.gffn.tensor, g.gffn.offset + l * D, [[0, 128], [1, D]])
        kb.dma(grow[:], src_g, writes=["p_grow"])
        state = {"col": None, "epoch": -1, "ngb": 0, "ndg": 0}
        info = {}
        slot_buf = {}

        def routing(n):
            t = tiles[n]
            pi = n % 2
            col = mod_col(t)
            if col != state["col"]:
                state["col"] = col
                state["epoch"] += 1
                ep = state["epoch"] % 2
                for j, idx in enumerate((4, 3)):
                    srcm = AP(g.mrow.tensor, g.mrow.offset + (l * 3 + col) * 6144 + idx * D, [[0, 128], [1, D]])
                    kb.dma(abrow[:, j, :], srcm, reads=["mrow"], writes=["p_abrow"])
                srcm = AP(g.mrow.tensor, g.mrow.offset + (l * 3 + col) * 6144 + 5 * D, [[0, 128], [1, D]])
                kb.dma(m5[ep][:], srcm, reads=["mrow"], writes=[f"p_m5{ep}"])
                kb.op('dve', lambda e: e.scalar_tensor_tensor(out=abrow[:, 0, :], in0=abrow[:, 0, :], scalar=1.0, in1=grow[:],
                                                             op0=ALU.add, op1=ALU.mult), reads=["p_abrow", "p_grow"], writes=["p_abrow"])
            ep = state["epoch"] % 2
            xt, kx, ss, ks = nt.run(src[t * 128:(t + 1) * 128, :], l, 1, col, hT[pi][:], f"p_hT{pi}")
            info[n] = (xt, kx, ep)
            kxt = f"p_xtok{pi}"
            kb.op('dve', lambda e: e.scalar_tensor_tensor(out=xtok[pi][:], in0=xt[:], scalar=ss[:, 1:2], in1=abrow[:, 0, :],
                                                         op0=ALU.mult, op1=ALU.mult), reads=[kx, ks, "p_abrow"], writes=[kxt])
            kb.op('dve', lambda e: e.tensor_tensor(out=xtok[pi][:], in0=xtok[pi][:], in1=abrow[:, 1, :], op=ALU.add),
                  reads=[kxt, "p_abrow"], writes=[kxt])
            yield
            for hp4 in range(4):
                for j in range(4):
                    hp = hp4 * 4 + j
                    for k in range(8):
                        kb.op('pe', lambda e: e.matmul(qps[:, j * 128:(j + 1) * 128], lhsT=wq[:, k, hp * 128:(hp + 1) * 128],
                                                       rhs=hT[pi][:, k, :], start=(k == 0), stop=(k == 7)),
                              reads=["p_wq", f"p_hT{pi}"], writes=["p_qps"])
                for j in range(4):
                    hp = hp4 * 4 + j
                    kb.op('act', lambda e: e.activation(out=qT[:, hp, :], in_=qps[:, j * 128:(j + 1) * 128], func=AF.Identity,
                                                        bias=bq[:, hp:hp + 1]), reads=["p_qps", "p_bq"], writes=["p_qT"])
                yield
            for hp4 in range(4):
                sp = hp4 % 2
                for j in range(4):
                    hp = hp4 * 4 + j
                    kb.op('pe', lambda e: e.matmul(scps[sp][:, j * 128:(j + 1) * 128], lhsT=qT[:, hp, :], rhs=keysT[:, hp, :],
                                                   start=True, stop=True), reads=["p_qT", "p_keys"], writes=[f"p_scps{sp}"])
                for j in range(4):
                    hp = hp4 * 4 + j
                    scv = scps[sp][:, j * 128:(j + 1) * 128]
                    ksp = f"p_scps{sp}"
                    kb.op('dve', lambda e: e.max(out=sv[:, hp, 0:8], in_=scv), reads=[ksp], writes=["p_sv"])
                    kb.op('dve', lambda e: e.max_index(out=si[:, hp, 0:8], in_max=sv[:, hp, 0:8], in_values=scv),
                          reads=[ksp, "p_sv"], writes=["p_si"])
                    kb.op('dve', lambda e: e.match_replace(out=sc2[:], in_to_replace=sv[:, hp, 0:8], in_values=scv, imm_value=-1e30),
                          reads=[ksp, "p_sv"], writes=["p_sc2"])
                    kb.op('dve', lambda e: e.max(out=sv[:, hp, 8:16], in_=sc2[:]), reads=["p_sc2"], writes=["p_sv"])
                    kb.op('dve', lambda e: e.max_index(out=si[:, hp, 8:16], in_max=sv[:, hp, 8:16], in_values=sc2[:]),
                          reads=["p_sc2", "p_sv"], writes=["p_si"])
                    if j % 2 == 1:
                        yield
            kb.op('dve', lambda e: e.tensor_copy(out=sif[:], in_=si[:]), reads=["p_si"], writes=["p_sif"])
            sfa = sif[:].rearrange("p (h two) k -> p h two k", two=2)[:, :, 0, :]
            kb.op('dve', lambda e: e.tensor_scalar(out=sfa, in0=sfa, scalar1=128.0, scalar2=None, op0=ALU.mult), reads=["p_sif"], writes=["p_sif"])
            yield
            sv4 = sv[:].rearrange("p (h two) k -> p h two k", two=2)
            sf4 = sif[:].rearrange("p (h two) k -> p h two k", two=2)
            off = 0
            for (i0, ni, ln) in STAIR:
                cseg = cand[:, :, off:off + ni * ln].rearrange("p h (i j) -> p h i j", i=ni)
                xseg = cidx[:, :, off:off + ni * ln].rearrange("p h (i j) -> p h i j", i=ni)
                a0 = sv4[:, :, 0, i0:i0 + ni]
                a1 = sv4[:, :, 1, 0:ln]
                f0 = sf4[:, :, 0, i0:i0 + ni]
                f1 = sf4[:, :, 1, 0:ln]
                a0b = bc(a0, [a0.ap[0], a0.ap[1], a0.ap[2], [0, ln]])
                a1b = bc(a1, [a1.ap[0], a1.ap[1], [0, ni], a1.ap[2]])
                f0b = bc(f0, [f0.ap[0], f0.ap[1], f0.ap[2], [0, ln]])
                f1b = bc(f1, [f1.ap[0], f1.ap[1], [0, ni], f1.ap[2]])
                kb.op('dve', lambda e: e.tensor_tensor(out=cseg, in0=a0b, in1=a1b, op=ALU.add), reads=["p_sv"], writes=["p_cand"])
                kb.op('dve', lambda e: e.tensor_tensor(out=xseg, in0=f0b, in1=f1b, op=ALU.add), reads=["p_sif"], writes=["p_cidx"])
                off += ni * ln
            yield
            for h in range(8):
                kb.op('dve', lambda e: e.max(out=best[:, h, 0:8], in_=cand[:, h, :]), reads=["p_cand"], writes=["p_best"])
                kb.op('dve', lambda e: e.match_replace(out=cand2[:], in_to_replace=best[:, h, 0:8], in_values=cand[:, h, :], imm_value=-1e30),
                      reads=["p_cand", "p_best"], writes=["p_cand2"])
                kb.op('dve', lambda e: e.max(out=best[:, h, 8:16], in_=cand2[:]), reads=["p_cand2"], writes=["p_best"])
                for k in range(16):
                    kb.op('dve', lambda e: e.scalar_tensor_tensor(out=junk[:], in0=cand[:, h, :], scalar=best[:, h, k:k + 1], in1=cidx[:, h, :],
                                                                 op0=ALU.is_equal, op1=ALU.mult, accum_out=eif[:, h * 16 + k:h * 16 + k + 1]),
                          reads=["p_cand", "p_best", "p_cidx"] + (["p_eif"] if (h == 0 and k == 0) else []),
                          writes=(["p_eif"] if (k == 15 or (h == 0 and k == 0)) else []))
                yield
            kb.op('dve', lambda e: e.tensor_scalar(out=eif[:], in0=eif[:], scalar1=16383.0, scalar2=0.0, op0=ALU.min, op1=ALU.max),
                  reads=["p_eif"], writes=["p_eif"])
            kb.op('dve', lambda e: e.tensor_copy(out=eidx[pi][:], in_=eif[:]), reads=["p_eif"], writes=[f"p_eidx{pi}"])
            gt_ = gate[pi]
            kg = f"p_gate{pi}"
            bm = best[:, :, 0:1]
            kb.op('dve', lambda e: e.tensor_tensor(out=gt_[:], in0=best[:], in1=bc(bm, [bm.ap[0], bm.ap[1], [0, 16]]), op=ALU.subtract),
                  reads=["p_best"], writes=[kg])
            kb.op('act', lambda e: e.activation(out=gt_[:], in_=gt_[:], func=AF.Exp), reads=[kg], writes=[kg])
            kb.op('dve', lambda e: e.tensor_reduce(out=gsum[:], in_=gt_[:], axis=AX.X, op=ALU.add), reads=[kg], writes=["p_gsum"])
            kb.op('dve', lambda e: e.reciprocal(out=gsum[:], in_=gsum[:]), reads=["p_gsum"], writes=["p_gsum"])
            gs = gsum[:]
            kb.op('dve', lambda e: e.tensor_tensor(out=gt_[:], in0=gt_[:], in1=bc(gs, [gs.ap[0], gs.ap[1], [0, 16]]), op=ALU.mult),
                  reads=[kg, "p_gsum"], writes=[kg])
            yield

        NYIELD = 1 + 4 + 8 + 1 + 1 + 8 + 1

        def ggroup(n, grp):
            pi = n % 2
            for j in range(GS):
                k = grp * GS + j
                bi = state["ngb"] % NG
                state["ngb"] += 1
                slot_buf[(n, k)] = bi
                kb.dma(None, None, reads=[f"p_eidx{pi}"], writes=[f"p_gb{bi}"], q='pool',
                       fn=lambda e: e.indirect_dma_start(out=gb[bi][:], out_offset=None, in_=uvt,
                                                         in_offset=bass.IndirectOffsetOnAxis(ap=eidx[pi][:, k:k + 1], axis=0)))

        def dgroup(n, grp):
            pi = n % 2
            kd = f"p_dots{pi}_{grp}"
            for j in range(GS):
                k = grp * GS + j
                bi = slot_buf[(n, k)]
                kb.op('dve', lambda e: e.scalar_tensor_tensor(out=bigj[:], in0=gb[bi][:, 0:D], scalar=1.0, in1=xtok[pi][:], op0=ALU.mult, op1=ALU.mult,
                                                             accum_out=dots[pi][:, k:k + 1]),
                      reads=[f"p_gb{bi}", f"p_xtok{pi}"] + ([kd] if j == 0 else []), writes=([kd] if j in (0, GS - 1) else []))

        def vgroup(n, grp):
            pi = n % 2
            kd = f"p_dots{pi}_{grp}"
            cs = slice(grp * GS, (grp + 1) * GS)
            ka = f"p_act_{grp % 2}"
            kt = f"p_gtmp_{grp % 2}"
            gelu_tanh(kb, actv[:, cs], dots[pi][:, cs], gtmp[:, cs], kd, kt, ka, eng2='dve')
            gfl = gate[pi][:].rearrange("p h k -> p (h k)")
            kb.op('dve', lambda e: e.tensor_tensor(out=actv[:, cs], in0=actv[:, cs], in1=gfl[:, cs], op=ALU.mult),
                  reads=[ka, f"p_gate{pi}"], writes=[ka])
            for j in range(GS):
                k = grp * GS + j
                bi = slot_buf.pop((n, k))
                di = state["ndg"] % NDG
                state["ndg"] += 1
                kb.op('act', lambda e: e.activation(out=dg[di][:], in_=g.ident[:], func=AF.Identity, scale=actv[:, k:k + 1]),
                      reads=["ident", ka], writes=[f"p_dg{di}"])
                for half in range(2):
                    kb.op('pe', lambda e: e.matmul(vps[half][:], lhsT=dg[di][:], rhs=gb[bi][:, D + half * 512:D + (half + 1) * 512],
                                                   start=(k == 0), stop=(k == 127)),
                          reads=[f"p_dg{di}", f"p_gb{bi}"], writes=[f"p_vps{half}"])

        def vfin(n):
            t = tiles[n]
            xt, kx, ep = info.pop(n)
            for half in range(2):
                hs = slice(half * 512, (half + 1) * 512)
                kb.op('dve', lambda e: e.tensor_tensor(out=acc[:, hs], in0=vps[half][:], in1=m5[ep][:, hs], op=ALU.mult),
                      reads=[f"p_vps{half}", f"p_m5{ep}"], writes=["p_acc"])
            kb.op('dve', lambda e: e.tensor_tensor(out=acc[:], in0=acc[:], in1=xt[:], op=ALU.add), reads=["p_acc", kx], writes=["p_acc"])
            r0 = dst_row(t)
            g.out_toks.append(kb.dma(dst[r0:r0 + 128, :], acc[:], reads=["p_acc"], writes=["stream_out"]))

        NTL = len(tiles)
        for _ in routing(0):
            pass
        ggroup(0, 0)
        dgroup(0, 0)
        per = -(-NYIELD // (NGRP - 2))
        for n in range(NTL):
            rgen = routing(n + 1) if n + 1 < NTL else None
            for grp in range(NGRP):
                nxt = None
                if grp + 1 < NGRP:
                    nxt = (n, grp + 1)
                else:
                    if rgen is not None:
                        for _ in rgen:
                            pass
                        rgen = None
                    if n + 1 < NTL:
                        nxt = (n + 1, 0)
                if nxt is not None:
                    ggroup(*nxt)
                vgroup(n, grp)
                if nxt is not None:
                    dgroup(*nxt)
                if rgen is not None:
                    for _ in range(per):
                        if next(rgen, "done") == "done":
                            rgen = None
                            break
            vfin(n)


LT = TL // 128
ST_ = TS // 128


def stage_o1(kb, g, l, src):
    with kb.stage() as st:
        wbf = kb.sb("o1_w", [128, 8, 1536], BF16)
        load_weight_bf(kb, wbf, g.o_w_in, D, 1536, "o1_w")
        nt = NormT(kb, g, "o1n")
        hT = [kb.sb(f"o1_hT{i}", [128, 8, 128], BF16) for i in range(2)]
        gq = kb.sb("o1_gq", [128, 16, 64], F32)
        for hd in range(16):
            srcg = g.q_gain if hd < 12 else g.k_gain
            kb.dma(gq[:, hd, :], bc(srcg, [[0, 128], [1, 64]]), writes=["o1_gq"])
        cs_t = [kb.sb(f"o1_cs{i}", [128, 2, 32], F32) for i in range(2)]
        ps_sh = kb.ps("o1_pssh", [128, 512], F32)
        pss2 = [[kb.ps(f"o1_ps{i}_{j}", [128, 512], F32) for j in range(2)] + [ps_sh] for i in range(2)]
        tps1 = kb.ps("o1_tp", [128, 1024], BF16)
        tps = [tps1, tps1]
        qk_ = [kb.sb(f"o1_qk{i}", [128, 16, 64], F32) for i in range(2)]
        sq_ = [kb.sb(f"o1_sq{i}", [128, 16, 64], F32) for i in range(2)]
        ss_ = [kb.sb(f"o1_ss{i}", [128, 16], F32) for i in range(2)]
        t1_ = [kb.sb(f"o1_t1{i}", [128, 16, 32], F32) for i in range(2)]
        t2_ = [kb.sb(f"o1_t2{i}", [128, 16, 32], F32) for i in range(2)]
        qr_ = [kb.sb(f"o1_qr{i}", [128, 16, 64], BF16) for i in range(2)]
        qrT = [kb.sb(f"o1_qrT{i}", [64, 16, 128], BF16) for i in range(2)]
        vz = [kb.sb(f"o1_vz{i}", [128, 512], BF16) for i in range(2)]
        def phase_a(t):
            i = t % 2
            pss = pss2[i]
            PK = f"o1_ps{i}_"
            PKN = [PK + "0", PK + "1", "o1_pssh"]
            qk = qk_[i]
            KQ = f"o1_qk{i}"
            nt.run(src[t * 128:(t + 1) * 128, :], l, 0, mod_col(t), hT[i][:], f"o1_hT{i}")
            bg_step(g)
            kb.dma(cs_t[i][:], g.rope[(t % ST_) * 128:(t % ST_ + 1) * 128], writes=[f"o1_cs{i}"])
            for bnk in range(3):
                for k in range(8):
                    kb.op('pe', lambda e: e.matmul(pss[bnk][:], lhsT=hT[i][:, k, :], rhs=wbf[:, k, bnk * 512:(bnk + 1) * 512],
                                                   start=(k == 0), stop=(k == 7)), reads=[f"o1_hT{i}", "o1_w"], writes=[PKN[bnk]])
            qkf = qk[:].rearrange("p h d -> p (h d)")
            kb.op('act', lambda e: e.activation(out=qkf[:, 0:512], in_=pss[0][:], func=AF.Copy), reads=[PK + "0"], writes=[KQ])
            kb.op('act', lambda e: e.activation(out=qkf[:, 512:1024], in_=pss[1][:], func=AF.Copy), reads=[PK + "1"], writes=[KQ])
            kb.op('act', lambda e: e.activation(out=vz[i][:], in_=pss[2][:], func=AF.Copy), reads=["o1_pssh"], writes=[f"o1_vz{i}"])
            kb.dma(g.vtok2[t * 128:(t + 1) * 128, :], vz[i][:, 0:256], reads=[f"o1_vz{i}"], writes=["vtok2"])
            kb.dma(g.ztok[t * 128:(t + 1) * 128, :], vz[i][:, 256:512], reads=[f"o1_vz{i}"], writes=["ztok"])

        def phase_b(t):
            i = t % 2
            qk, sq, ss, t1, t2, qr = qk_[i], sq_[i], ss_[i], t1_[i], t2_[i], qr_[i]
            KQ, KS, KSS, KT1, KT2, KQR = f"o1_qk{i}", f"o1_sq{i}", f"o1_ss{i}", f"o1_t1{i}", f"o1_t2{i}", f"o1_qr{i}"
            kb.op('dve', lambda e: e.tensor_tensor(out=sq[:], in0=qk[:], in1=qk[:], op=ALU.mult), reads=[KQ], writes=[KS])
            kb.op('dve', lambda e: e.tensor_reduce(out=ss[:], in_=sq[:], axis=AX.X, op=ALU.add), reads=[KS], writes=[KSS])
            kb.op('dve', lambda e: e.tensor_scalar(out=ss[:], in0=ss[:], scalar1=1.0 / 64.0, scalar2=EPS, op0=ALU.mult, op1=ALU.add),
                  reads=[KSS], writes=[KSS])
            kb.op('act', lambda e: e.activation(out=ss[:], in_=ss[:], func=AF.Sqrt), reads=[KSS], writes=[KSS])
            kb.op('dve', lambda e: e.reciprocal(out=ss[:], in_=ss[:]), reads=[KSS], writes=[KSS])
            ssv = ss[:]
            kb.op('dve', lambda e: e.tensor_tensor(out=qk[:], in0=qk[:], in1=bc(ssv, [ssv.ap[0], ssv.ap[1], [0, 64]]), op=ALU.mult),
                  reads=[KQ, KSS], writes=[KQ])
            kb.op('dve', lambda e: e.tensor_tensor(out=qk[:], in0=qk[:], in1=gq[:], op=ALU.mult), reads=[KQ, "o1_gq"], writes=[KQ])
            cosv = cs_t[i][:, 0, :]
            sinv = cs_t[i][:, 1, :]
            cb_ = bc(cosv, [cosv.ap[0], [0, 16], [1, 32]])
            sb_ = bc(sinv, [sinv.ap[0], [0, 16], [1, 32]])
            x1 = qk[:, :, 0:32]
            x2 = qk[:, :, 32:64]
            kc = f"o1_cs{i}"
            kb.op('dve', lambda e: e.tensor_tensor(out=t1[:], in0=x1, in1=cb_, op=ALU.mult), reads=[KQ, kc], writes=[KT1])
            kb.op('dve', lambda e: e.tensor_tensor(out=t2[:], in0=x2, in1=sb_, op=ALU.mult), reads=[KQ, kc], writes=[KT2])
            kb.op('dve', lambda e: e.tensor_tensor(out=qr[:, :, 0:32], in0=t1[:], in1=t2[:], op=ALU.subtract),
                  reads=[KT1, KT2], writes=[KQR])
            kb.op('dve', lambda e: e.tensor_tensor(out=t1[:], in0=x2, in1=cb_, op=ALU.mult), reads=[KQ, kc, KQR], writes=[KT1])
            kb.op('dve', lambda e: e.tensor_tensor(out=t2[:], in0=x1, in1=sb_, op=ALU.mult), reads=[KQ, kc, KQR], writes=[KT2])
            kb.op('dve', lambda e: e.tensor_tensor(out=qr[:, :, 32:64], in0=t1[:], in1=t2[:], op=ALU.add),
                  reads=[KT1, KT2], writes=[KQR])
            for bnk in range(2):
                for hd in range(bnk * 8, bnk * 8 + 8):
                    kb.op('pe', lambda e: e.transpose(out=tps1[0:64, (hd % 8) * 128:(hd % 8 + 1) * 128], in_=qr[:, hd, :], identity=g.ident[:]),
                          reads=[KQR, "ident"], writes=["o1_tp"])
                tv = tps1[0:64, :].rearrange("p (h t) -> p h t", h=8)
                if bnk == 0:
                    kb.op('act', lambda e: e.activation(out=qrT[i][:, 0:8, :], in_=tv, func=AF.Copy), reads=["o1_tp"], writes=[f"o1_qrT{i}"])
                else:
                    kb.op('dve', lambda e: e.tensor_copy(out=qrT[i][:, 8:16, :], in_=tv), reads=["o1_tp"], writes=[f"o1_qrT{i}"])
            dstT = AP(g.qkT.tensor, g.qkT.offset + t * 128, [[NT, 64], [64 * NT, 16], [1, 128]])
            kb.dma(dstT, qrT[i][:], reads=[f"o1_qrT{i}"], writes=["qkT"])

        phase_a(0)
        for t in range(NTILE):
            if t + 1 < NTILE:
                phase_a(t + 1)
            phase_b(t)


def stage_o2(kb, g):
    with kb.stage() as st:
        kT = kb.sb("a_kT", [64, TS], BF16)
        qT = kb.sb("a_qT", [64, 3, TL], BF16)
        V = kb.sb("a_V", [128, ST_, 64], BF16)
        aT = kb.sb("a_aT", [64, 3, TL], BF16)
        ones = kb.sb("a_ones", [128, 64], BF16)
        kb.op('pool', lambda e: e.memset(ones[:], 1.0), writes=["a_ones"])
        am32 = kb.sb("a_m32", [128, 2, 128], F32)
        am = kb.sb("a_m", [128, 2, 128], BF16)
        kb.dma(am32[:], g.amask[:], writes=["a_m32"])
        kb.op('dve', lambda e: e.tensor_copy(out=am[:], in_=am32[:]), reads=["a_m32"], writes=["a_m"])
        es = kb.sb("a_es", [64, 12], F32)
        kb.dma(es[:], bc(g.sinks, [[0, 64], [1, 12]]), writes=["a_es"])
        kb.op('act', lambda e: e.activation(out=es[:], in_=es[:], func=AF.Exp), reads=["a_es"], writes=["a_es"])
        P = [kb.sb(f"a_P{i}", [128, 3, 128], BF16) for i in range(4)]
        den = kb.sb("a_den", [64, 3, 128], F32)
        s_ps = [kb.ps(f"a_sps{i}", [128, 512], F32) for i in range(3)]
        o_ps = [kb.ps(f"a_ops{i}", [128, 512], F32) for i in range(2)]
        d_ps = [kb.ps(f"a_dps{i}", [128, 512], F32) for i in range(2)]
        nS = 0
        nP = 0
        nG = 0
        for b in range(NB):
            c0 = b * TS
            for kvh in range(4):
                kb.dma(kT[:], g.qkT[12 + kvh, :, c0:c0 + TS], reads=["qkT"], writes=["a_kT"])
                for gi in range(3):
                    kb.dma(qT[:, gi, :], g.qkT[kvh * 3 + gi, :, c0 + TC:c0 + TS], reads=["qkT"], writes=["a_qT"])
                vsrc = AP(g.vtok2.tensor, g.vtok2.offset + c0 * 256 + kvh * 64, [[256, 128], [128 * 256, ST_], [1, 64]])
                kb.dma(V[:], vsrc, reads=["vtok2"], writes=["a_V"])
                for n in range(LT):
                    if n % 4 == 1:
                        bg_step(g)
                    og = nG % 2
                    nG += 1
                    kts = [(0, None), (1, None)]
                    for m in (n - 1, n, n + 1):
                        if 0 <= m < LT:
                            kts.append((2 + m, 0 if m == n - 1 else (1 if m == n + 1 else None)))
                    for ki, (kt, mk) in enumerate(kts):
                        sp = nS % 3
                        nS += 1
                        pp = nP % 4
                        nP += 1
                        kb.op('pe', lambda e: e.matmul(s_ps[sp][:, 0:384], lhsT=kT[:, kt * 128:(kt + 1) * 128],
                                                       rhs=qT[:, :, n * 128:(n + 1) * 128], start=True, stop=True),
                              reads=["a_kT", "a_qT"], writes=[f"a_sps{sp}"])
                        kb.op('act', lambda e: e.activation(out=P[pp][:].rearrange("p g t -> p (g t)"), in_=s_ps[sp][:, 0:384],
                                                            func=AF.Exp, scale=0.125), reads=[f"a_sps{sp}"], writes=[f"a_P{pp}"])
                        if mk is not None:
                            mv = am[:, mk, :]
                            kb.op('dve', lambda e: e.tensor_tensor(out=P[pp][:], in0=P[pp][:], in1=bc(mv, [mv.ap[0], [0, 3], [1, 128]]),
                                                                    op=ALU.mult), reads=[f"a_P{pp}", "a_m"], writes=[f"a_P{pp}"])
                        first = ki == 0
                        lastk = ki == len(kts) - 1
                        Pf = P[pp][:].rearrange("p g t -> p (g t)")
                        kb.op('pe', lambda e: e.matmul(o_ps[og][0:64, 0:384], lhsT=V[:, kt, :], rhs=Pf, start=first, stop=lastk),
                              reads=["a_V", f"a_P{pp}"], writes=[f"a_ops{og}"])
                        kb.op('pe', lambda e: e.matmul(d_ps[og][0:64, 0:384], lhsT=ones[:], rhs=Pf, start=first, stop=lastk),
                              reads=["a_ones", f"a_P{pp}"], writes=[f"a_dps{og}"])
                    esv = es[:, kvh * 3:kvh * 3 + 3]
                    kb.op('dve', lambda e: e.tensor_tensor(out=den[:], in0=d_ps[og][0:64, 0:384].rearrange("p (g t) -> p g t", g=3),
                                                           in1=bc(esv, [esv.ap[0], [1, 3], [0, 128]]), op=ALU.add),
                          reads=[f"a_dps{og}", "a_es"], writes=["a_den"])
                    kb.op('dve', lambda e: e.reciprocal(out=den[:], in_=den[:]), reads=["a_den"], writes=["a_den"])
                    kb.op('dve', lambda e: e.tensor_tensor(out=aT[:, :, n * 128:(n + 1) * 128],
                                                           in0=o_ps[og][0:64, 0:384].rearrange("p (g t) -> p g t", g=3), in1=den[:], op=ALU.mult),
                          reads=[f"a_ops{og}", "a_den"], writes=["a_aT"])
                for gi in range(3):
                    hq = kvh * 3 + gi
                    kb.dma(g.ymT[hq * 64:(hq + 1) * 64, c0 + TC:c0 + TS], aT[:, gi, :], reads=["a_aT"], writes=["ymT"])


def stage_o3(kb, g):
    with kb.stage() as st:
        Z = kb.sb("f_Z", [128, 16, 256], BF16)
        tb = [[kb.sb(f"f_tb{j}{i}", [128, 16, 512], BF16) for i in range(2)] for j in range(2)]
        c64 = kb.sb("f_c64", [128, 2, 128], F32)
        c64b = kb.sb("f_c64b", [128, 2, 128], BF16)
        kb.dma(c64[:], g.dft64[:], writes=["f_c64"])
        kb.op('dve', lambda e: e.tensor_copy(out=c64b[:], in_=c64[:]), reads=["f_c64"], writes=["f_c64b"])
        PQ = [kb.sb(f"f_PQ{j}", [128, 512], BF16) for j in range(2)]
        yv = [kb.sb(f"f_y{i}", [128, 512], BF16) for i in range(2)]
        pq_ps = [kb.ps(f"f_pqps{j}", [128, 512], F32) for j in range(2)]
        y_ps = [kb.ps(f"f_yps{i}", [128, 512], F32) for i in range(2)]
        ny = 0
        nb_ = 0
        for b in range(NB):
            c0 = b * TS + TC
            zsrc = AP(g.ztok.tensor, g.ztok.offset + c0 * 256, [[256, 128], [128 * 256, 16], [1, 256]])
            kb.dma(Z[:], zsrc, reads=["ztok"], writes=["f_Z"])
            for tblk in range(4):
                bi = nb_ % 2
                nb_ += 1
                for j in range(2):
                    kb.dma(tb[j][bi][:], g.dftT[j, :, tblk * 512:(tblk + 1) * 512].rearrange("(sc p) t -> p sc t", p=128),
                           writes=[f"f_tb{j}{bi}"])
                for cc in range(2):
                    for j in range(2):
                        for sc in range(16):
                            kb.op('pe', lambda e: e.matmul(pq_ps[j][:], lhsT=Z[:, sc, cc * 128:(cc + 1) * 128], rhs=tb[j][bi][:, sc, :],
                                                           start=(sc == 0), stop=(sc == 15)),
                                  reads=["f_Z", f"f_tb{j}{bi}"], writes=[f"f_pqps{j}"])
                        kb.op('act' if j == 0 else 'dve',
                              (lambda e: e.activation(out=PQ[0][:], in_=pq_ps[0][:], func=AF.Copy)) if j == 0 else
                              (lambda e: e.tensor_copy(out=PQ[1][:], in_=pq_ps[1][:])),
                              reads=[f"f_pqps{j}"], writes=[f"f_PQ{j}"])
                    yi = ny % 2
                    ny += 1
                    kb.op('pe', lambda e: e.matmul(y_ps[yi][:], lhsT=c64b[:, 0, :], rhs=PQ[0][:], start=True, stop=False),
                          reads=["f_c64b", "f_PQ0"], writes=[f"f_yps{yi}"])
                    kb.op('pe', lambda e: e.matmul(y_ps[yi][:], lhsT=c64b[:, 1, :], rhs=PQ[1][:], start=False, stop=True),
                          reads=["f_c64b", "f_PQ1"], writes=[f"f_yps{yi}"])
                    kb.op('act', lambda e: e.activation(out=yv[yi][:], in_=y_ps[yi][:], func=AF.Copy), reads=[f"f_yps{yi}"], writes=[f"f_y{yi}"])
                    kb.dma(g.ymT[768 + cc * 128:768 + (cc + 1) * 128, c0 + tblk * 512:c0 + (tblk + 1) * 512], yv[yi][:],
                           reads=[f"f_y{yi}"], writes=["ymT"])


def stage_o4(kb, g, l, src, dst):
    with kb.stage() as st:
        op = OutProj(kb, g, l, g.o_w_out, 2, "o4")
        yT = [kb.sb(f"o4_yT{i}", [128, 8, 512], BF16) for i in range(2)]
        n = 0
        for b in range(NB):
            for q4 in range(LT // 4):
                t0 = b * ST_ + 2 + q4 * 4
                i = n % 2
                n += 1
                kb.dma(yT[i][:], g.ymT[:, t0 * 128:t0 * 128 + 512].rearrange("(k p) t -> p k t", p=128), reads=["ymT"], writes=[f"o4_yT{i}"])
                op.run(yT[i], f"o4_yT{i}", None, src, dst, tiles=[t0 + j for j in range(4)])


IN_SPECS = {
    "s_in": ([NT, D], F32),
    "cT": ([128, 8, 3], F32),
    "w_mod": ([2, D, 6 * D], F32),
    "b_mod": ([2, 6 * D], F32),
    "gcol": ([128, 2, 2, 8], F32),
    "lbl": ([128, 2, 2, 4], F32),
    "identf": ([128, 128], F32),
    "e_w_in": ([D, 3584], F32),
    "hmask": ([64, 2, 64], F32),
    "rg_blk": ([128, 16, 128], F32),
    "rg_sm": ([128, 44], F32),
    "e_w_out": ([D, D], F32),
    "hg_gain": ([128, 1], F32),
    "p_wq": ([2, D, 2048], F32),
    "keysT": ([2, 128, 16, 128], F32),
    "bqc": ([2, 128, 16], F32),
    "gffn": ([2, D], F32),
    "p_u0": ([16384, D], F32),
    "p_u1": ([16384, D], F32),
    "p_v0": ([16384, D], F32),
    "p_v1": ([16384, D], F32),
    "o_w_in": ([D, 1536], F32),
    "o_w_out": ([D, D], F32),
    "q_gain": ([64], F32),
    "k_gain": ([64], F32),
    "sinks": ([12], F32),
    "rope": ([TS, 2, 32], F32),
    "amask": ([128, 2, 128], F32),
    "dft64": ([128, 2, 128], F32),
    "dftT": ([2, TL, TL], BF16),
}


def build(upto="all", dbg=(), ptiles=None, ptiles1=None):
    nc = bass.Bass("TRN2", target_bir_lowering=False)
    kb = KB(nc)
    g = Ctx()
    g.ptiles = ptiles
    g.ptiles1 = ptiles1
    g.dbg = set(dbg)
    for name, (shape, dt) in IN_SPECS.items():
        setattr(g, name, nc.dram_tensor(name, list(shape), dt, kind="ExternalInput").ap())

    def scratch(name, shape, dt):
        kind = "ExternalOutput" if name in g.dbg else "Internal"
        t = nc.dram_tensor(name, list(shape), dt, kind=kind).ap()
        setattr(g, name, t)
        return t

    scratch("mrow", [2, 3, 6 * D], F32)
    g.out = nc.dram_tensor("out", [NB * TL, D], F32, kind="ExternalOutput").ap()

    scratch("fm", [3072, NT], F32)
    scratch("vtok", [NT, 512], BF16)
    g.stream = g.s_in
    scratch("uv0", [16384, 2 * D], BF16)
    scratch("uv1", [16384, 2 * D], BF16)
    scratch("oT", [512, NT], F32)
    scratch("ybT", [512, NT], BF16)
    scratch("s1", [NT, D], F32)
    scratch("s2", [NT, D], F32)
    scratch("qkT", [16, 64, NT], BF16)
    scratch("vtok2", [NT, 256], BF16)
    scratch("ztok", [NT, 256], BF16)
    scratch("ymT", [D, NT], BF16)
    scratch("s3", [NT, D], F32)
    alloc_globals(kb, g)
    g.bg = None
    if upto.startswith("odd"):
        with kb.stage() as outer0:
            stage_mod(kb, g)
            stage_prep(kb, g)
    else:
      with kb.stage() as outer0:
        g.bg = UVConv(kb, g, 0)
        stage_mod(kb, g)
        stage_prep(kb, g)
        stage_e1(kb, g)
        if upto != "e1":
            stage_e2(kb, g)
        if upto not in ("e1", "e2"):
            stage_e3(kb, g)
        if upto not in ("e1", "e2", "e3"):
            stage_e4(kb, g, 0, g.s_in, g.s1)
        if upto not in ("e1", "e2", "e3", "e4"):
            g.bg.finish()
        g.bg = None
    if upto in ("e1", "e2", "e3", "e4"):
        return finish(kb, nc)
    if True:
        pass
    g.out_toks = []
    ptiles = list(range(NTILE)) if g.ptiles is None else g.ptiles
    if not upto.startswith("odd"):
        stage_peer(kb, g, 0, g.s1, g.s2, ptiles, lambda t: t * 128)
    if upto == "p0":
        return finish(kb, nc)
    s2 = g.s_in if upto.startswith("odd") else g.s2
    with kb.stage() as outer1:
        g.bg = UVConv(kb, g, 1)
        stage_o1(kb, g, 1, s2)
        if upto != "odd1":
            stage_o2(kb, g)
        if upto not in ("odd1", "odd2"):
            stage_o3(kb, g)
        if upto not in ("odd1", "odd2", "odd3"):
            stage_o4(kb, g, 1, s2, g.s3)
        if not upto.startswith("odd"):
            g.bg.finish()
        g.bg = None
    if upto.startswith("odd"):
        return finish(kb, nc)
    lat_tiles = [b * ST_ + 2 + j for b in range(NB) for j in range(LT)]
    g.out_toks = []
    stage_peer(kb, g, 1, g.s3, g.out, lat_tiles if g.ptiles1 is None else g.ptiles1,
               lambda t: (t // ST_) * TL + (t % ST_ - 2) * 128)
    return finish(kb, nc)


def finish(kb, nc):
    kb.barrier()
    print(f"[build] inst={kb.n_inst} waits={kb.n_wait} dmas={kb.ndma}")
    kb.close()
    return nc


def colform(v):
    v = np.asarray(v, np.float32)
    return np.ascontiguousarray(v.reshape(-1, 128).T)


_SHARED = {}


def prep_core(inp, core):
    b0 = core * NB
    key = id(inp)
    if key not in _SHARED:
        _SHARED.clear()
        _SHARED[key] = _prep_shared(inp)
    m = dict(_SHARED[key])
    s = np.concatenate([np.concatenate([inp["ctx"][b0 + i], inp["x"][b0 + i]], axis=0) for i in range(NB)], axis=0)
    m["s_in"] = np.ascontiguousarray(s, np.float32)
    cv = np.stack([inp["c"][b0], inp["c"][b0 + 1], inp["c_ctx"]], axis=1)
    m["cT"] = np.ascontiguousarray(cv.reshape(8, 128, 3).transpose(1, 0, 2), np.float32)
    return m


def _prep_shared(inp):
    m = {}
    m["w_mod"] = np.ascontiguousarray(inp["w_mod"], np.float32)
    m["b_mod"] = np.ascontiguousarray(inp["b_mod"], np.float32)
    gc = np.zeros((128, 2, 2, 8), np.float32)
    for l in range(2):
        gc[:, l, 0, :] = colform(inp["g_mix"][l])
        gc[:, l, 1, :] = colform(inp["g_ffn"][l])
    m["gcol"] = gc
    lbl = np.zeros((128, 2, 2, 4), np.float32)
    for d in range(2):
        for j in range(2):
            lbl[:, d, j, :] = colform(inp["hg_lb_logits"][d, j])
    m["lbl"] = lbl
    m["identf"] = np.eye(128, dtype=np.float32)
    hm = np.zeros((64, 2, 64), np.float32)
    ii = np.arange(64)
    hm[:, 0, :] = (ii[:, None] <= ii[None, :])
    hm[:, 1, :] = (ii[:, None] >= ii[None, :])
    m["hmask"] = hm
    blk = np.zeros((128, 16, 128), np.float32)
    for ct in range(4):
        for ty, wname in enumerate(("rg_wa", "rg_wx")):
            for d in range(2):
                wi = (ct * 2 + ty) * 2 + d
                for half in range(2):
                    blk[half * 64:(half + 1) * 64, wi, half * 64:(half + 1) * 64] = inp[wname][0, d, 2 * ct + half]
    m["rg_blk"] = blk
    sm = np.zeros((128, 44), np.float32)
    for j in range(4):
        sm[:, j * 4:(j + 1) * 4] = colform(inp["rg_conv_w"][0, j])
    sm[:, 16:20] = colform(inp["rg_conv_b"][0])
    for d in range(2):
        sm[:, 20 + d * 4:24 + d * 4] = colform(inp["rg_ba"][0, d])
        sm[:, 28 + d * 4:32 + d * 4] = colform(inp["rg_bx"][0, d])
        sm[:, 36 + d * 4:40 + d * 4] = colform(inp["rg_lambda"][0, d])
    m["rg_sm"] = sm
    m["e_w_out"] = np.ascontiguousarray(inp["e_w_out"][0], np.float32)
    m["p_wq"] = np.ascontiguousarray(inp["p_wq"], np.float32)
    m["keysT"] = np.ascontiguousarray(inp["p_keys"].reshape(2, 16, 128, 128).transpose(0, 3, 1, 2), np.float32)
    m["bqc"] = np.ascontiguousarray(inp["p_bq"].reshape(2, 16, 128).transpose(0, 2, 1), np.float32)
    m["gffn"] = np.ascontiguousarray(inp["g_ffn"], np.float32)
    for l in range(2):
        m[f"p_u{l}"] = np.ascontiguousarray(inp["p_u"][l], np.float32)
        m[f"p_v{l}"] = np.ascontiguousarray(inp["p_v"][l], np.float32)
    m["o_w_in"] = np.ascontiguousarray(inp["o_w_in"][0], np.float32)
    m["o_w_out"] = np.ascontiguousarray(inp["o_w_out"][0], np.float32)
    m["q_gain"] = np.ascontiguousarray(inp["q_gain"][0], np.float32)
    m["k_gain"] = np.ascontiguousarray(inp["k_gain"][0], np.float32)
    m["sinks"] = np.ascontiguousarray(inp["sinks"][0], np.float32)
    m.update(const_tables())
    m["hg_gain"] = np.ascontiguousarray(inp["hg_gain"][0].reshape(128, 1), np.float32)
    m["e_w_in"] = np.ascontiguousarray(inp["e_w_in"][0], np.float32)
    return m


_CONST = {}


def const_tables():
    if _CONST:
        return _CONST
    import ml_dtypes
    rope = np.zeros((TS, 2, 32), np.float32)
    rope[:TC, 0, :] = 1.0
    pos = np.arange(TL)
    row = (pos // 64).astype(np.float64)
    colp = (pos % 64).astype(np.float64)
    inv = 10000.0 ** (-np.arange(16, dtype=np.float64) / 16)
    ang = np.concatenate([row[:, None] * inv, colp[:, None] * inv], axis=-1)
    rope[TC:, 0, :] = np.cos(ang)
    rope[TC:, 1, :] = np.sin(ang)
    _CONST["rope"] = rope
    ii = np.arange(128)
    am = np.zeros((128, 2, 128), np.float32)
    am[:, 0, :] = (ii[:, None] >= ii[None, :])
    am[:, 1, :] = (ii[:, None] <= ii[None, :])
    _CONST["amask"] = am
    k64 = np.arange(64)
    a64 = 2 * np.pi * np.outer(k64, k64) / 64
    c64 = np.cos(a64) / 8.0
    s64 = np.sin(a64) / 8.0
    d64 = np.zeros((128, 2, 128), np.float32)
    for hlf in range(2):
        d64[hlf * 64:(hlf + 1) * 64, 0, hlf * 64:(hlf + 1) * 64] = c64
        d64[hlf * 64:(hlf + 1) * 64, 1, hlf * 64:(hlf + 1) * 64] = -s64
    _CONST["dft64"] = d64
    kT = np.arange(TL)
    aT = 2 * np.pi * ((np.outer(kT, kT) % TL).astype(np.float64)) / TL
    dT = np.stack([np.cos(aT), np.sin(aT)], 0) / np.sqrt(TL)
    _CONST["dftT"] = dT.astype(ml_dtypes.bfloat16)
    return _CONST


_PROG = {}


def kernel(**inputs):
    inp = {k: np.asarray(v) for k, v in inputs.items()}
    if "nc" not in _PROG:
        _PROG["nc"] = build("all")
    nc = _PROG["nc"]
    maps = [prep_core(inp, c) for c in range(8)]
    res = run_bass_kernel_spmd(nc, maps, core_ids=list(range(8)))
    out = np.concatenate([np.asarray(r["out"]).reshape(NB, TL, D) for r in res.results], axis=0)
    return np.ascontiguousarray(out, dtype=np.float32)
```

```python
import numpy as np
from contextlib import ExitStack
import concourse.bass as bass
import concourse.mybir as mybir
from concourse.bass_utils import run_bass_kernel_spmd

F32 = mybir.dt.float32
BF16 = mybir.dt.bfloat16
I32 = mybir.dt.int32
U32 = mybir.dt.uint32
ALU = mybir.AluOpType
AF = mybir.ActivationFunctionType
AX = mybir.AxisListType
AP = bass.AP

NDS = 24
EPOCH = 30000


class KB:
    def __init__(self, nc):
        self.nc = nc
        self.es = ExitStack()
        self.engs = {'pe': nc.tensor, 'dve': nc.vector, 'act': nc.scalar, 'pool': nc.gpsimd, 'sp': nc.sync}
        self.csem = {}
        self.ccnt = {}
        self.nsem = 0
        for e in self.engs:
            self._new_epoch(e)
        self.dsem = [self.es.enter_context(nc.semaphore(f"dma{i}")) for i in range(NDS)]
        self.dcount = [0] * NDS
        self.ndma = 0
        self.seen = {e: {} for e in self.engs}
        self.lastw = {}
        self.readers = {}
        self.stage_stack = []
        self.n_inst = 0
        self.n_wait = 0

    def _new_epoch(self, e):
        self.nsem += 1
        self.csem[e] = (f"c{e}{self.nsem}", self.es.enter_context(self.nc.semaphore(f"c_{e}_{self.nsem}")))
        self.ccnt[e] = 0

    def sb(self, name, shape, dtype, stack=None):
        st = stack if stack is not None else (self.stage_stack[-1] if self.stage_stack else self.es)
        self.uid = getattr(self, "uid", 0) + 1
        return st.enter_context(self.nc.sbuf_tensor(f"sb{self.uid}_{name}", list(shape), dtype))

    def ps(self, name, shape, dtype, stack=None):
        st = stack if stack is not None else (self.stage_stack[-1] if self.stage_stack else self.es)
        self.uid = getattr(self, "uid", 0) + 1
        return st.enter_context(self.nc.psum_tensor(f"ps{self.uid}_{name}", list(shape), dtype))

    def dram(self, name, shape, dtype, kind="Internal"):
        return self.nc.dram_tensor(name, list(shape), dtype, kind=kind)

    def _need(self, eng, tok, waits):
        if tok is None:
            return
        key, sem, cnt, teng = tok
        if teng == 'pe' and eng == 'pe':
            return
        if self.seen[eng].get(key, 0) >= cnt:
            return
        if key not in waits or waits[key][1] < cnt:
            waits[key] = (sem, cnt)

    def _collect(self, eng, reads, writes):
        waits = {}
        for r in reads:
            self._need(eng, self.lastw.get(r), waits)
        for w in writes:
            self._need(eng, self.lastw.get(w), waits)
            for t in self.readers.get(w, {}).values():
                self._need(eng, t, waits)
        for key, (sem, cnt) in waits.items():
            self.engs[eng].wait_ge(sem, cnt)
            self.seen[eng][key] = cnt
            self.n_wait += 1

    def _record(self, tok, reads, writes):
        for r in reads:
            self.readers.setdefault(r, {})[tok[0] if tok[3] == 'dma' else tok[3]] = tok
        for w in writes:
            self.lastw[w] = tok
            self.readers[w] = {}

    def op(self, eng, fn, reads=(), writes=()):
        self._collect(eng, reads, writes)
        inst = fn(self.engs[eng])
        if self.ccnt[eng] >= EPOCH:
            self._new_epoch(eng)
        key, sem = self.csem[eng]
        inst.then_inc(sem, 1)
        self.ccnt[eng] += 1
        tok = (key, sem, self.ccnt[eng], eng)
        self._record(tok, reads, writes)
        self.n_inst += 1
        return tok

    def dma(self, out, in_, reads=(), writes=(), q='sp', fn=None, **kw):
        i = self.ndma % NDS
        sem = self.dsem[i]
        self._collect(q, reads, writes)
        if self.dcount[i] > 0 and self.seen[q].get(f"d{i}", 0) < 16 * self.dcount[i]:
            self.engs[q].wait_ge(sem, 16 * self.dcount[i])
            self.seen[q][f"d{i}"] = 16 * self.dcount[i]
        if fn is not None:
            inst = fn(self.engs[q])
        else:
            inst = self.engs[q].dma_start(out=out, in_=in_, **kw)
        inst.then_inc(sem, 16)
        self.dcount[i] += 1
        self.ndma += 1
        tok = (f"d{i}", sem, 16 * self.dcount[i], 'dma')
        self._record(tok, reads, writes)
        self.n_inst += 1
        return tok

    def barrier(self):
        for e in self.engs:
            for f in self.engs:
                if f == e:
                    continue
                key, sem = self.csem[f]
                c = self.ccnt[f]
                if c > 0 and self.seen[e].get(key, 0) < c:
                    self.engs[e].wait_ge(sem, c)
                    self.seen[e][key] = c
            for i in range(NDS):
                c = 16 * self.dcount[i]
                if c > 0 and self.seen[e].get(f"d{i}", 0) < c:
                    self.engs[e].wait_ge(self.dsem[i], c)
                    self.seen[e][f"d{i}"] = c
        self.lastw = {}
        self.readers = {}

    def final_wait(self, toks, eng='sp'):
        for tok in toks:
            key, sem, cnt, _ = tok
            self.engs[eng].wait_ge(sem, cnt)

    class _Stage:
        def __init__(self, kb):
            self.kb = kb

        def __enter__(self):
            st = ExitStack()
            self.kb.stage_stack.append(st)
            return st

        def __exit__(self, *a):
            self.kb.barrier()
            st = self.kb.stage_stack.pop()
            st.close()

    def stage(self):
        return KB._Stage(self)

    def close(self):
        self.es.close()


def bc(ap, shape_pairs, offset_add=0):
    return AP(ap.tensor, ap.offset + offset_add, [list(p) for p in shape_pairs])


D = 1024
NB = 2
TC = 256
TL = 2048
TS = TC + TL
NT = NB * TS
NTILE = NT // 128
EPS = 1e-6


def tile_info(t):
    b = t // (TS // 128)
    r = t % (TS // 128)
    return b, r < (TC // 128)


def mod_col(t):
    b, isc = tile_info(t)
    return 2 if isc else b


class Ctx:
    def dump(self, kb, name, ap, key, dtype=F32):
        if name not in self.dbg or hasattr(self, "_d_" + name):
            return
        t = kb.nc.dram_tensor(name, list(ap.shape), dtype, kind="ExternalOutput").ap()
        setattr(self, "_d_" + name, t)
        kb.dma(t, ap, reads=[key], writes=["dbg_" + name])


def stage_mod(kb, g):
    with kb.stage() as st:
        cT = kb.sb("m_cT", [128, 8, 3], F32)
        sc = kb.sb("m_sc", [128, 8, 3], F32)
        kb.dma(cT[:], g.cT[:], writes=["m_cT"])
        kb.op('act', lambda e: e.activation(out=sc[:], in_=cT[:], func=AF.Silu), reads=["m_cT"], writes=["m_sc"])
        wb = [kb.sb(f"m_w{i}", [128, 8, 512], F32) for i in range(2)]
        bb = [kb.sb(f"m_b{i}", [3, 512], F32) for i in range(2)]
        ob = [kb.sb(f"m_o{i}", [3, 512], F32) for i in range(2)]
        pss = [kb.ps(f"m_ps{i}", [3, 512], F32) for i in range(2)]
        it = 0
        for l in range(2):
            for cb in range(12):
                i = it % 2
                it += 1
                kb.dma(wb[i][:], g.w_mod[l, :, cb * 512:(cb + 1) * 512].rearrange("(k p) n -> p k n", p=128),
                       writes=[f"m_w{i}"])
                kb.dma(bb[i][:], bc(g.b_mod[l, cb * 512:(cb + 1) * 512], [[0, 3], [1, 512]]), writes=[f"m_b{i}"])
                for k in range(8):
                    kb.op('pe', lambda e, k=k, i=i: e.matmul(pss[i][:], lhsT=sc[:, k, :], rhs=wb[i][:, k, :],
                                                             start=(k == 0), stop=(k == 7)),
                          reads=["m_sc", f"m_w{i}"], writes=[f"m_ps{i}"])
                kb.op('dve', lambda e, i=i: e.tensor_tensor(out=ob[i][:], in0=pss[i][:], in1=bb[i][:], op=ALU.add),
                      reads=[f"m_ps{i}", f"m_b{i}"], writes=[f"m_o{i}"])
                kb.dma(g.mrow[l, :, cb * 512:(cb + 1) * 512], ob[i][:], reads=[f"m_o{i}"], writes=["mrow"])


def alloc_globals(kb, g):
    g.AB = kb.sb("AB", [128, 2 * 2 * 3 * 2 * 8], F32, stack=kb.es)
    g.lbc = kb.sb("lbc", [128, 2, 2, 4], F32, stack=kb.es)
    g.ident = kb.sb("ident", [128, 128], BF16, stack=kb.es)
    g.idf = kb.sb("idf", [128, 128], F32, stack=kb.es)


def stage_prep(kb, g):
    kb.dma(g.idf[:], g.identf[:], writes=["idf"])
    kb.op('dve', lambda e: e.tensor_copy(out=g.ident[:], in_=g.idf[:]), reads=["idf"], writes=["ident"])
    with kb.stage() as st:
        mcol = kb.sb("p_mcol", [128, 2, 3, 48], F32)
        gcol = kb.sb("p_gcol", [128, 2, 2, 8], F32)
        lg = kb.sb("p_lg", [128, 2, 2, 4], F32)
        dd = kb.sb("p_dd", [128, 2, 4], F32)
        for l in range(2):
            src = AP(g.mrow.tensor, g.mrow.offset + l * 3 * 6144, [[1, 128], [6144, 3], [128, 48]])
            kb.dma(mcol[:, l], src, reads=["mrow"], writes=["p_mcol"], allow_slow_non_contiguous=True)
        kb.dma(gcol[:], g.gcol[:], writes=["p_gcol"])
        kb.dma(lg[:], g.lbl[:], writes=["p_lg"])
        AB = g.AB[:].rearrange("p (l n c a k) -> p l n c a k", l=2, n=2, c=3, a=2)
        for l in range(2):
            for n in range(2):
                for c in range(3):
                    sh = mcol[:, l, c, (3 * n) * 8:(3 * n) * 8 + 8]
                    scl = mcol[:, l, c, (3 * n + 1) * 8:(3 * n + 1) * 8 + 8]
                    kb.op('dve', lambda e, l=l, n=n, c=c, scl=scl: e.scalar_tensor_tensor(
                        out=AB[:, l, n, c, 0, :], in0=scl, scalar=1.0, in1=gcol[:, l, n, :], op0=ALU.add, op1=ALU.mult),
                        reads=["p_mcol", "p_gcol"], writes=["AB"])
                    kb.op('dve', lambda e, l=l, n=n, c=c, sh=sh: e.tensor_copy(out=AB[:, l, n, c, 1, :], in_=sh),
                          reads=["p_mcol"], writes=["AB"])
        kb.op('dve', lambda e: e.tensor_tensor(out=dd[:], in0=lg[:, :, 0, :], in1=lg[:, :, 1, :], op=ALU.subtract),
              reads=["p_lg"], writes=["p_dd"])
        kb.op('act', lambda e: e.activation(out=g.lbc[:, 0], in_=dd[:], func=AF.Sigmoid), reads=["p_dd"], writes=["lbc"])
        kb.op('dve', lambda e: e.tensor_scalar(out=g.lbc[:, 1], in0=g.lbc[:, 0], scalar1=-1.0, scalar2=1.0,
                                               op0=ALU.mult, op1=ALU.add), reads=["lbc"], writes=["lbc"])


def load_weight_bf(kb, wbf, wsrc, K, N, tag, pieces=512):
    KC = K // 128
    stg = [kb.sb(f"{tag}_stg{i}", [128, KC, pieces], F32) for i in range(2)]
    for j, c0 in enumerate(range(0, N, pieces)):
        i = j % 2
        n = min(pieces, N - c0)
        kb.dma(stg[i][:, :, :n], wsrc[:, c0:c0 + n].rearrange("(k p) n -> p k n", p=128), writes=[f"{tag}_stg{i}"])
        kb.op('pool', lambda e, i=i, c0=c0, n=n: e.tensor_copy(out=wbf[:, :, c0:c0 + n], in_=stg[i][:, :, :n]),
              reads=[f"{tag}_stg{i}"], writes=[tag])


class NormT:
    def __init__(self, kb, g, tag, nx=2):
        self.kb, self.g, self.tag = kb, g, tag
        self.nx = nx
        self.xt = [kb.sb(f"{tag}_xt{i}", [128, D], F32) for i in range(nx)]
        self.xn = [kb.sb(f"{tag}_xn{i}", [128, D], BF16) for i in range(2)]
        self.junk = kb.sb(f"{tag}_junk", [128, D], BF16)
        self.ss = [kb.sb(f"{tag}_ss{i}", [128, 2], F32) for i in range(2)]
        self.tp = [kb.ps(f"{tag}_tp{i}", [128, 8, 128], BF16) for i in range(2)]
        self.n = 0

    def run(self, src_tile_ap, l, nrm, col, dst, dst_key, keep_x=None):
        kb, g, tag = self.kb, self.g, self.tag
        i = self.n % 2
        ix = self.n % self.nx
        self.n += 1
        xt, xn, ss, tp = self.xt[ix], self.xn[i], self.ss[i], self.tp[i]
        kx, kn, ks, kt = f"{tag}_xt{ix}", f"{tag}_xn{i}", f"{tag}_ss{i}", f"{tag}_tp{i}"
        kb.dma(xt[:], src_tile_ap, reads=["stream"], writes=[kx])
        kb.op('act', lambda e: e.activation(out=self.junk[:], in_=xt[:], func=AF.Square, accum_out=ss[:, 0:1]),
              reads=[kx], writes=[ks, f"{tag}_junk"])
        kb.op('dve', lambda e: e.tensor_scalar(out=ss[:, 1:2], in0=ss[:, 0:1], scalar1=1.0 / D, scalar2=EPS,
                                               op0=ALU.mult, op1=ALU.add), reads=[ks], writes=[ks])
        kb.op('act', lambda e: e.activation(out=ss[:, 1:2], in_=ss[:, 1:2], func=AF.Sqrt), reads=[ks], writes=[ks])
        kb.op('dve', lambda e: e.reciprocal(out=ss[:, 1:2], in_=ss[:, 1:2]), reads=[ks], writes=[ks])
        kb.op('dve', lambda e: e.tensor_scalar(out=xn[:], in0=xt[:], scalar1=ss[:, 1:2], scalar2=None, op0=ALU.mult),
              reads=[kx, ks], writes=[kn])
        for k in range(8):
            kb.op('pe', lambda e, k=k: e.transpose(out=tp[:, k, :], in_=xn[:, k * 128:(k + 1) * 128], identity=g.ident[:]),
                  reads=[kn, "ident"], writes=[kt])
        AB = g.AB[:].rearrange("p (l n c a k) -> p l n c a k", l=2, n=2, c=3, a=2)
        A = AB[:, l, nrm, col, 0, :]
        B = AB[:, l, nrm, col, 1, :]
        for k in range(8):
            kb.op('act', lambda e, k=k: e.activation(out=dst[:, k, :], in_=tp[:, k, :], func=AF.Identity,
                                                      scale=A[:, k:k + 1], bias=B[:, k:k + 1]),
                  reads=[kt, "AB"], writes=[dst_key])
        return xt, kx, ss, ks


E_FM = {}
for _i in range(4):
    E_FM[_i] = ("q", _i)
    E_FM[8 + _i] = ("ffw", 4 + _i)
    E_FM[12 + _i] = ("fbw", 8 + _i)
    E_FM[16 + _i] = ("g", 12 + _i)
    E_FM[20 + _i] = ("xr", 16 + _i)
    E_FM[24 + _i] = ("gate", 20 + _i)


def stage_e1(kb, g, l=0):
    with kb.stage() as st:
        wbf = kb.sb("e1_w", [128, 8, 3584], BF16)
        load_weight_bf(kb, wbf, g.e_w_in, D, 3584, "e1_w")
        nt = NormT(kb, g, "e1n")
        hT = [kb.sb(f"e1_hT{i}", [128, 8, 512], BF16) for i in range(2)]
        pss = [kb.ps(f"e1_ps{i}", [128, 512], F32) for i in range(4)]
        ev = [kb.sb(f"e1_ev{i}", [128, 512], F32) for i in range(4)]
        evv = [kb.sb(f"e1_evv{i}", [64, 512], BF16) for i in range(2)]
        nps = 0
        nev = 0
        nvv = 0
        def norm_block(blk):
            hi = blk % 2
            for tt in range(4):
                t = blk * 4 + tt
                nt.run(g.stream[t * 128:(t + 1) * 128, :], l, 0, mod_col(t), hT[hi][:, :, tt * 128:(tt + 1) * 128], f"e1_hT{hi}")
                bg_step(g)

        norm_block(0)
        for blk in range(NT // 512):
            hi = blk % 2
            if blk + 1 < NT // 512:
                norm_block(blk + 1)
            order = [0, 1, 2, 3, 16, 17, 18, 19, 8, 9, 10, 11, 12, 13, 14, 15, 20, 21, 22, 23, 24, 25, 26, 27]
            for cc in order:
                kind, fr = E_FM[cc]
                p = nps % 4
                nps += 1
                for k in range(8):
                    kb.op('pe', lambda e, k=k, p=p, cc=cc: e.matmul(pss[p][:], lhsT=wbf[:, k, cc * 128:(cc + 1) * 128],
                                                                   rhs=hT[hi][:, k, :], start=(k == 0), stop=(k == 7)),
                          reads=["e1_w", f"e1_hT{hi}"], writes=[f"e1_ps{p}"])
                v = nev % 4
                nev += 1
                if kind in ("q", "g"):
                    kb.op('act', lambda e, p=p, v=v: e.activation(out=ev[v][:], in_=pss[p][:], func=AF.Silu),
                          reads=[f"e1_ps{p}"], writes=[f"e1_ev{v}"])
                elif kind in ("ffw", "fbw"):
                    d = 0 if kind == "ffw" else 1
                    ch = cc % 4
                    kb.op('act', lambda e, p=p, v=v: e.activation(out=ev[v][:], in_=pss[p][:], func=AF.Sigmoid),
                          reads=[f"e1_ps{p}"], writes=[f"e1_ev{v}"])
                    kb.op('dve', lambda e, v=v, d=d, ch=ch: e.tensor_scalar(
                        out=ev[v][:], in0=ev[v][:], scalar1=g.lbc[:, 1, d, ch:ch + 1], scalar2=g.lbc[:, 0, d, ch:ch + 1],
                        op0=ALU.mult, op1=ALU.add), reads=[f"e1_ev{v}", "lbc"], writes=[f"e1_ev{v}"])
                else:
                    kb.op('dve', lambda e, p=p, v=v: e.tensor_copy(out=ev[v][:], in_=pss[p][:]),
                          reads=[f"e1_ps{p}"], writes=[f"e1_ev{v}"])
                kb.dma(g.fm[fr * 128:(fr + 1) * 128, blk * 512:(blk + 1) * 512], ev[v][:], reads=[f"e1_ev{v}"], writes=["fm"])
            for c8 in range(8):
                p = nps % 4
                nps += 1
                for k in range(8):
                    kb.op('pe', lambda e, k=k, p=p, c8=c8: e.matmul(pss[p][0:64, :], lhsT=hT[hi][:, k, c8 * 64:(c8 + 1) * 64],
                                                                   rhs=wbf[:, k, 512:1024], start=(k == 0), stop=(k == 7)),
                          reads=["e1_w", f"e1_hT{hi}"], writes=[f"e1_ps{p}"])
                v = nvv % 2
                nvv += 1
                kb.op('act', lambda e, p=p, v=v: e.activation(out=evv[v][:], in_=pss[p][0:64, :], func=AF.Copy),
                      reads=[f"e1_ps{p}"], writes=[f"e1_evv{v}"])
                r0 = blk * 512 + c8 * 64
                kb.dma(g.vtok[r0:r0 + 64, :], evv[v][:], reads=[f"e1_evv{v}"], writes=["vtok"])


NCH = TS // 64
NCC = TC // 64


def stage_e2(kb, g):
    with kb.stage() as st:
        W = TS
        q = kb.sb("h_q", [128, W], F32)
        f = [kb.sb(f"h_f{d}", [128, W], F32) for d in range(2)]
        lf = kb.sb("h_lf", [128, W], F32)
        a = kb.sb("h_a", [128, W], F32)
        E = kb.sb("h_E", [128, W], F32)
        omf = kb.sb("h_omf", [128, W], F32)
        tmp = kb.sb("h_tmp", [128, W], F32)
        rmask = kb.sb("h_rmask", [128, W], F32)
        qe = [kb.sb(f"h_qe{d}", [128, W], BF16) for d in range(2)]
        ke = [kb.sb(f"h_ke{d}", [128, W], BF16) for d in range(2)]
        kend = [kb.sb(f"h_kend{d}", [128, W], BF16) for d in range(2)]
        edec = [kb.sb(f"h_edec{d}", [128, NCH], F32) for d in range(2)]
        oT = [kb.sb(f"h_oT{d}", [128, W], F32) for d in range(2)]
        vt = kb.sb("h_vt", [64, NCH, 128], BF16)
        msk = kb.sb("h_msk", [64, 2, 64], F32)
        S32 = [kb.sb(f"h_S32{d}", [128, 128], F32) for d in range(2)]
        Sbf = [kb.sb(f"h_Sbf{d}", [128, 128], BF16) for d in range(2)]
        attsb = [[kb.sb(f"h_att{d}{i}", [64, 64], BF16) for i in range(2)] for d in range(2)]
        kendsb = [[kb.sb(f"h_ks{d}{i}", [64, 128], BF16) for i in range(2)] for d in range(2)]
        att_ps = [kb.ps(f"h_attps{i}", [128, 512], F32)[0:64, 0:64] for i in range(2)]
        kend_ps = [kb.ps(f"h_kps{i}", [128, 1024], BF16)[0:64, 0:128] for i in range(2)]
        o_ps = [kb.ps(f"h_ops{i}", [128, 512], F32)[:, 0:64] for i in range(2)]
        sup_ps = [kb.ps(f"h_sps{i}", [128, 512], F32)[:, 0:128] for i in range(2)]

        kb.dma(msk[:], g.hmask[:], writes=["h_msk"])
        kb.op('pool', lambda e: e.memset(rmask[:], 1.0), writes=["h_rmask"])
        r3 = rmask[:].rearrange("p (c t) -> p c t", t=64)
        kb.op('pool', lambda e: e.memset(r3[:, :, 0:1], 0.0), writes=["h_rmask"])

        def c3(t):
            return t[:].rearrange("p (c t) -> p c t", t=64)

        for b in range(NB):
            for h in range(4):
                c0 = b * TS
                kb.dma(q[:], g.fm[h * 128:(h + 1) * 128, c0:c0 + W], reads=["fm"], writes=["h_q"])
                for d in range(2):
                    kb.dma(f[d][:], g.fm[(4 + 4 * d + h) * 128:(5 + 4 * d + h) * 128, c0:c0 + W], reads=["fm"], writes=[f"h_f{d}"])
                vsrc = AP(g.vtok.tensor, g.vtok.offset + c0 * 512 + h * 128, [[512, 64], [64 * 512, NCH], [1, 128]])
                kb.dma(vt[:], vsrc, reads=["vtok"], writes=["h_vt"])
                for d in range(2):
                    fd = f[d]
                    kf = f"h_f{d}"
                    kb.op('act', lambda e: e.activation(out=lf[:], in_=fd[:], func=AF.Ln), reads=[kf], writes=["h_lf"])
                    kb.op('dve', lambda e: e.tensor_tensor_scan(out=a[:], data0=rmask[:], data1=lf[:], initial=0.0,
                                                                op0=ALU.mult, op1=ALU.add),
                          reads=["h_rmask", "h_lf"], writes=["h_a"])
                    a3 = c3(a)
                    if d == 1:
                        kb.op('dve', lambda e: e.tensor_tensor(out=tmp[:], in0=lf[:], in1=a[:], op=ALU.subtract),
                              reads=["h_lf", "h_a"], writes=["h_tmp"])
                        bend = a3[:, :, 63:64]
                        bend_b = bc(bend, [bend.ap[0], bend.ap[1], [0, 64]])
                        kb.op('dve', lambda e: e.tensor_tensor(out=c3(lf), in0=c3(tmp), in1=bend_b, op=ALU.add),
                              reads=["h_tmp", "h_a"], writes=["h_lf"])
                        kb.op('pool', lambda e: e.tensor_copy(out=a[:], in_=lf[:]), reads=["h_lf"], writes=["h_a"])
                    last = 63 if d == 0 else 0
                    alast = a3[:, :, last:last + 1]
                    alast_b = bc(alast, [alast.ap[0], alast.ap[1], [0, 64]])
                    kb.op('act', lambda e: e.activation(out=E[:], in_=a[:], func=AF.Exp), reads=["h_a"], writes=["h_E"])
                    kb.op('dve', lambda e: e.tensor_tensor(out=qe[d][:], in0=q[:], in1=E[:], op=ALU.mult),
                          reads=["h_q", "h_E"], writes=[f"h_qe{d}"])
                    E3 = c3(E)
                    kb.op('pool', lambda e: e.tensor_copy(out=edec[d][:], in_=E3[:, :, last]), reads=["h_E"], writes=[f"h_edec{d}"])
                    kb.op('dve', lambda e: e.tensor_scalar(out=omf[:], in0=fd[:], scalar1=-1.0, scalar2=1.0,
                                                           op0=ALU.mult, op1=ALU.add), reads=[kf], writes=["h_omf"])
                    kb.op('act', lambda e: e.activation(out=E[:], in_=a[:], func=AF.Exp, scale=-1.0), reads=["h_a"], writes=["h_E"])
                    kb.op('dve', lambda e: e.tensor_tensor(out=ke[d][:], in0=omf[:], in1=E[:], op=ALU.mult),
                          reads=["h_omf", "h_E"], writes=[f"h_ke{d}"])
                    kb.op('dve', lambda e: e.tensor_tensor(out=c3(tmp), in0=alast_b, in1=a3, op=ALU.subtract),
                          reads=["h_a"], writes=["h_tmp"])
                    kb.op('act', lambda e: e.activation(out=E[:], in_=tmp[:], func=AF.Exp), reads=["h_tmp"], writes=["h_E"])
                    kb.op('dve', lambda e: e.tensor_tensor(out=kend[d][:], in0=omf[:], in1=E[:], op=ALU.mult),
                          reads=["h_omf", "h_E"], writes=[f"h_kend{d}"])
                    if d == 0:
                        g.dump(kb, "d_a", a[:], "h_a")
                        g.dump(kb, "d_lf", lf[:], "h_lf")
                        g.dump(kb, "d_qe", qe[0][:], "h_qe0", BF16)
                        g.dump(kb, "d_ke", ke[0][:], "h_ke0", BF16)
                        g.dump(kb, "d_kend", kend[0][:], "h_kend0", BF16)
                        g.dump(kb, "d_edec", edec[0][:], "h_edec0")
                    kb.op('pool', lambda e: e.memset(S32[d][:], 0.0), writes=[f"h_S32{d}"])
                    kb.op('pool', lambda e: e.memset(Sbf[d][:], 0.0), writes=[f"h_Sbf{d}"])
                order = [list(range(NCH)), list(range(NCC - 1, -1, -1)) + list(range(NCH - 1, NCC - 1, -1))]
                for stp in range(NCH):
                    if stp % 9 == 4:
                        bg_step(g)
                    for d in range(2):
                        ci = order[d][stp]
                        i = stp % 2
                        cs = slice(ci * 64, (ci + 1) * 64)
                        kb.op('pe', lambda e: e.matmul(att_ps[i], lhsT=ke[d][:, cs], rhs=qe[d][:, cs], start=True, stop=True),
                              reads=[f"h_ke{d}", f"h_qe{d}"], writes=[f"h_attps{i}"])
                        kb.op('dve', lambda e: e.tensor_tensor(out=attsb[d][i][:], in0=att_ps[i], in1=msk[:, d, :], op=ALU.mult),
                              reads=[f"h_attps{i}", "h_msk"], writes=[f"h_att{d}{i}"])
                        if d == 0 and stp == 0:
                            g.dump(kb, "d_att", attsb[0][0][:], "h_att00", BF16)
                        kb.op('pe', lambda e: e.transpose(out=kend_ps[i], in_=kend[d][:, cs], identity=g.ident[:]),
                              reads=[f"h_kend{d}", "ident"], writes=[f"h_kps{i}"])
                        kb.op('act', lambda e: e.activation(out=kendsb[d][i][:], in_=kend_ps[i], func=AF.Copy),
                              reads=[f"h_kps{i}"], writes=[f"h_ks{d}{i}"])
                        kb.op('pe', lambda e: e.matmul(o_ps[i], lhsT=Sbf[d][:], rhs=qe[d][:, cs], start=True, stop=False),
                              reads=[f"h_Sbf{d}", f"h_qe{d}"], writes=[f"h_ops{i}"])
                        kb.op('pe', lambda e: e.matmul(o_ps[i], lhsT=vt[:, ci, :], rhs=attsb[d][i][:], start=False, stop=True),
                              reads=["h_vt", f"h_att{d}{i}"], writes=[f"h_ops{i}"])
                        kb.op('act', lambda e: e.activation(out=oT[d][:, cs], in_=o_ps[i], func=AF.Copy),
                              reads=[f"h_ops{i}"], writes=[f"h_oT{d}"])
                        kb.op('pe', lambda e: e.matmul(sup_ps[i], lhsT=kendsb[d][i][:], rhs=vt[:, ci, :], start=True, stop=True),
                              reads=[f"h_ks{d}{i}", "h_vt"], writes=[f"h_sps{i}"])
                        kb.op('dve', lambda e: e.scalar_tensor_tensor(out=S32[d][:], in0=S32[d][:], scalar=edec[d][:, ci:ci + 1],
                                                                     in1=sup_ps[i], op0=ALU.mult, op1=ALU.add),
                              reads=[f"h_S32{d}", f"h_edec{d}", f"h_sps{i}"], writes=[f"h_S32{d}"])
                        kb.op('act', lambda e: e.activation(out=Sbf[d][:], in_=S32[d][:], func=AF.Copy),
                              reads=[f"h_S32{d}"], writes=[f"h_Sbf{d}"])
                        if d == 0 and stp == 0:
                            g.dump(kb, "d_S0", S32[0][:], "h_S320")
                            g.dump(kb, "d_ks0", kendsb[0][0][:], "h_ks00", BF16)
                            g.dump(kb, "d_vt", vt[:], "h_vt", BF16)
                        if d == 0 and stp == 1:
                            g.dump(kb, "d_S1", S32[0][:], "h_S320")
                        if d == 0 and stp == 2:
                            g.dump(kb, "d_o01", oT[0][:, 0:192], "h_oT0")
                kb.op('pool', lambda e: e.tensor_tensor(out=oT[0][:], in0=oT[0][:], in1=oT[1][:], op=ALU.add),
                      reads=["h_oT0", "h_oT1"], writes=["h_oT0"])
                kb.dma(g.oT[h * 128:(h + 1) * 128, c0:c0 + W], oT[0][:], reads=["h_oT0"], writes=["oT"])


GELU_C = 1.5957691216057308
STAIR = [(0, 1, 16), (1, 1, 8), (2, 1, 5), (3, 1, 4), (4, 1, 3), (5, 3, 2), (8, 8, 1)]
NCAND = sum(ni * ln for _, ni, ln in STAIR)
VPE = True


def gelu_tanh(kb, out, x, tmp, kx, kt, ko, eng2='pool'):
    kb.op('act', lambda e: e.activation(out=out, in_=x, func=AF.Gelu_apprx_tanh), reads=[kx], writes=[ko])


def stage_e3(kb, g):
    with kb.stage() as st:
        W = TS
        wblk32 = kb.sb("r_w32", [128, 16, 128], F32)
        wblk = kb.sb("r_w", [128, 16, 128], BF16)
        sm = kb.sb("r_sm", [128, 44], F32)
        spc = kb.sb("r_spc", [128, 2, 2, 4], F32)
        kb.dma(wblk32[:], g.rg_blk[:], writes=["r_w32"])
        kb.op('pool', lambda e: e.tensor_copy(out=wblk[:], in_=wblk32[:]), reads=["r_w32"], writes=["r_w"])
        kb.dma(sm[:], g.rg_sm[:], writes=["r_sm"])
        smv = sm[:]
        cw = smv[:, 0:16].rearrange("p (j c) -> p j c", j=4)
        cb = smv[:, 16:20]
        ba = smv[:, 20:28].rearrange("p (d c) -> p d c", d=2)
        bx = smv[:, 28:36].rearrange("p (d c) -> p d c", d=2)
        lam = smv[:, 36:44].rearrange("p (d c) -> p d c", d=2)
        kb.op('act', lambda e: e.activation(out=spc[:, 0], in_=lam, func=AF.Exp, scale=-1.0), reads=["r_sm"], writes=["r_spc"])
        kb.op('act', lambda e: e.activation(out=spc[:, 0], in_=spc[:, 0], func=AF.Ln, bias=1.0), reads=["r_spc"], writes=["r_spc"])
        kb.op('dve', lambda e: e.tensor_scalar(out=spc[:, 1], in0=spc[:, 0], scalar1=-16.0, scalar2=None, op0=ALU.mult),
              reads=["r_spc"], writes=["r_spc"])
        kb.op('dve', lambda e: e.tensor_scalar(out=spc[:, 0], in0=spc[:, 0], scalar1=-8.0, scalar2=None, op0=ALU.mult),
              reads=["r_spc"], writes=["r_spc"])

        x = kb.sb("r_x", [128, W], F32)
        gt = kb.sb("r_gt", [128, W], F32)
        xc = kb.sb("r_xc", [128, W], F32)
        xcb = kb.sb("r_xcb", [128, W], BF16)
        r = kb.sb("r_r", [128, W], F32)
        ig = kb.sb("r_i", [128, W], F32)
        aa = kb.sb("r_a", [128, W], F32)
        u = kb.sb("r_u", [128, W], F32)
        hf = kb.sb("r_hf", [128, W], F32)
        hb = kb.sb("r_hb", [128, W], F32)
        tmp = kb.sb("r_tmp", [128, W], F32)
        yb = kb.sb("r_yb", [128, W], BF16)
        pss = [kb.ps(f"r_ps{i}", [128, 512], F32) for i in range(4)]
        nps = 0
        segs = [(0, TC), (TC, TS)]
        tblocks = [(i * 512, min(512, W - i * 512)) for i in range((W + 511) // 512)]
        for b in range(NB):
            for ct in range(4):
                c0 = b * TS
                kb.dma(x[:], g.fm[(16 + ct) * 128:(17 + ct) * 128, c0:c0 + W], reads=["fm"], writes=["r_x"])
                kb.dma(gt[:], g.fm[(20 + ct) * 128:(21 + ct) * 128, c0:c0 + W], reads=["fm"], writes=["r_gt"])
                kb.op('dve', lambda e: e.tensor_scalar(out=xc[:], in0=x[:], scalar1=cw[:, 2, ct:ct + 1], scalar2=cb[:, ct:ct + 1],
                                                       op0=ALU.mult, op1=ALU.add), reads=["r_x", "r_sm"], writes=["r_xc"])
                for j in (0, 1, 3):
                    sft = j - 2
                    for (s0, s1) in segs:
                        lo = max(s0, s0 - sft)
                        hi = min(s1, s1 - sft)
                        kb.op('dve', lambda e: e.scalar_tensor_tensor(out=xc[:, lo:hi], in0=x[:, lo + sft:hi + sft],
                                                                     scalar=cw[:, j, ct:ct + 1], in1=xc[:, lo:hi],
                                                                     op0=ALU.mult, op1=ALU.add),
                              reads=["r_x", "r_sm", "r_xc"], writes=["r_xc"])
                kb.op('pool', lambda e: e.tensor_copy(out=xcb[:], in_=xc[:]), reads=["r_xc"], writes=["r_xcb"])
                for d in range(2):
                    for ty, dst, bias, kd in ((0, r, ba, "r_r"), (1, ig, bx, "r_i")):
                        wi = (ct * 2 + ty) * 2 + d
                        for (t0, tn) in tblocks:
                            p = nps % 4
                            nps += 1
                            kb.op('pe', lambda e: e.matmul(pss[p][:, :tn], lhsT=wblk[:, wi, :], rhs=xcb[:, t0:t0 + tn], start=True, stop=True),
                                  reads=["r_w", "r_xcb"], writes=[f"r_ps{p}"])
                            kb.op('act', lambda e: e.activation(out=dst[:, t0:t0 + tn], in_=pss[p][:, :tn], func=AF.Sigmoid,
                                                                bias=bias[:, d, ct:ct + 1]),
                                  reads=[f"r_ps{p}", "r_sm"], writes=[kd])
                    kb.op('act', lambda e: e.activation(out=aa[:], in_=r[:], func=AF.Exp, scale=spc[:, 0, d, ct:ct + 1]),
                          reads=["r_r", "r_spc"], writes=["r_a"])
                    kb.op('act', lambda e: e.activation(out=tmp[:], in_=r[:], func=AF.Exp, scale=spc[:, 1, d, ct:ct + 1]),
                          reads=["r_r", "r_spc"], writes=["r_tmp"])
                    kb.op('dve', lambda e: e.tensor_scalar(out=tmp[:], in0=tmp[:], scalar1=-1.0, scalar2=1.0, op0=ALU.mult, op1=ALU.add),
                          reads=["r_tmp"], writes=["r_tmp"])
                    kb.op('act', lambda e: e.activation(out=tmp[:], in_=tmp[:], func=AF.Sqrt), reads=["r_tmp"], writes=["r_tmp"])
                    kb.op('pool', lambda e: e.tensor_tensor(out=u[:], in0=ig[:], in1=xc[:], op=ALU.mult), reads=["r_i", "r_xc"], writes=["r_u"])
                    kb.op('dve', lambda e: e.tensor_tensor(out=u[:], in0=u[:], in1=tmp[:], op=ALU.mult), reads=["r_u", "r_tmp"], writes=["r_u"])
                    if d == 0:
                        kb.op('dve', lambda e: e.tensor_tensor_scan(out=hf[:], data0=aa[:], data1=u[:], initial=0.0,
                                                                    op0=ALU.mult, op1=ALU.add), reads=["r_a", "r_u"], writes=["r_hf"])
                    else:
                        def rev(tl, s0, s1):
                            v = tl[:, s0:s1]
                            return bc(v, [v.ap[0], [-1, s1 - s0]], offset_add=(s1 - s0 - 1))
                        kb.op('dve', lambda e: e.tensor_tensor_scan(out=rev(hb, 0, TC), data0=rev(aa, 0, TC), data1=rev(u, 0, TC),
                                                                    initial=0.0, op0=ALU.mult, op1=ALU.add),
                              reads=["r_a", "r_u"], writes=["r_hb"])
                        kb.op('dve', lambda e: e.tensor_tensor_scan(out=rev(hb, TC, TS), data0=rev(aa, TC, TS), data1=rev(u, TC, TS),
                                                                    initial=hb[:, 0:1], op0=ALU.mult, op1=ALU.add),
                              reads=["r_a", "r_u", "r_hb"], writes=["r_hb"])
                kb.op('pool', lambda e: e.tensor_tensor(out=hf[:], in0=hf[:], in1=hb[:], op=ALU.add), reads=["r_hf", "r_hb"], writes=["r_hf"])
                gelu_tanh(kb, u[:], gt[:], tmp[:], "r_gt", "r_tmp", "r_u")
                kb.op('dve', lambda e: e.tensor_tensor(out=yb[:], in0=u[:], in1=hf[:], op=ALU.mult), reads=["r_u", "r_hf"], writes=["r_yb"])
                kb.dma(g.ybT[ct * 128:(ct + 1) * 128, c0:c0 + W], yb[:], reads=["r_yb"], writes=["ybT"])


def load_gate_rows(kb, g, l, idx, tag):
    mg = kb.sb(f"{tag}_mg", [128, 3, D], F32)
    for c in range(3):
        src = AP(g.mrow.tensor, g.mrow.offset + (l * 3 + c) * 6144 + idx * D, [[0, 128], [1, D]])
        kb.dma(mg[:, c, :], src, reads=["mrow"], writes=[f"{tag}_mg"])
    return mg


class OutProj:
    def __init__(self, kb, g, l, w_dram, gate_idx, tag):
        self.kb, self.g, self.tag = kb, g, tag
        self.w = kb.sb(f"{tag}_w", [128, 8, D], BF16)
        load_weight_bf(kb, self.w, w_dram, D, D, f"{tag}_w")
        self.mg = load_gate_rows(kb, g, l, gate_idx, tag)
        self.ps = [kb.ps(f"{tag}_ps{i}", [128, 512], F32) for i in range(2)]
        self.xt = [kb.sb(f"{tag}_xt{i}", [128, D], F32) for i in range(2)]
        self.n = 0

    def run(self, yT, ky, blk, src, dst, tiles=None):
        kb, tag = self.kb, self.tag
        for tt in range(4):
            t = blk * 4 + tt if tiles is None else tiles[tt]
            i = self.n % 2
            self.n += 1
            xt = self.xt[i]
            kb.dma(xt[:], src[t * 128:(t + 1) * 128, :], reads=["stream"], writes=[f"{tag}_xt{i}"])
            col = mod_col(t)
            for half in range(2):
                for k in range(8):
                    kb.op('pe', lambda e: e.matmul(self.ps[half][:], lhsT=yT[:, k, tt * 128:(tt + 1) * 128],
                                                   rhs=self.w[:, k, half * 512:(half + 1) * 512], start=(k == 0), stop=(k == 7)),
                          reads=[ky, f"{tag}_w"], writes=[f"{tag}_ps{half}"])
            tmpk = f"{tag}_tmp{i}"
            if not hasattr(self, "tmp"):
                self.tmp = [kb.sb(f"{tag}_tmp{j}", [128, D], F32) for j in range(2)]
            tmp = self.tmp[i]
            for half in range(2):
                hs = slice(half * 512, (half + 1) * 512)
                kb.op('dve', lambda e: e.tensor_tensor(out=tmp[:, hs], in0=self.ps[half][:], in1=self.mg[:, col, hs], op=ALU.mult),
                      reads=[f"{tag}_ps{half}", f"{tag}_mg"], writes=[tmpk])
            kb.op('pool', lambda e: e.tensor_tensor(out=tmp[:], in0=tmp[:], in1=xt[:], op=ALU.add),
                  reads=[tmpk, f"{tag}_xt{i}"], writes=[tmpk])
            kb.dma(dst[t * 128:(t + 1) * 128, :], tmp[:], reads=[tmpk], writes=["stream_out"])


def stage_e4(kb, g, l, src, dst):
    with kb.stage() as st:
        op = OutProj(kb, g, l, g.e_w_out, 2, "e4")
        ones = kb.sb("e4_ones", [128, 128], F32)
        kb.op('pool', lambda e: e.memset(ones[:], 1.0 / 128.0), writes=["e4_ones"])
        gain = kb.sb("e4_gain", [128, 1], F32)
        kb.dma(gain[:], g.hg_gain[:], writes=["e4_gain"])
        o_ = [kb.sb(f"e4_o{i}", [128, 4, 512], F32) for i in range(2)]
        gg_ = [kb.sb(f"e4_g{i}", [128, 4, 512], F32) for i in range(2)]
        sq_ = [kb.sb(f"e4_sq{i}", [128, 4, 512], F32) for i in range(2)]
        rs_ = [kb.sb(f"e4_rs{i}", [128, 512], F32) for i in range(2)]
        yT = [kb.sb(f"e4_yT{i}", [128, 8, 512], BF16) for i in range(2)]
        sps = [kb.ps(f"e4_sps{i}", [128, 512], F32) for i in range(2)]

        def merge(blk):
            bi = blk % 2
            o, gg, sq, rs = o_[bi], gg_[bi], sq_[bi], rs_[bi]
            KO, KG, KSQ, KRS = f"e4_o{bi}", f"e4_g{bi}", f"e4_sq{bi}", f"e4_rs{bi}"
            cs = slice(blk * 512, (blk + 1) * 512)
            kb.dma(o[:], g.oT[:, cs].rearrange("(h p) t -> p h t", p=128), reads=["oT"], writes=[KO])
            kb.dma(gg[:], g.fm[12 * 128:16 * 128, cs].rearrange("(h p) t -> p h t", p=128), reads=["fm"], writes=[KG])
            kb.dma(yT[bi][:, 4:8, :], g.ybT[:, cs].rearrange("(h p) t -> p h t", p=128), reads=["ybT"], writes=[f"e4_yT{bi}"])
            kb.op('act', lambda e: e.activation(out=sq[:], in_=o[:], func=AF.Square), reads=[KO], writes=[KSQ])
            for h in range(4):
                p = h % 2
                kb.op('pe', lambda e: e.matmul(sps[p][:], lhsT=ones[:], rhs=sq[:, h, :], start=True, stop=True),
                      reads=["e4_ones", KSQ], writes=[f"e4_sps{p}"])
                kb.op('dve', lambda e: e.tensor_scalar(out=rs[:], in0=sps[p][:], scalar1=EPS, scalar2=None, op0=ALU.add),
                      reads=[f"e4_sps{p}"], writes=[KRS])
                kb.op('act', lambda e: e.activation(out=rs[:], in_=rs[:], func=AF.Sqrt), reads=[KRS], writes=[KRS])
                kb.op('dve', lambda e: e.reciprocal(out=rs[:], in_=rs[:]), reads=[KRS], writes=[KRS])
                kb.op('dve', lambda e: e.tensor_tensor(out=rs[:], in0=rs[:], in1=o[:, h, :], op=ALU.mult),
                      reads=[KRS, KO], writes=[KRS])
                kb.op('dve', lambda e: e.scalar_tensor_tensor(out=yT[bi][:, h, :], in0=rs[:], scalar=gain[:, 0:1], in1=gg[:, h, :],
                                                             op0=ALU.mult, op1=ALU.mult),
                      reads=[KRS, "e4_gain", KG], writes=[f"e4_yT{bi}"])

        NBLK = NT // 512
        merge(0)
        for blk in range(NBLK):
            if blk + 1 < NBLK:
                merge(blk + 1)
            op.run(yT[blk % 2], f"e4_yT{blk % 2}", blk, src, dst)


class UVConv:
    R = 2

    def __init__(self, kb, g, l):
        self.kb, self.g, self.l = kb, g, l
        R = self.R
        self.ld = [[kb.sb(f"uv_ld{j}{i}", [128, R, D], F32) for i in range(2)] for j in range(2)]
        self.ob = [kb.sb(f"uv_ob{i}", [128, R, 2 * D], BF16) for i in range(2)]
        self.tabs = (g.p_u0, g.p_v0) if l == 0 else (g.p_u1, g.p_v1)
        self.dst = g.uv0 if l == 0 else g.uv1
        self.nit = 128 // R
        self.loaded = 0
        self.done = 0
        self._load()

    def _load(self):
        if self.loaded >= self.nit:
            return
        kb, R = self.kb, self.R
        i = self.loaded % 2
        r0 = self.loaded * R
        for j in range(2):
            srcp = AP(self.tabs[j].tensor, self.tabs[j].offset + r0 * D, [[128 * D, 128], [D, R], [1, D]])
            kb.dma(self.ld[j][i][:], srcp, writes=[f"uv_ld{j}{i}"])
        self.loaded += 1

    def step(self):
        if self.done >= self.nit:
            return
        kb, R = self.kb, self.R
        self._load()
        i = self.done % 2
        r0 = self.done * R
        for j in range(2):
            kb.op('pool', lambda e: e.tensor_copy(out=self.ob[i][:, :, j * D:(j + 1) * D], in_=self.ld[j][i][:]),
                  reads=[f"uv_ld{j}{i}"], writes=[f"uv_ob{i}"])
        dstp = AP(self.dst.tensor, self.dst.offset + r0 * 2 * D, [[128 * 2 * D, 128], [2 * D, R], [1, 2 * D]])
        kb.dma(dstp, self.ob[i][:], reads=[f"uv_ob{i}"], writes=["uv"])
        self.done += 1

    def finish(self):
        while self.done < self.nit:
            self.step()


def bg_step(g, n=1):
    bg = getattr(g, "bg", None)
    if bg is not None:
        for _ in range(n):
            bg.step()


def stage_peer(kb, g, l, src, dst, tiles, dst_row):
    GS = 8
    NGRP = 128 // GS
    with kb.stage() as st:
        wq = kb.sb("p_wq", [128, 8, 2048], BF16)
        with kb.stage() as st2:
            load_weight_bf(kb, wq, g.p_wq[l], D, 2048, "p_wq", pieces=256)
        uvt = g.uv0 if l == 0 else g.uv1
        keysT = kb.sb("p_keys", [128, 16, 128], F32)
        kb.dma(keysT[:], g.keysT[l], writes=["p_keys"])
        bq = kb.sb("p_bq", [128, 16], F32)
        kb.dma(bq[:], g.bqc[l], writes=["p_bq"])
        nt = NormT(kb, g, "pn", nx=2)
        hT = [kb.sb(f"p_hT{i}", [128, 8, 128], BF16) for i in range(2)]
        qT = kb.sb("p_qT", [128, 16, 128], F32)
        sc2 = kb.sb("p_sc2", [128, 128], F32)
        sv = kb.sb("p_sv", [128, 16, 16], F32)
        si = kb.sb("p_si", [128, 16, 16], U32)
        sif = kb.sb("p_sif", [128, 16, 16], F32)
        cand = kb.sb("p_cand", [128, 8, NCAND], F32)
        cand2 = kb.sb("p_cand2", [128, NCAND], F32)
        cidx = kb.sb("p_cidx", [128, 8, NCAND], F32)
        junk = kb.sb("p_junk", [128, NCAND], F32)
        best = kb.sb("p_best", [128, 8, 16], F32)
        eif = kb.sb("p_eif", [128, 128], F32)
        eidx = [kb.sb(f"p_eidx{i}", [128, 128], I32) for i in range(2)]
        gate = [kb.sb(f"p_gate{i}", [128, 8, 16], F32) for i in range(2)]
        gsum = kb.sb("p_gsum", [128, 8], F32)
        dots = [kb.sb(f"p_dots{i}", [128, 128], F32) for i in range(2)]
        actv = kb.sb("p_act", [128, 128], F32)
        gtmp = kb.sb("p_gtmp", [128, 128], F32)
        abrow = kb.sb("p_abrow", [128, 2, D], F32)
        m5 = [kb.sb(f"p_m5{i}", [128, D], F32) for i in range(2)]
        grow = kb.sb("p_grow", [128, D], F32)
        xtok = [kb.sb(f"p_xtok{i}", [128, D], F32) for i in range(2)]
        bigj = kb.sb("p_bigj", [128, D], BF16)
        acc = kb.sb("p_acc", [128, D], F32)
        NG = 22
        gb = [kb.sb(f"p_gb{i}", [128, 2 * D], BF16) for i in range(NG)]
        NDG = 8
        dg = [kb.sb(f"p_dg{i}", [128, 128], BF16) for i in range(NDG)]
        qps = kb.ps("p_qps", [128, 512], F32)
        scps = [kb.ps(f"p_scps{i}", [128, 512], F32) for i in range(2)]
        vps = [kb.ps(f"p_vps{i}", [128, 512], F32) for i in range(2)]
        src_g = AP(g.gffn.tensor, g.gffn.offset + l * D, [[0, 128], [1, D]])
        kb.dma(grow[:], src_g, writes=["p_grow"])
        state = {"col": None, "epoch": -1, "ngb": 0, "ndg": 0}
        info = {}
        slot_buf = {}

        def routing(n):
            t = tiles[n]
            pi = n % 2
            col = mod_col(t)
            if col != state["col"]:
                state["col"] = col
                state["epoch"] += 1
                ep = state["epoch"] % 2
                for j, idx in enumerate((4, 3)):
                    srcm = AP(g.mrow.tensor, g.mrow.offset + (l * 3 + col) * 6144 + idx * D, [[0, 128], [1, D]])
                    kb.dma(abrow[:, j, :], srcm, reads=["mrow"], writes=["p_abrow"])
                srcm = AP(g.mrow.tensor, g.mrow.offset + (l * 3 + col) * 6144 + 5 * D, [[0, 128], [1, D]])
                kb.dma(m5[ep][:], srcm, reads=["mrow"], writes=[f"p_m5{ep}"])
                kb.op('dve', lambda e: e.scalar_tensor_tensor(out=abrow[:, 0, :], in0=abrow[:, 0, :], scalar=1.0, in1=grow[:],
                                                             op0=ALU.add, op1=ALU.mult), reads=["p_abrow", "p_grow"], writes=["p_abrow"])
            ep = state["epoch"] % 2
            xt, kx, ss, ks = nt.run(src[t * 128:(t + 1) * 128, :], l, 1, col, hT[pi][:], f"p_hT{pi}")
            info[n] = (xt, kx, ep)
            kxt = f"p_xtok{pi}"
            kb.op('dve', lambda e: e.scalar_tensor_tensor(out=xtok[pi][:], in0=xt[:], scalar=ss[:, 1:2], in1=abrow[:, 0, :],
                                                         op0=ALU.mult, op1=ALU.mult), reads=[kx, ks, "p_abrow"], writes=[kxt])
            kb.op('dve', lambda e: e.tensor_tensor(out=xtok[pi][:], in0=xtok[pi][:], in1=abrow[:, 1, :], op=ALU.add),
                  reads=[kxt, "p_abrow"], writes=[kxt])
            yield
            for hp4 in range(4):
                for j in range(4):
                    hp = hp4 * 4 + j
                    for k in range(8):
                        kb.op('pe', lambda e: e.matmul(qps[:, j * 128:(j + 1) * 128], lhsT=wq[:, k, hp * 128:(hp + 1) * 128],
                                                       rhs=hT[pi][:, k, :], start=(k == 0), stop=(k == 7)),
                              reads=["p_wq", f"p_hT{pi}"], writes=["p_qps"])
                for j in range(4):
                    hp = hp4 * 4 + j
                    kb.op('act', lambda e: e.activation(out=qT[:, hp, :], in_=qps[:, j * 128:(j + 1) * 128], func=AF.Identity,
                                                        bias=bq[:, hp:hp + 1]), reads=["p_qps", "p_bq"], writes=["p_qT"])
                yield
            for hp4 in range(4):
                sp = hp4 % 2
                for j in range(4):
                    hp = hp4 * 4 + j
                    kb.op('pe', lambda e: e.matmul(scps[sp][:, j * 128:(j + 1) * 128], lhsT=qT[:, hp, :], rhs=keysT[:, hp, :],
                                                   start=True, stop=True), reads=["p_qT", "p_keys"], writes=[f"p_scps{sp}"])
                for j in range(4):
                    hp = hp4 * 4 + j
                    scv = scps[sp][:, j * 128:(j + 1) * 128]
                    ksp = f"p_scps{sp}"
                    kb.op('dve', lambda e: e.max(out=sv[:, hp, 0:8], in_=scv), reads=[ksp], writes=["p_sv"])
                    kb.op('dve', lambda e: e.max_index(out=si[:, hp, 0:8], in_max=sv[:, hp, 0:8], in_values=scv),
                          reads=[ksp, "p_sv"], writes=["p_si"])
                    kb.op('dve', lambda e: e.match_replace(out=sc2[:], in_to_replace=sv[:, hp, 0:8], in_values=scv, imm_value=-1e30),
                          reads=[ksp, "p_sv"], writes=["p_sc2"])
                    kb.op('dve', lambda e: e.max(out=sv[:, hp, 8:16], in_=sc2[:]), reads=["p_sc2"], writes=["p_sv"])
                    kb.op('dve', lambda e: e.max_index(out=si[:, hp, 8:16], in_max=sv[:, hp, 8:16], in_values=sc2[:]),
                          reads=["p_sc2", "p_sv"], writes=["p_si"])
                    if j % 2 == 1:
                        yield
            kb.op('dve', lambda e: e.tensor_copy(out=sif[:], in_=si[:]), reads=["p_si"], writes=["p_sif"])
            sfa = sif[:].rearrange("p (h two) k -> p h two k", two=2)[:, :, 0, :]
            kb.op('dve', lambda e: e.tensor_scalar(out=sfa, in0=sfa, scalar1=128.0, scalar2=None, op0=ALU.mult), reads=["p_sif"], writes=["p_sif"])
            yield
            sv4 = sv[:].rearrange("p (h two) k -> p h two k", two=2)
            sf4 = sif[:].rearrange("p (h two) k -> p h two k", two=2)
            off = 0
            for (i0, ni, ln) in STAIR:
                cseg = cand[:, :, off:off + ni * ln].rearrange("p h (i j) -> p h i j", i=ni)
                xseg = cidx[:, :, off:off + ni * ln].rearrange("p h (i j) -> p h i j", i=ni)
                a0 = sv4[:, :, 0, i0:i0 + ni]
                a1 = sv4[:, :, 1, 0:ln]
                f0 = sf4[:, :, 0, i0:i0 + ni]
                f1 = sf4[:, :, 1, 0:ln]
                a0b = bc(a0, [a0.ap[0], a0.ap[1], a0.ap[2], [0, ln]])
                a1b = bc(a1, [a1.ap[0], a1.ap[1], [0, ni], a1.ap[2]])
                f0b = bc(f0, [f0.ap[0], f0.ap[1], f0.ap[2], [0, ln]])
                f1b = bc(f1, [f1.ap[0], f1.ap[1], [0, ni], f1.ap[2]])
                kb.op('dve', lambda e: e.tensor_tensor(out=cseg, in0=a0b, in1=a1b, op=ALU.add), reads=["p_sv"], writes=["p_cand"])
                kb.op('dve', lambda e: e.tensor_tensor(out=xseg, in0=f0b, in1=f1b, op=ALU.add), reads=["p_sif"], writes=["p_cidx"])
                off += ni * ln
            yield
            for h in range(8):
                kb.op('dve', lambda e: e.max(out=best[:, h, 0:8], in_=cand[:, h, :]), reads=["p_cand"], writes=["p_best"])
                kb.op('dve', lambda e: e.match_replace(out=cand2[:], in_to_replace=best[:, h, 0:8], in_values=cand[:, h, :], imm_value=-1e30),
                      reads=["p_cand", "p_best"], writes=["p_cand2"])
                kb.op('dve', lambda e: e.max(out=best[:, h, 8:16], in_=cand2[:]), reads=["p_cand2"], writes=["p_best"])
                for k in range(16):
                    kb.op('dve', lambda e: e.scalar_tensor_tensor(out=junk[:], in0=cand[:, h, :], scalar=best[:, h, k:k + 1], in1=cidx[:, h, :],
                                                                 op0=ALU.is_equal, op1=ALU.mult, accum_out=eif[:, h * 16 + k:h * 16 + k + 1]),
                          reads=["p_cand", "p_best", "p_cidx"] + (["p_eif"] if (h == 0 and k == 0) else []),
                          writes=(["p_eif"] if (k == 15 or (h == 0 and k == 0)) else []))
                yield
            kb.op('dve', lambda e: e.tensor_scalar(out=eif[:], in0=eif[:], scalar1=16383.0, scalar2=0.0, op0=ALU.min, op1=ALU.max),
                  reads=["p_eif"], writes=["p_eif"])
            kb.op('dve', lambda e: e.tensor_copy(out=eidx[pi][:], in_=eif[:]), reads=["p_eif"], writes=[f"p_eidx{pi}"])
            gt_ = gate[pi]
            kg = f"p_gate{pi}"
            bm = best[:, :, 0:1]
            kb.op('dve', lambda e: e.tensor_tensor(out=gt_[:], in0=best[:], in1=bc(bm, [bm.ap[0], bm.ap[1], [0, 16]]), op=ALU.subtract),
                  reads=["p_best"], writes=[kg])
            kb.op('act', lambda e: e.activation(out=gt_[:], in_=gt_[:], func=AF.Exp), reads=[kg], writes=[kg])
            kb.op('dve', lambda e: e.tensor_reduce(out=gsum[:], in_=gt_[:], axis=AX.X, op=ALU.add), reads=[kg], writes=["p_gsum"])
            kb.op('dve', lambda e: e.reciprocal(out=gsum[:], in_=gsum[:]), reads=["p_gsum"], writes=["p_gsum"])
            gs = gsum[:]
            kb.op('dve', lambda e: e.tensor_tensor(out=gt_[:], in0=gt_[:], in1=bc(gs, [gs.ap[0], gs.ap[1], [0, 16]]), op=ALU.mult),
                  reads=[kg, "p_gsum"], writes=[kg])
            yield

        NYIELD = 1 + 4 + 8 + 1 + 1 + 8 + 1

        def ggroup(n, grp):
            pi = n % 2
            for j in range(GS):
                k = grp * GS + j
                bi = state["ngb"] % NG
                state["ngb"] += 1
                slot_buf[(n, k)] = bi
                kb.dma(None, None, reads=[f"p_eidx{pi}"], writes=[f"p_gb{bi}"], q='pool',
                       fn=lambda e: e.indirect_dma_start(out=gb[bi][:], out_offset=None, in_=uvt,
                                                         in_offset=bass.IndirectOffsetOnAxis(ap=eidx[pi][:, k:k + 1], axis=0)))

        def dgroup(n, grp):
            pi = n % 2
            kd = f"p_dots{pi}_{grp}"
            for j in range(GS):
                k = grp * GS + j
                bi = slot_buf[(n, k)]
                kb.op('dve', lambda e: e.scalar_tensor_tensor(out=bigj[:], in0=gb[bi][:, 0:D], scalar=1.0, in1=xtok[pi][:], op0=ALU.mult, op1=ALU.mult,
                                                             accum_out=dots[pi][:, k:k + 1]),
                      reads=[f"p_gb{bi}", f"p_xtok{pi}"] + ([kd] if j == 0 else []), writes=([kd] if j in (0, GS - 1) else []))

        def vgroup(n, grp):
            pi = n % 2
            kd = f"p_dots{pi}_{grp}"
            cs = slice(grp * GS, (grp + 1) * GS)
            ka = f"p_act_{grp % 2}"
            kt = f"p_gtmp_{grp % 2}"
            gelu_tanh(kb, actv[:, cs], dots[pi][:, cs], gtmp[:, cs], kd, kt, ka, eng2='dve')
            gfl = gate[pi][:].rearrange("p h k -> p (h k)")
            kb.op('dve', lambda e: e.tensor_tensor(out=actv[:, cs], in0=actv[:, cs], in1=gfl[:, cs], op=ALU.mult),
                  reads=[ka, f"p_gate{pi}"], writes=[ka])
            for j in range(GS):
                k = grp * GS + j
                bi = slot_buf.pop((n, k))
                di = state["ndg"] % NDG
                state["ndg"] += 1
                kb.op('act', lambda e: e.activation(out=dg[di][:], in_=g.ident[:], func=AF.Identity, scale=actv[:, k:k + 1]),
                      reads=["ident", ka], writes=[f"p_dg{di}"])
                for half in range(2):
                    kb.op('pe', lambda e: e.matmul(vps[half][:], lhsT=dg[di][:], rhs=gb[bi][:, D + half * 512:D + (half + 1) * 512],
                                                   start=(k == 0), stop=(k == 127)),
                          reads=[f"p_dg{di}", f"p_gb{bi}"], writes=[f"p_vps{half}"])

        def vfin(n):
            t = tiles[n]
            xt, kx, ep = info.pop(n)
            for half in range(2):
                hs = slice(half * 512, (half + 1) * 512)
                kb.op('dve', lambda e: e.tensor_tensor(out=acc[:, hs], in0=vps[half][:], in1=m5[ep][:, hs], op=ALU.mult),
                      reads=[f"p_vps{half}", f"p_m5{ep}"], writes=["p_acc"])
            kb.op('dve', lambda e: e.tensor_tensor(out=acc[:], in0=acc[:], in1=xt[:], op=ALU.add), reads=["p_acc", kx], writes=["p_acc"])
            r0 = dst_row(t)
            g.out_toks.append(kb.dma(dst[r0:r0 + 128, :], acc[:], reads=["p_acc"], writes=["stream_out"]))

        NTL = len(tiles)
        for _ in routing(0):
            pass
        ggroup(0, 0)
        dgroup(0, 0)
        per = -(-NYIELD // (NGRP - 2))
        for n in range(NTL):
            rgen = routing(n + 1) if n + 1 < NTL else None
            for grp in range(NGRP):
                nxt = None
                if grp + 1 < NGRP:
                    nxt = (n, grp + 1)
                else:
                    if rgen is not None:
                        for _ in rgen:
                            pass
                        rgen = None
                    if n + 1 < NTL:
                        nxt = (n + 1, 0)
                if nxt is not None:
                    ggroup(*nxt)
                vgroup(n, grp)
                if nxt is not None:
                    dgroup(*nxt)
                if rgen is not None:
                    for _ in range(per):
                        if next(rgen, "done") == "done":
                            rgen = None
                            break
            vfin(n)


LT = TL // 128
ST_ = TS // 128


def stage_o1(kb, g, l, src):
    with kb.stage() as st:
        wbf = kb.sb("o1_w", [128, 8, 1536], BF16)
        load_weight_bf(kb, wbf, g.o_w_in, D, 1536, "o1_w")
        nt = NormT(kb, g, "o1n")
        hT = [kb.sb(f"o1_hT{i}", [128, 8, 128], BF16) for i in range(2)]
        gq = kb.sb("o1_gq", [128, 16, 64], F32)
        for hd in range(16):
            srcg = g.q_gain if hd < 12 else g.k_gain
            kb.dma(gq[:, hd, :], bc(srcg, [[0, 128], [1, 64]]), writes=["o1_gq"])
        cs_t = [kb.sb(f"o1_cs{i}", [128, 2, 32], F32) for i in range(2)]
        ps_sh = kb.ps("o1_pssh", [128, 512], F32)
        pss2 = [[kb.ps(f"o1_ps{i}_{j}", [128, 512], F32) for j in range(2)] + [ps_sh] for i in range(2)]
        tps1 = kb.ps("o1_tp", [128, 1024], BF16)
        tps = [tps1, tps1]
        qk_ = [kb.sb(f"o1_qk{i}", [128, 16, 64], F32) for i in range(2)]
        sq_ = [kb.sb(f"o1_sq{i}", [128, 16, 64], F32) for i in range(2)]
        ss_ = [kb.sb(f"o1_ss{i}", [128, 16], F32) for i in range(2)]
        t1_ = [kb.sb(f"o1_t1{i}", [128, 16, 32], F32) for i in range(2)]
        t2_ = [kb.sb(f"o1_t2{i}", [128, 16, 32], F32) for i in range(2)]
        qr_ = [kb.sb(f"o1_qr{i}", [128, 16, 64], BF16) for i in range(2)]
        qrT = [kb.sb(f"o1_qrT{i}", [64, 16, 128], BF16) for i in range(2)]
        vz = [kb.sb(f"o1_vz{i}", [128, 512], BF16) for i in range(2)]
        def phase_a(t):
            i = t % 2
            pss = pss2[i]
            PK = f"o1_ps{i}_"
            PKN = [PK + "0", PK + "1", "o1_pssh"]
            qk = qk_[i]
            KQ = f"o1_qk{i}"
            nt.run(src[t * 128:(t + 1) * 128, :], l, 0, mod_col(t), hT[i][:], f"o1_hT{i}")
            bg_step(g)
            kb.dma(cs_t[i][:], g.rope[(t % ST_) * 128:(t % ST_ + 1) * 128], writes=[f"o1_cs{i}"])
            for bnk in range(3):
                for k in range(8):
                    kb.op('pe', lambda e: e.matmul(pss[bnk][:], lhsT=hT[i][:, k, :], rhs=wbf[:, k, bnk * 512:(bnk + 1) * 512],
                                                   start=(k == 0), stop=(k == 7)), reads=[f"o1_hT{i}", "o1_w"], writes=[PKN[bnk]])
            qkf = qk[:].rearrange("p h d -> p (h d)")
            kb.op('act', lambda e: e.activation(out=qkf[:, 0:512], in_=pss[0][:], func=AF.Copy), reads=[PK + "0"], writes=[KQ])
            kb.op('act', lambda e: e.activation(out=qkf[:, 512:1024], in_=pss[1][:], func=AF.Copy), reads=[PK + "1"], writes=[KQ])
            kb.op('act', lambda e: e.activation(out=vz[i][:], in_=pss[2][:], func=AF.Copy), reads=["o1_pssh"], writes=[f"o1_vz{i}"])
            kb.dma(g.vtok2[t * 128:(t + 1) * 128, :], vz[i][:, 0:256], reads=[f"o1_vz{i}"], writes=["vtok2"])
            kb.dma(g.ztok[t * 128:(t + 1) * 128, :], vz[i][:, 256:512], reads=[f"o1_vz{i}"], writes=["ztok"])

        def phase_b(t):
            i = t % 2
            qk, sq, ss, t1, t2, qr = qk_[i], sq_[i], ss_[i], t1_[i], t2_[i], qr_[i]
            KQ, KS, KSS, KT1, KT2, KQR = f"o1_qk{i}", f"o1_sq{i}", f"o1_ss{i}", f"o1_t1{i}", f"o1_t2{i}", f"o1_qr{i}"
            kb.op('dve', lambda e: e.tensor_tensor(out=sq[:], in0=qk[:], in1=qk[:], op=ALU.mult), reads=[KQ], writes=[KS])
            kb.op('dve', lambda e: e.tensor_reduce(out=ss[:], in_=sq[:], axis=AX.X, op=ALU.add), reads=[KS], writes=[KSS])
            kb.op('dve', lambda e: e.tensor_scalar(out=ss[:], in0=ss[:], scalar1=1.0 / 64.0, scalar2=EPS, op0=ALU.mult, op1=ALU.add),
                  reads=[KSS], writes=[KSS])
            kb.op('act', lambda e: e.activation(out=ss[:], in_=ss[:], func=AF.Sqrt), reads=[KSS], writes=[KSS])
            kb.op('dve', lambda e: e.reciprocal(out=ss[:], in_=ss[:]), reads=[KSS], writes=[KSS])
            ssv = ss[:]
            kb.op('dve', lambda e: e.tensor_tensor(out=qk[:], in0=qk[:], in1=bc(ssv, [ssv.ap[0], ssv.ap[1], [0, 64]]), op=ALU.mult),
                  reads=[KQ, KSS], writes=[KQ])
            kb.op('dve', lambda e: e.tensor_tensor(out=qk[:], in0=qk[:], in1=gq[:], op=ALU.mult), reads=[KQ, "o1_gq"], writes=[KQ])
            cosv = cs_t[i][:, 0, :]
            sinv = cs_t[i][:, 1, :]
            cb_ = bc(cosv, [cosv.ap[0], [0, 16], [1, 32]])
            sb_ = bc(sinv, [sinv.ap[0], [0, 16], [1, 32]])
            x1 = qk[:, :, 0:32]
            x2 = qk[:, :, 32:64]
            kc = f"o1_cs{i}"
            kb.op('dve', lambda e: e.tensor_tensor(out=t1[:], in0=x1, in1=cb_, op=ALU.mult), reads=[KQ, kc], writes=[KT1])
            kb.op('dve', lambda e: e.tensor_tensor(out=t2[:], in0=x2, in1=sb_, op=ALU.mult), reads=[KQ, kc], writes=[KT2])
            kb.op('dve', lambda e: e.tensor_tensor(out=qr[:, :, 0:32], in0=t1[:], in1=t2[:], op=ALU.subtract),
                  reads=[KT1, KT2], writes=[KQR])
            kb.op('dve', lambda e: e.tensor_tensor(out=t1[:], in0=x2, in1=cb_, op=ALU.mult), reads=[KQ, kc, KQR], writes=[KT1])
            kb.op('dve', lambda e: e.tensor_tensor(out=t2[:], in0=x1, in1=sb_, op=ALU.mult), reads=[KQ, kc, KQR], writes=[KT2])
            kb.op('dve', lambda e: e.tensor_tensor(out=qr[:, :, 32:64], in0=t1[:], in1=t2[:], op=ALU.add),
                  reads=[KT1, KT2], writes=[KQR])
            for bnk in range(2):
                for hd in range(bnk * 8, bnk * 8 + 8):
                    kb.op('pe', lambda e: e.transpose(out=tps1[0:64, (hd % 8) * 128:(hd % 8 + 1) * 128], in_=qr[:, hd, :], identity=g.ident[:]),
                          reads=[KQR, "ident"], writes=["o1_tp"])
                tv = tps1[0:64, :].rearrange("p (h t) -> p h t", h=8)
                if bnk == 0:
                    kb.op('act', lambda e: e.activation(out=qrT[i][:, 0:8, :], in_=tv, func=AF.Copy), reads=["o1_tp"], writes=[f"o1_qrT{i}"])
                else:
                    kb.op('dve', lambda e: e.tensor_copy(out=qrT[i][:, 8:16, :], in_=tv), reads=["o1_tp"], writes=[f"o1_qrT{i}"])
            dstT = AP(g.qkT.tensor, g.qkT.offset + t * 128, [[NT, 64], [64 * NT, 16], [1, 128]])
            kb.dma(dstT, qrT[i][:], reads=[f"o1_qrT{i}"], writes=["qkT"])

        phase_a(0)
        for t in range(NTILE):
            if t + 1 < NTILE:
                phase_a(t + 1)
            phase_b(t)


def stage_o2(kb, g):
    with kb.stage() as st:
        kT = kb.sb("a_kT", [64, TS], BF16)
        qT = kb.sb("a_qT", [64, 3, TL], BF16)
        V = kb.sb("a_V", [128, ST_, 64], BF16)
        aT = kb.sb("a_aT", [64, 3, TL], BF16)
        ones = kb.sb("a_ones", [128, 64], BF16)
        kb.op('pool', lambda e: e.memset(ones[:], 1.0), writes=["a_ones"])
        am32 = kb.sb("a_m32", [128, 2, 128], F32)
        am = kb.sb("a_m", [128, 2, 128], BF16)
        kb.dma(am32[:], g.amask[:], writes=["a_m32"])
        kb.op('dve', lambda e: e.tensor_copy(out=am[:], in_=am32[:]), reads=["a_m32"], writes=["a_m"])
        es = kb.sb("a_es", [64, 12], F32)
        kb.dma(es[:], bc(g.sinks, [[0, 64], [1, 12]]), writes=["a_es"])
        kb.op('act', lambda e: e.activation(out=es[:], in_=es[:], func=AF.Exp), reads=["a_es"], writes=["a_es"])
        P = [kb.sb(f"a_P{i}", [128, 3, 128], BF16) for i in range(4)]
        den = kb.sb("a_den", [64, 3, 128], F32)
        s_ps = [kb.ps(f"a_sps{i}", [128, 512], F32) for i in range(3)]
        o_ps = [kb.ps(f"a_ops{i}", [128, 512], F32) for i in range(2)]
        d_ps = [kb.ps(f"a_dps{i}", [128, 512], F32) for i in range(2)]
        nS = 0
        nP = 0
        nG = 0
        for b in range(NB):
            c0 = b * TS
            for kvh in range(4):
                kb.dma(kT[:], g.qkT[12 + kvh, :, c0:c0 + TS], reads=["qkT"], writes=["a_kT"])
                for gi in range(3):
                    kb.dma(qT[:, gi, :], g.qkT[kvh * 3 + gi, :, c0 + TC:c0 + TS], reads=["qkT"], writes=["a_qT"])
                vsrc = AP(g.vtok2.tensor, g.vtok2.offset + c0 * 256 + kvh * 64, [[256, 128], [128 * 256, ST_], [1, 64]])
                kb.dma(V[:], vsrc, reads=["vtok2"], writes=["a_V"])
                for n in range(LT):
                    if n % 4 == 1:
                        bg_step(g)
                    og = nG % 2
                    nG += 1
                    kts = [(0, None), (1, None)]
                    for m in (n - 1, n, n + 1):
                        if 0 <= m < LT:
                            kts.append((2 + m, 0 if m == n - 1 else (1 if m == n + 1 else None)))
                    for ki, (kt, mk) in enumerate(kts):
                        sp = nS % 3
                        nS += 1
                        pp = nP % 4
                        nP += 1
                        kb.op('pe', lambda e: e.matmul(s_ps[sp][:, 0:384], lhsT=kT[:, kt * 128:(kt + 1) * 128],
                                                       rhs=qT[:, :, n * 128:(n + 1) * 128], start=True, stop=True),
                              reads=["a_kT", "a_qT"], writes=[f"a_sps{sp}"])
                        kb.op('act', lambda e: e.activation(out=P[pp][:].rearrange("p g t -> p (g t)"), in_=s_ps[sp][:, 0:384],
                                                            func=AF.Exp, scale=0.125), reads=[f"a_sps{sp}"], writes=[f"a_P{pp}"])
                        if mk is not None:
                            mv = am[:, mk, :]
                            kb.op('dve', lambda e: e.tensor_tensor(out=P[pp][:], in0=P[pp][:], in1=bc(mv, [mv.ap[0], [0, 3], [1, 128]]),
                                                                    op=ALU.mult), reads=[f"a_P{pp}", "a_m"], writes=[f"a_P{pp}"])
                        first = ki == 0
                        lastk = ki == len(kts) - 1
                        Pf = P[pp][:].rearrange("p g t -> p (g t)")
                        kb.op('pe', lambda e: e.matmul(o_ps[og][0:64, 0:384], lhsT=V[:, kt, :], rhs=Pf, start=first, stop=lastk),
                              reads=["a_V", f"a_P{pp}"], writes=[f"a_ops{og}"])
                        kb.op('pe', lambda e: e.matmul(d_ps[og][0:64, 0:384], lhsT=ones[:], rhs=Pf, start=first, stop=lastk),
                              reads=["a_ones", f"a_P{pp}"], writes=[f"a_dps{og}"])
                    esv = es[:, kvh * 3:kvh * 3 + 3]
                    kb.op('dve', lambda e: e.tensor_tensor(out=den[:], in0=d_ps[og][0:64, 0:384].rearrange("p (g t) -> p g t", g=3),
                                                           in1=bc(esv, [esv.ap[0], [1, 3], [0, 128]]), op=ALU.add),
                          reads=[f"a_dps{og}", "a_es"], writes=["a_den"])
                    kb.op('dve', lambda e: e.reciprocal(out=den[:], in_=den[:]), reads=["a_den"], writes=["a_den"])
                    kb.op('dve', lambda e: e.tensor_tensor(out=aT[:, :, n * 128:(n + 1) * 128],
                                                           in0=o_ps[og][0:64, 0:384].rearrange("p (g t) -> p g t", g=3), in1=den[:], op=ALU.mult),
                          reads=[f"a_ops{og}", "a_den"], writes=["a_aT"])
                for gi in range(3):
                    hq = kvh * 3 + gi
                    kb.dma(g.ymT[hq * 64:(hq + 1) * 64, c0 + TC:c0 + TS], aT[:, gi, :], reads=["a_aT"], writes=["ymT"])


def stage_o3(kb, g):
    with kb.stage() as st:
        Z = kb.sb("f_Z", [128, 16, 256], BF16)
        tb = [[kb.sb(f"f_tb{j}{i}", [128, 16, 512], BF16) for i in range(2)] for j in range(2)]
        c64 = kb.sb("f_c64", [128, 2, 128], F32)
        c64b = kb.sb("f_c64b", [128, 2, 128], BF16)
        kb.dma(c64[:], g.dft64[:], writes=["f_c64"])
        kb.op('dve', lambda e: e.tensor_copy(out=c64b[:], in_=c64[:]), reads=["f_c64"], writes=["f_c64b"])
        PQ = [kb.sb(f"f_PQ{j}", [128, 512], BF16) for j in range(2)]
        yv = [kb.sb(f"f_y{i}", [128, 512], BF16) for i in range(2)]
        pq_ps = [kb.ps(f"f_pqps{j}", [128, 512], F32) for j in range(2)]
        y_ps = [kb.ps(f"f_yps{i}", [128, 512], F32) for i in range(2)]
        ny = 0
        nb_ = 0
        for b in range(NB):
            c0 = b * TS + TC
            zsrc = AP(g.ztok.tensor, g.ztok.offset + c0 * 256, [[256, 128], [128 * 256, 16], [1, 256]])
            kb.dma(Z[:], zsrc, reads=["ztok"], writes=["f_Z"])
            for tblk in range(4):
                bi = nb_ % 2
                nb_ += 1
                for j in range(2):
                    kb.dma(tb[j][bi][:], g.dftT[j, :, tblk * 512:(tblk + 1) * 512].rearrange("(sc p) t -> p sc t", p=128),
                           writes=[f"f_tb{j}{bi}"])
                for cc in range(2):
                    for j in range(2):
                        for sc in range(16):
                            kb.op('pe', lambda e: e.matmul(pq_ps[j][:], lhsT=Z[:, sc, cc * 128:(cc + 1) * 128], rhs=tb[j][bi][:, sc, :],
                                                           start=(sc == 0), stop=(sc == 15)),
                                  reads=["f_Z", f"f_tb{j}{bi}"], writes=[f"f_pqps{j}"])
                        kb.op('act' if j == 0 else 'dve',
                              (lambda e: e.activation(out=PQ[0][:], in_=pq_ps[0][:], func=AF.Copy)) if j == 0 else
                              (lambda e: e.tensor_copy(out=PQ[1][:], in_=pq_ps[1][:])),
                              reads=[f"f_pqps{j}"], writes=[f"f_PQ{j}"])
                    yi = ny % 2
                    ny += 1
                    kb.op('pe', lambda e: e.matmul(y_ps[yi][:], lhsT=c64b[:, 0, :], rhs=PQ[0][:], start=True, stop=False),
                          reads=["f_c64b", "f_PQ0"], writes=[f"f_yps{yi}"])
                    kb.op('pe', lambda e: e.matmul(y_ps[yi][:], lhsT=c64b[:, 1, :], rhs=PQ[1][:], start=False, stop=True),
                          reads=["f_c64b", "f_PQ1"], writes=[f"f_yps{yi}"])
                    kb.op('act', lambda e: e.activation(out=yv[yi][:], in_=y_ps[yi][:], func=AF.Copy), reads=[f"f_yps{yi}"], writes=[f"f_y{yi}"])
                    kb.dma(g.ymT[768 + cc * 128:768 + (cc + 1) * 128, c0 + tblk * 512:c0 + (tblk + 1) * 512], yv[yi][:],
                           reads=[f"f_y{yi}"], writes=["ymT"])


def stage_o4(kb, g, l, src, dst):
    with kb.stage() as st:
        op = OutProj(kb, g, l, g.o_w_out, 2, "o4")
        yT = [kb.sb(f"o4_yT{i}", [128, 8, 512], BF16) for i in range(2)]
        n = 0
        for b in range(NB):
            for q4 in range(LT // 4):
                t0 = b * ST_ + 2 + q4 * 4
                i = n % 2
                n += 1
                kb.dma(yT[i][:], g.ymT[:, t0 * 128:t0 * 128 + 512].rearrange("(k p) t -> p k t", p=128), reads=["ymT"], writes=[f"o4_yT{i}"])
                op.run(yT[i], f"o4_yT{i}", None, src, dst, tiles=[t0 + j for j in range(4)])


IN_SPECS = {
    "s_in": ([NT, D], F32),
    "cT": ([128, 8, 3], F32),
    "w_mod": ([2, D, 6 * D], F32),
    "b_mod": ([2, 6 * D], F32),
    "gcol": ([128, 2, 2, 8], F32),
    "lbl": ([128, 2, 2, 4], F32),
    "identf": ([128, 128], F32),
    "e_w_in": ([D, 3584], F32),
    "hmask": ([64, 2, 64], F32),
    "rg_blk": ([128, 16, 128], F32),
    "rg_sm": ([128, 44], F32),
    "e_w_out": ([D, D], F32),
    "hg_gain": ([128, 1], F32),
    "p_wq": ([2, D, 2048], F32),
    "keysT": ([2, 128, 16, 128], F32),
    "bqc": ([2, 128, 16], F32),
    "gffn": ([2, D], F32),
    "p_u0": ([16384, D], F32),
    "p_u1": ([16384, D], F32),
    "p_v0": ([16384, D], F32),
    "p_v1": ([16384, D], F32),
    "o_w_in": ([D, 1536], F32),
    "o_w_out": ([D, D], F32),
    "q_gain": ([64], F32),
    "k_gain": ([64], F32),
    "sinks": ([12], F32),
    "rope": ([TS, 2, 32], F32),
    "amask": ([128, 2, 128], F32),
    "dft64": ([128, 2, 128], F32),
    "dftT": ([2, TL, TL], BF16),
}


def build(upto="all", dbg=(), ptiles=None, ptiles1=None):
    nc = bass.Bass("TRN2", target_bir_lowering=False)
    kb = KB(nc)
    g = Ctx()
    g.ptiles = ptiles
    g.ptiles1 = ptiles1
    g.dbg = set(dbg)
    for name, (shape, dt) in IN_SPECS.items():
        setattr(g, name, nc.dram_tensor(name, list(shape), dt, kind="ExternalInput").ap())

    def scratch(name, shape, dt):
        kind = "ExternalOutput" if name in g.dbg else "Internal"
        t = nc.dram_tensor(name, list(shape), dt, kind=kind).ap()
        setattr(g, name, t)
        return t

    scratch("mrow", [2, 3, 6 * D], F32)
    g.out = nc.dram_tensor("out", [NB * TL, D], F32, kind="ExternalOutput").ap()

    scratch("fm", [3072, NT], F32)
    scratch("vtok", [NT, 512], BF16)
    g.stream = g.s_in
    scratch("uv0", [16384, 2 * D], BF16)
    scratch("uv1", [16384, 2 * D], BF16)
    scratch("oT", [512, NT], F32)
    scratch("ybT", [512, NT], BF16)
    scratch("s1", [NT, D], F32)
    scratch("s2", [NT, D], F32)
    scratch("qkT", [16, 64, NT], BF16)
    scratch("vtok2", [NT, 256], BF16)
    scratch("ztok", [NT, 256], BF16)
    scratch("ymT", [D, NT], BF16)
    scratch("s3", [NT, D], F32)
    alloc_globals(kb, g)
    g.bg = None
    if upto.startswith("odd"):
        with kb.stage() as outer0:
            stage_mod(kb, g)
            stage_prep(kb, g)
    else:
      with kb.stage() as outer0:
        g.bg = UVConv(kb, g, 0)
        stage_mod(kb, g)
        stage_prep(kb, g)
        stage_e1(kb, g)
        if upto != "e1":
            stage_e2(kb, g)
        if upto not in ("e1", "e2"):
            stage_e3(kb, g)
        if upto not in ("e1", "e2", "e3"):
            stage_e4(kb, g, 0, g.s_in, g.s1)
        if upto not in ("e1", "e2", "e3", "e4"):
            g.bg.finish()
        g.bg = None
    if upto in ("e1", "e2", "e3", "e4"):
        return finish(kb, nc)
    if True:
        pass
    g.out_toks = []
    ptiles = list(range(NTILE)) if g.ptiles is None else g.ptiles
    if not upto.startswith("odd"):
        stage_peer(kb, g, 0, g.s1, g.s2, ptiles, lambda t: t * 128)
    if upto == "p0":
        return finish(kb, nc)
    s2 = g.s_in if upto.startswith("odd") else g.s2
    with kb.stage() as outer1:
        g.bg = UVConv(kb, g, 1)
        stage_o1(kb, g, 1, s2)
        if upto != "odd1":
            stage_o2(kb, g)
        if upto not in ("odd1", "odd2"):
            stage_o3(kb, g)
        if upto not in ("odd1", "odd2", "odd3"):
            stage_o4(kb, g, 1, s2, g.s3)
        if not upto.startswith("odd"):
            g.bg.finish()
        g.bg = None
    if upto.startswith("odd"):
        return finish(kb, nc)
    lat_tiles = [b * ST_ + 2 + j for b in range(NB) for j in range(LT)]
    g.out_toks = []
    stage_peer(kb, g, 1, g.s3, g.out, lat_tiles if g.ptiles1 is None else g.ptiles1,
               lambda t: (t // ST_) * TL + (t % ST_ - 2) * 128)
    return finish(kb, nc)


def finish(kb, nc):
    kb.barrier()
    print(f"[build] inst={kb.n_inst} waits={kb.n_wait} dmas={kb.ndma}")
    kb.close()
    return nc


def colform(v):
    v = np.asarray(v, np.float32)
    return np.ascontiguousarray(v.reshape(-1, 128).T)


_SHARED = {}


def prep_core(inp, core):
    b0 = core * NB
    key = id(inp)
    if key not in _SHARED:
        _SHARED.clear()
        _SHARED[key] = _prep_shared(inp)
    m = dict(_SHARED[key])
    s = np.concatenate([np.concatenate([inp["ctx"][b0 + i], inp["x"][b0 + i]], axis=0) for i in range(NB)], axis=0)
    m["s_in"] = np.ascontiguousarray(s, np.float32)
    cv = np.stack([inp["c"][b0], inp["c"][b0 + 1], inp["c_ctx"]], axis=1)
    m["cT"] = np.ascontiguousarray(cv.reshape(8, 128, 3).transpose(1, 0, 2), np.float32)
    return m


def _prep_shared(inp):
    m = {}
    m["w_mod"] = np.ascontiguousarray(inp["w_mod"], np.float32)
    m["b_mod"] = np.ascontiguousarray(inp["b_mod"], np.float32)
    gc = np.zeros((128, 2, 2, 8), np.float32)
    for l in range(2):
        gc[:, l, 0, :] = colform(inp["g_mix"][l])
        gc[:, l, 1, :] = colform(inp["g_ffn"][l])
    m["gcol"] = gc
    lbl = np.zeros((128, 2, 2, 4), np.float32)
    for d in range(2):
        for j in range(2):
            lbl[:, d, j, :] = colform(inp["hg_lb_logits"][d, j])
    m["lbl"] = lbl
    m["identf"] = np.eye(128, dtype=np.float32)
    hm = np.zeros((64, 2, 64), np.float32)
    ii = np.arange(64)
    hm[:, 0, :] = (ii[:, None] <= ii[None, :])
    hm[:, 1, :] = (ii[:, None] >= ii[None, :])
    m["hmask"] = hm
    blk = np.zeros((128, 16, 128), np.float32)
    for ct in range(4):
        for ty, wname in enumerate(("rg_wa", "rg_wx")):
            for d in range(2):
                wi = (ct * 2 + ty) * 2 + d
                for half in range(2):
                    blk[half * 64:(half + 1) * 64, wi, half * 64:(half + 1) * 64] = inp[wname][0, d, 2 * ct + half]
    m["rg_blk"] = blk
    sm = np.zeros((128, 44), np.float32)
    for j in range(4):
        sm[:, j * 4:(j + 1) * 4] = colform(inp["rg_conv_w"][0, j])
    sm[:, 16:20] = colform(inp["rg_conv_b"][0])
    for d in range(2):
        sm[:, 20 + d * 4:24 + d * 4] = colform(inp["rg_ba"][0, d])
        sm[:, 28 + d * 4:32 + d * 4] = colform(inp["rg_bx"][0, d])
        sm[:, 36 + d * 4:40 + d * 4] = colform(inp["rg_lambda"][0, d])
    m["rg_sm"] = sm
    m["e_w_out"] = np.ascontiguousarray(inp["e_w_out"][0], np.float32)
    m["p_wq"] = np.ascontiguousarray(inp["p_wq"], np.float32)
    m["keysT"] = np.ascontiguousarray(inp["p_keys"].reshape(2, 16, 128, 128).transpose(0, 3, 1, 2), np.float32)
    m["bqc"] = np.ascontiguousarray(inp["p_bq"].reshape(2, 16, 128).transpose(0, 2, 1), np.float32)
    m["gffn"] = np.ascontiguousarray(inp["g_ffn"], np.float32)
    for l in range(2):
        m[f"p_u{l}"] = np.ascontiguousarray(inp["p_u"][l], np.float32)
        m[f"p_v{l}"] = np.ascontiguousarray(inp["p_v"][l], np.float32)
    m["o_w_in"] = np.ascontiguousarray(inp["o_w_in"][0], np.float32)
    m["o_w_out"] = np.ascontiguousarray(inp["o_w_out"][0], np.float32)
    m["q_gain"] = np.ascontiguousarray(inp["q_gain"][0], np.float32)
    m["k_gain"] = np.ascontiguousarray(inp["k_gain"][0], np.float32)
    m["sinks"] = np.ascontiguousarray(inp["sinks"][0], np.float32)
    m.update(const_tables())
    m["hg_gain"] = np.ascontiguousarray(inp["hg_gain"][0].reshape(128, 1), np.float32)
    m["e_w_in"] = np.ascontiguousarray(inp["e_w_in"][0], np.float32)
    return m


_CONST = {}


def const_tables():
    if _CONST:
        return _CONST
    import ml_dtypes
    rope = np.zeros((TS, 2, 32), np.float32)
    rope[:TC, 0, :] = 1.0
    pos = np.arange(TL)
    row = (pos // 64).astype(np.float64)
    colp = (pos % 64).astype(np.float64)
    inv = 10000.0 ** (-np.arange(16, dtype=np.float64) / 16)
    ang = np.concatenate([row[:, None] * inv, colp[:, None] * inv], axis=-1)
    rope[TC:, 0, :] = np.cos(ang)
    rope[TC:, 1, :] = np.sin(ang)
    _CONST["rope"] = rope
    ii = np.arange(128)
    am = np.zeros((128, 2, 128), np.float32)
    am[:, 0, :] = (ii[:, None] >= ii[None, :])
    am[:, 1, :] = (ii[:, None] <= ii[None, :])
    _CONST["amask"] = am
    k64 = np.arange(64)
    a64 = 2 * np.pi * np.outer(k64, k64) / 64
    c64 = np.cos(a64) / 8.0
    s64 = np.sin(a64) / 8.0
    d64 = np.zeros((128, 2, 128), np.float32)
    for hlf in range(2):
        d64[hlf * 64:(hlf + 1) * 64, 0, hlf * 64:(hlf + 1) * 64] = c64
        d64[hlf * 64:(hlf + 1) * 64, 1, hlf * 64:(hlf + 1) * 64] = -s64
    _CONST["dft64"] = d64
    kT = np.arange(TL)
    aT = 2 * np.pi * ((np.outer(kT, kT) % TL).astype(np.float64)) / TL
    dT = np.stack([np.cos(aT), np.sin(aT)], 0) / np.sqrt(TL)
    _CONST["dftT"] = dT.astype(ml_dtypes.bfloat16)
    return _CONST


_PROG = {}


def kernel(**inputs):
    inp = {k: np.asarray(v) for k, v in inputs.items()}
    if "nc" not in _PROG:
        _PROG["nc"] = build("all")
    nc = _PROG["nc"]
    maps = [prep_core(inp, c) for c in range(8)]
    res = run_bass_kernel_spmd(nc, maps, core_ids=list(range(8)))
    out = np.concatenate([np.asarray(r["out"]).reshape(NB, TL, D) for r in res.results], axis=0)
    return np.ascontiguousarray(out, dtype=np.float32)
```

```python
import numpy as np
from contextlib import ExitStack
import concourse.bass as bass
import concourse.mybir as mybir
from concourse.bass_utils import run_bass_kernel_spmd

F32 = mybir.dt.float32
BF16 = mybir.dt.bfloat16
I32 = mybir.dt.int32
U32 = mybir.dt.uint32
ALU = mybir.AluOpType
AF = mybir.ActivationFunctionType
AX = mybir.AxisListType
AP = bass.AP

NDS = 24
EPOCH = 30000


class KB:
    def __init__(self, nc):
        self.nc = nc
        self.es = ExitStack()
        self.engs = {'pe': nc.tensor, 'dve': nc.vector, 'act': nc.scalar, 'pool': nc.gpsimd, 'sp': nc.sync}
        self.csem = {}
        self.ccnt = {}
        self.nsem = 0
        for e in self.engs:
            self._new_epoch(e)
        self.dsem = [self.es.enter_context(nc.semaphore(f"dma{i}")) for i in range(NDS)]
        self.dcount = [0] * NDS
        self.ndma = 0
        self.seen = {e: {} for e in self.engs}
        self.lastw = {}
        self.readers = {}
        self.stage_stack = []
        self.n_inst = 0
        self.n_wait = 0

    def _new_epoch(self, e):
        self.nsem += 1
        self.csem[e] = (f"c{e}{self.nsem}", self.es.enter_context(self.nc.semaphore(f"c_{e}_{self.nsem}")))
        self.ccnt[e] = 0

    def sb(self, name, shape, dtype, stack=None):
        st = stack if stack is not None else (self.stage_stack[-1] if self.stage_stack else self.es)
        self.uid = getattr(self, "uid", 0) + 1
        return st.enter_context(self.nc.sbuf_tensor(f"sb{self.uid}_{name}", list(shape), dtype))

    def ps(self, name, shape, dtype, stack=None):
        st = stack if stack is not None else (self.stage_stack[-1] if self.stage_stack else self.es)
        self.uid = getattr(self, "uid", 0) + 1
        return st.enter_context(self.nc.psum_tensor(f"ps{self.uid}_{name}", list(shape), dtype))

    def dram(self, name, shape, dtype, kind="Internal"):
        return self.nc.dram_tensor(name, list(shape), dtype, kind=kind)

    def _need(self, eng, tok, waits):
        if tok is None:
            return
        key, sem, cnt, teng = tok
        if teng == 'pe' and eng == 'pe':
            return
        if self.seen[eng].get(key, 0) >= cnt:
            return
        if key not in waits or waits[key][1] < cnt:
            waits[key] = (sem, cnt)

    def _collect(self, eng, reads, writes):
        waits = {}
        for r in reads:
            self._need(eng, self.lastw.get(r), waits)
        for w in writes:
            self._need(eng, self.lastw.get(w), waits)
            for t in self.readers.get(w, {}).values():
                self._need(eng, t, waits)
        for key, (sem, cnt) in waits.items():
            self.engs[eng].wait_ge(sem, cnt)
            self.seen[eng][key] = cnt
            self.n_wait += 1

    def _record(self, tok, reads, writes):
        for r in reads:
            self.readers.setdefault(r, {})[tok[0] if tok[3] == 'dma' else tok[3]] = tok
        for w in writes:
            self.lastw[w] = tok
            self.readers[w] = {}

    def op(self, eng, fn, reads=(), writes=()):
        self._collect(eng, reads, writes)
        inst = fn(self.engs[eng])
        if self.ccnt[eng] >= EPOCH:
            self._new_epoch(eng)
        key, sem = self.csem[eng]
        inst.then_inc(sem, 1)
        self.ccnt[eng] += 1
        tok = (key, sem, self.ccnt[eng], eng)
        self._record(tok, reads, writes)
        self.n_inst += 1
        return tok

    def dma(self, out, in_, reads=(), writes=(), q='sp', fn=None, **kw):
        i = self.ndma % NDS
        sem = self.dsem[i]
        self._collect(q, reads, writes)
        if self.dcount[i] > 0 and self.seen[q].get(f"d{i}", 0) < 16 * self.dcount[i]:
            self.engs[q].wait_ge(sem, 16 * self.dcount[i])
            self.seen[q][f"d{i}"] = 16 * self.dcount[i]
        if fn is not None:
            inst = fn(self.engs[q])
        else:
            inst = self.engs[q].dma_start(out=out, in_=in_, **kw)
        inst.then_inc(sem, 16)
        self.dcount[i] += 1
        self.ndma += 1
        tok = (f"d{i}", sem, 16 * self.dcount[i], 'dma')
        self._record(tok, reads, writes)
        self.n_inst += 1
        return tok

    def barrier(self):
        for e in self.engs:
            for f in self.engs:
                if f == e:
                    continue
                key, sem = self.csem[f]
                c = self.ccnt[f]
                if c > 0 and self.seen[e].get(key, 0) < c:
                    self.engs[e].wait_ge(sem, c)
                    self.seen[e][key] = c
            for i in range(NDS):
                c = 16 * self.dcount[i]
                if c > 0 and self.seen[e].get(f"d{i}", 0) < c:
                    self.engs[e].wait_ge(self.dsem[i], c)
                    self.seen[e][f"d{i}"] = c
        self.lastw = {}
        self.readers = {}

    def final_wait(self, toks, eng='sp'):
        for tok in toks:
            key, sem, cnt, _ = tok
            self.engs[eng].wait_ge(sem, cnt)

    class _Stage:
        def __init__(self, kb):
            self.kb = kb

        def __enter__(self):
            st = ExitStack()
            self.kb.stage_stack.append(st)
            return st

        def __exit__(self, *a):
            self.kb.barrier()
            st = self.kb.stage_stack.pop()
            st.close()

    def stage(self):
        return KB._Stage(self)

    def close(self):
        self.es.close()


def bc(ap, shape_pairs, offset_add=0):
    return AP(ap.tensor, ap.offset + offset_add, [list(p) for p in shape_pairs])


D = 1024
NB = 2
TC = 256
TL = 2048
TS = TC + TL
NT = NB * TS
NTILE = NT // 128
EPS = 1e-6


def tile_info(t):
    b = t // (TS // 128)
    r = t % (TS // 128)
    return b, r < (TC // 128)


def mod_col(t):
    b, isc = tile_info(t)
    return 2 if isc else b


class Ctx:
    def dump(self, kb, name, ap, key, dtype=F32):
        if name not in self.dbg or hasattr(self, "_d_" + name):
            return
        t = kb.nc.dram_tensor(name, list(ap.shape), dtype, kind="ExternalOutput").ap()
        setattr(self, "_d_" + name, t)
        kb.dma(t, ap, reads=[key], writes=["dbg_" + name])


def stage_mod(kb, g):
    with kb.stage() as st:
        cT = kb.sb("m_cT", [128, 8, 3], F32)
        sc = kb.sb("m_sc", [128, 8, 3], F32)
        kb.dma(cT[:], g.cT[:], writes=["m_cT"])
        kb.op('act', lambda e: e.activation(out=sc[:], in_=cT[:], func=AF.Silu), reads=["m_cT"], writes=["m_sc"])
        wb = [kb.sb(f"m_w{i}", [128, 8, 512], F32) for i in range(2)]
        bb = [kb.sb(f"m_b{i}", [3, 512], F32) for i in range(2)]
        ob = [kb.sb(f"m_o{i}", [3, 512], F32) for i in range(2)]
        pss = [kb.ps(f"m_ps{i}", [3, 512], F32) for i in range(2)]
        it = 0
        for l in range(2):
            for cb in range(12):
                i = it % 2
                it += 1
                kb.dma(wb[i][:], g.w_mod[l, :, cb * 512:(cb + 1) * 512].rearrange("(k p) n -> p k n", p=128),
                       writes=[f"m_w{i}"])
                kb.dma(bb[i][:], bc(g.b_mod[l, cb * 512:(cb + 1) * 512], [[0, 3], [1, 512]]), writes=[f"m_b{i}"])
                for k in range(8):
                    kb.op('pe', lambda e, k=k, i=i: e.matmul(pss[i][:], lhsT=sc[:, k, :], rhs=wb[i][:, k, :],
                                                             start=(k == 0), stop=(k == 7)),
                          reads=["m_sc", f"m_w{i}"], writes=[f"m_ps{i}"])
                kb.op('dve', lambda e, i=i: e.tensor_tensor(out=ob[i][:], in0=pss[i][:], in1=bb[i][:], op=ALU.add),
                      reads=[f"m_ps{i}", f"m_b{i}"], writes=[f"m_o{i}"])
                kb.dma(g.mrow[l, :, cb * 512:(cb + 1) * 512], ob[i][:], reads=[f"m_o{i}"], writes=["mrow"])


def alloc_globals(kb, g):
    g.AB = kb.sb("AB", [128, 2 * 2 * 3 * 2 * 8], F32, stack=kb.es)
    g.lbc = kb.sb("lbc", [128, 2, 2, 4], F32, stack=kb.es)
    g.ident = kb.sb("ident", [128, 128], BF16, stack=kb.es)
    g.idf = kb.sb("idf", [128, 128], F32, stack=kb.es)


def stage_prep(kb, g):
    kb.dma(g.idf[:], g.identf[:], writes=["idf"])
    kb.op('dve', lambda e: e.tensor_copy(out=g.ident[:], in_=g.idf[:]), reads=["idf"], writes=["ident"])
    with kb.stage() as st:
        mcol = kb.sb("p_mcol", [128, 2, 3, 48], F32)
        gcol = kb.sb("p_gcol", [128, 2, 2, 8], F32)
        lg = kb.sb("p_lg", [128, 2, 2, 4], F32)
        dd = kb.sb("p_dd", [128, 2, 4], F32)
        for l in range(2):
            src = AP(g.mrow.tensor, g.mrow.offset + l * 3 * 6144, [[1, 128], [6144, 3], [128, 48]])
            kb.dma(mcol[:, l], src, reads=["mrow"], writes=["p_mcol"], allow_slow_non_contiguous=True)
        kb.dma(gcol[:], g.gcol[:], writes=["p_gcol"])
        kb.dma(lg[:], g.lbl[:], writes=["p_lg"])
        AB = g.AB[:].rearrange("p (l n c a k) -> p l n c a k", l=2, n=2, c=3, a=2)
        for l in range(2):
            for n in range(2):
                for c in range(3):
                    sh = mcol[:, l, c, (3 * n) * 8:(3 * n) * 8 + 8]
                    scl = mcol[:, l, c, (3 * n + 1) * 8:(3 * n + 1) * 8 + 8]
                    kb.op('dve', lambda e, l=l, n=n, c=c, scl=scl: e.scalar_tensor_tensor(
                        out=AB[:, l, n, c, 0, :], in0=scl, scalar=1.0, in1=gcol[:, l, n, :], op0=ALU.add, op1=ALU.mult),
                        reads=["p_mcol", "p_gcol"], writes=["AB"])
                    kb.op('dve', lambda e, l=l, n=n, c=c, sh=sh: e.tensor_copy(out=AB[:, l, n, c, 1, :], in_=sh),
                          reads=["p_mcol"], writes=["AB"])
        kb.op('dve', lambda e: e.tensor_tensor(out=dd[:], in0=lg[:, :, 0, :], in1=lg[:, :, 1, :], op=ALU.subtract),
              reads=["p_lg"], writes=["p_dd"])
        kb.op('act', lambda e: e.activation(out=g.lbc[:, 0], in_=dd[:], func=AF.Sigmoid), reads=["p_dd"], writes=["lbc"])
        kb.op('dve', lambda e: e.tensor_scalar(out=g.lbc[:, 1], in0=g.lbc[:, 0], scalar1=-1.0, scalar2=1.0,
                                               op0=ALU.mult, op1=ALU.add), reads=["lbc"], writes=["lbc"])


def load_weight_bf(kb, wbf, wsrc, K, N, tag, pieces=512):
    KC = K // 128
    stg = [kb.sb(f"{tag}_stg{i}", [128, KC, pieces], F32) for i in range(2)]
    for j, c0 in enumerate(range(0, N, pieces)):
        i = j % 2
        n = min(pieces, N - c0)
        kb.dma(stg[i][:, :, :n], wsrc[:, c0:c0 + n].rearrange("(k p) n -> p k n", p=128), writes=[f"{tag}_stg{i}"])
        kb.op('pool', lambda e, i=i, c0=c0, n=n: e.tensor_copy(out=wbf[:, :, c0:c0 + n], in_=stg[i][:, :, :n]),
              reads=[f"{tag}_stg{i}"], writes=[tag])


class NormT:
    def __init__(self, kb, g, tag, nx=2):
        self.kb, self.g, self.tag = kb, g, tag
        self.nx = nx
        self.xt = [kb.sb(f"{tag}_xt{i}", [128, D], F32) for i in range(nx)]
        self.xn = [kb.sb(f"{tag}_xn{i}", [128, D], BF16) for i in range(2)]
        self.junk = kb.sb(f"{tag}_junk", [128, D], BF16)
        self.ss = [kb.sb(f"{tag}_ss{i}", [128, 2], F32) for i in range(2)]
        self.tp = [kb.ps(f"{tag}_tp{i}", [128, 8, 128], BF16) for i in range(2)]
        self.n = 0

    def run(self, src_tile_ap, l, nrm, col, dst, dst_key, keep_x=None):
        kb, g, tag = self.kb, self.g, self.tag
        i = self.n % 2
        ix = self.n % self.nx
        self.n += 1
        xt, xn, ss, tp = self.xt[ix], self.xn[i], self.ss[i], self.tp[i]
        kx, kn, ks, kt = f"{tag}_xt{ix}", f"{tag}_xn{i}", f"{tag}_ss{i}", f"{tag}_tp{i}"
        kb.dma(xt[:], src_tile_ap, reads=["stream"], writes=[kx])
        kb.op('act', lambda e: e.activation(out=self.junk[:], in_=xt[:], func=AF.Square, accum_out=ss[:, 0:1]),
              reads=[kx], writes=[ks, f"{tag}_junk"])
        kb.op('dve', lambda e: e.tensor_scalar(out=ss[:, 1:2], in0=ss[:, 0:1], scalar1=1.0 / D, scalar2=EPS,
                                               op0=ALU.mult, op1=ALU.add), reads=[ks], writes=[ks])
        kb.op('act', lambda e: e.activation(out=ss[:, 1:2], in_=ss[:, 1:2], func=AF.Sqrt), reads=[ks], writes=[ks])
        kb.op('dve', lambda e: e.reciprocal(out=ss[:, 1:2], in_=ss[:, 1:2]), reads=[ks], writes=[ks])
        kb.op('act', lambda e: e.activation(out=xn[:], in_=xt[:], func=AF.Identity, scale=ss[:, 1:2]),
              reads=[kx, ks], writes=[kn])
        for k in range(8):
            kb.op('pe', lambda e, k=k: e.transpose(out=tp[:, k, :], in_=xn[:, k * 128:(k + 1) * 128], identity=g.ident[:]),
                  reads=[kn, "ident"], writes=[kt])
        AB = g.AB[:].rearrange("p (l n c a k) -> p l n c a k", l=2, n=2, c=3, a=2)
        A = AB[:, l, nrm, col, 0, :]
        B = AB[:, l, nrm, col, 1, :]
        for k in range(8):
            kb.op('act', lambda e, k=k: e.activation(out=dst[:, k, :], in_=tp[:, k, :], func=AF.Identity,
                                                      scale=A[:, k:k + 1], bias=B[:, k:k + 1]),
                  reads=[kt, "AB"], writes=[dst_key])
        return xt, kx, ss, ks


E_FM = {}
for _i in range(4):
    E_FM[_i] = ("q", _i)
    E_FM[8 + _i] = ("ffw", 4 + _i)
    E_FM[12 + _i] = ("fbw", 8 + _i)
    E_FM[16 + _i] = ("g", 12 + _i)
    E_FM[20 + _i] = ("xr", 16 + _i)
    E_FM[24 + _i] = ("gate", 20 + _i)


def stage_e1(kb, g, l=0):
    with kb.stage() as st:
        wbf = kb.sb("e1_w", [128, 8, 3584], BF16)
        load_weight_bf(kb, wbf, g.e_w_in, D, 3584, "e1_w")
        nt = NormT(kb, g, "e1n")
        hT = [kb.sb(f"e1_hT{i}", [128, 8, 512], BF16) for i in range(2)]
        pss = [kb.ps(f"e1_ps{i}", [128, 512], F32) for i in range(4)]
        ev = [kb.sb(f"e1_ev{i}", [128, 512], F32) for i in range(4)]
        evv = [kb.sb(f"e1_evv{i}", [64, 512], BF16) for i in range(2)]
        nps = 0
        nev = 0
        nvv = 0
        def norm_block(blk):
            hi = blk % 2
            for tt in range(4):
                t = blk * 4 + tt
                nt.run(g.stream[t * 128:(t + 1) * 128, :], l, 0, mod_col(t), hT[hi][:, :, tt * 128:(tt + 1) * 128], f"e1_hT{hi}")
                bg_step(g)

        norm_block(0)
        for blk in range(NT // 512):
            hi = blk % 2
            if blk + 1 < NT // 512:
                norm_block(blk + 1)
            order = [0, 1, 2, 3, 16, 17, 18, 19, 8, 9, 10, 11, 12, 13, 14, 15, 20, 21, 22, 23, 24, 25, 26, 27]
            for cc in order:
                kind, fr = E_FM[cc]
                p = nps % 4
                nps += 1
                for k in range(8):
                    kb.op('pe', lambda e, k=k, p=p, cc=cc: e.matmul(pss[p][:], lhsT=wbf[:, k, cc * 128:(cc + 1) * 128],
                                                                   rhs=hT[hi][:, k, :], start=(k == 0), stop=(k == 7)),
                          reads=["e1_w", f"e1_hT{hi}"], writes=[f"e1_ps{p}"])
                v = nev % 4
                nev += 1
                if kind in ("q", "g"):
                    kb.op('act', lambda e, p=p, v=v: e.activation(out=ev[v][:], in_=pss[p][:], func=AF.Silu),
                          reads=[f"e1_ps{p}"], writes=[f"e1_ev{v}"])
                elif kind in ("ffw", "fbw"):
                    d = 0 if kind == "ffw" else 1
                    ch = cc % 4
                    kb.op('act', lambda e, p=p, v=v: e.activation(out=ev[v][:], in_=pss[p][:], func=AF.Sigmoid),
                          reads=[f"e1_ps{p}"], writes=[f"e1_ev{v}"])
                    kb.op('dve', lambda e, v=v, d=d, ch=ch: e.tensor_scalar(
                        out=ev[v][:], in0=ev[v][:], scalar1=g.lbc[:, 1, d, ch:ch + 1], scalar2=g.lbc[:, 0, d, ch:ch + 1],
                        op0=ALU.mult, op1=ALU.add), reads=[f"e1_ev{v}", "lbc"], writes=[f"e1_ev{v}"])
                else:
                    kb.op('dve', lambda e, p=p, v=v: e.tensor_copy(out=ev[v][:], in_=pss[p][:]),
                          reads=[f"e1_ps{p}"], writes=[f"e1_ev{v}"])
                kb.dma(g.fm[fr * 128:(fr + 1) * 128, blk * 512:(blk + 1) * 512], ev[v][:], reads=[f"e1_ev{v}"], writes=["fm"])
            for c8 in range(8):
                p = nps % 4
                nps += 1
                for k in range(8):
                    kb.op('pe', lambda e, k=k, p=p, c8=c8: e.matmul(pss[p][0:64, :], lhsT=hT[hi][:, k, c8 * 64:(c8 + 1) * 64],
                                                                   rhs=wbf[:, k, 512:1024], start=(k == 0), stop=(k == 7)),
                          reads=["e1_w", f"e1_hT{hi}"], writes=[f"e1_ps{p}"])
                v = nvv % 2
                nvv += 1
                kb.op('act', lambda e, p=p, v=v: e.activation(out=evv[v][:], in_=pss[p][0:64, :], func=AF.Copy),
                      reads=[f"e1_ps{p}"], writes=[f"e1_evv{v}"])
                r0 = blk * 512 + c8 * 64
                kb.dma(g.vtok[r0:r0 + 64, :], evv[v][:], reads=[f"e1_evv{v}"], writes=["vtok"])


NCH = TS // 64
NCC = TC // 64


def stage_e2(kb, g):
    with kb.stage() as st:
        W = TS
        q = kb.sb("h_q", [128, W], F32)
        f = [kb.sb(f"h_f{d}", [128, W], F32) for d in range(2)]
        lf = kb.sb("h_lf", [128, W], F32)
        a = kb.sb("h_a", [128, W], F32)
        E = kb.sb("h_E", [128, W], F32)
        omf = kb.sb("h_omf", [128, W], F32)
        tmp = kb.sb("h_tmp", [128, W], F32)
        rmask = kb.sb("h_rmask", [128, W], F32)
        qe = [kb.sb(f"h_qe{d}", [128, W], BF16) for d in range(2)]
        ke = [kb.sb(f"h_ke{d}", [128, W], BF16) for d in range(2)]
        kend = [kb.sb(f"h_kend{d}", [128, W], BF16) for d in range(2)]
        edec = [kb.sb(f"h_edec{d}", [128, NCH], F32) for d in range(2)]
        oT = [kb.sb(f"h_oT{d}", [128, W], F32) for d in range(2)]
        vt = kb.sb("h_vt", [64, NCH, 128], BF16)
        msk = kb.sb("h_msk", [64, 2, 64], F32)
        S32 = [kb.sb(f"h_S32{d}", [128, 128], F32) for d in range(2)]
        Sbf = [kb.sb(f"h_Sbf{d}", [128, 128], BF16) for d in range(2)]
        attsb = [[kb.sb(f"h_att{d}{i}", [64, 64], BF16) for i in range(2)] for d in range(2)]
        kendsb = [[kb.sb(f"h_ks{d}{i}", [64, 128], BF16) for i in range(2)] for d in range(2)]
        att_ps = [kb.ps(f"h_attps{i}", [128, 512], F32)[0:64, 0:64] for i in range(2)]
        kend_ps = [kb.ps(f"h_kps{i}", [128, 1024], BF16)[0:64, 0:128] for i in range(2)]
        o_ps = [kb.ps(f"h_ops{i}", [128, 512], F32)[:, 0:64] for i in range(2)]
        sup_ps = [kb.ps(f"h_sps{i}", [128, 512], F32)[:, 0:128] for i in range(2)]

        kb.dma(msk[:], g.hmask[:], writes=["h_msk"])
        kb.op('pool', lambda e: e.memset(rmask[:], 1.0), writes=["h_rmask"])
        r3 = rmask[:].rearrange("p (c t) -> p c t", t=64)
        kb.op('pool', lambda e: e.memset(r3[:, :, 0:1], 0.0), writes=["h_rmask"])

        def c3(t):
            return t[:].rearrange("p (c t) -> p c t", t=64)

        for b in range(NB):
            for h in range(4):
                c0 = b * TS
                kb.dma(q[:], g.fm[h * 128:(h + 1) * 128, c0:c0 + W], reads=["fm"], writes=["h_q"])
                for d in range(2):
                    kb.dma(f[d][:], g.fm[(4 + 4 * d + h) * 128:(5 + 4 * d + h) * 128, c0:c0 + W], reads=["fm"], writes=[f"h_f{d}"])
                vsrc = AP(g.vtok.tensor, g.vtok.offset + c0 * 512 + h * 128, [[512, 64], [64 * 512, NCH], [1, 128]])
                kb.dma(vt[:], vsrc, reads=["vtok"], writes=["h_vt"])
                for d in range(2):
                    fd = f[d]
                    kf = f"h_f{d}"
                    kb.op('act', lambda e: e.activation(out=lf[:], in_=fd[:], func=AF.Ln), reads=[kf], writes=["h_lf"])
                    kb.op('dve', lambda e: e.tensor_tensor_scan(out=a[:], data0=rmask[:], data1=lf[:], initial=0.0,
                                                                op0=ALU.mult, op1=ALU.add),
                          reads=["h_rmask", "h_lf"], writes=["h_a"])
                    a3 = c3(a)
                    if d == 1:
                        kb.op('dve', lambda e: e.tensor_tensor(out=tmp[:], in0=lf[:], in1=a[:], op=ALU.subtract),
                              reads=["h_lf", "h_a"], writes=["h_tmp"])
                        bend = a3[:, :, 63:64]
                        bend_b = bc(bend, [bend.ap[0], bend.ap[1], [0, 64]])
                        kb.op('dve', lambda e: e.tensor_tensor(out=c3(lf), in0=c3(tmp), in1=bend_b, op=ALU.add),
                              reads=["h_tmp", "h_a"], writes=["h_lf"])
                        kb.op('pool', lambda e: e.tensor_copy(out=a[:], in_=lf[:]), reads=["h_lf"], writes=["h_a"])
                    last = 63 if d == 0 else 0
                    alast = a3[:, :, last:last + 1]
                    alast_b = bc(alast, [alast.ap[0], alast.ap[1], [0, 64]])
                    kb.op('act', lambda e: e.activation(out=E[:], in_=a[:], func=AF.Exp), reads=["h_a"], writes=["h_E"])
                    kb.op('dve', lambda e: e.tensor_tensor(out=qe[d][:], in0=q[:], in1=E[:], op=ALU.mult),
                          reads=["h_q", "h_E"], writes=[f"h_qe{d}"])
                    E3 = c3(E)
                    kb.op('pool', lambda e: e.tensor_copy(out=edec[d][:], in_=E3[:, :, last]), reads=["h_E"], writes=[f"h_edec{d}"])
                    kb.op('dve', lambda e: e.tensor_scalar(out=omf[:], in0=fd[:], scalar1=-1.0, scalar2=1.0,
                                                           op0=ALU.mult, op1=ALU.add), reads=[kf], writes=["h_omf"])
                    kb.op('act', lambda e: e.activation(out=E[:], in_=a[:], func=AF.Exp, scale=-1.0), reads=["h_a"], writes=["h_E"])
                    kb.op('dve', lambda e: e.tensor_tensor(out=ke[d][:], in0=omf[:], in1=E[:], op=ALU.mult),
                          reads=["h_omf", "h_E"], writes=[f"h_ke{d}"])
                    kb.op('dve', lambda e: e.tensor_tensor(out=c3(tmp), in0=alast_b, in1=a3, op=ALU.subtract),
                          reads=["h_a"], writes=["h_tmp"])
                    kb.op('act', lambda e: e.activation(out=E[:], in_=tmp[:], func=AF.Exp), reads=["h_tmp"], writes=["h_E"])
                    kb.op('dve', lambda e: e.tensor_tensor(out=kend[d][:], in0=omf[:], in1=E[:], op=ALU.mult),
                          reads=["h_omf", "h_E"], writes=[f"h_kend{d}"])
                    if d == 0:
                        g.dump(kb, "d_a", a[:], "h_a")
                        g.dump(kb, "d_lf", lf[:], "h_lf")
                        g.dump(kb, "d_qe", qe[0][:], "h_qe0", BF16)
                        g.dump(kb, "d_ke", ke[0][:], "h_ke0", BF16)
                        g.dump(kb, "d_kend", kend[0][:], "h_kend0", BF16)
                        g.dump(kb, "d_edec", edec[0][:], "h_edec0")
                    kb.op('pool', lambda e: e.memset(S32[d][:], 0.0), writes=[f"h_S32{d}"])
                    kb.op('pool', lambda e: e.memset(Sbf[d][:], 0.0), writes=[f"h_Sbf{d}"])
                order = [list(range(NCH)), list(range(NCC - 1, -1, -1)) + list(range(NCH - 1, NCC - 1, -1))]
                for stp in range(NCH):
                    if stp % 9 == 4:
                        bg_step(g)
                    for d in range(2):
                        ci = order[d][stp]
                        i = stp % 2
                        cs = slice(ci * 64, (ci + 1) * 64)
                        kb.op('pe', lambda e: e.matmul(att_ps[i], lhsT=ke[d][:, cs], rhs=qe[d][:, cs], start=True, stop=True),
                              reads=[f"h_ke{d}", f"h_qe{d}"], writes=[f"h_attps{i}"])
                        kb.op('dve', lambda e: e.tensor_tensor(out=attsb[d][i][:], in0=att_ps[i], in1=msk[:, d, :], op=ALU.mult),
                              reads=[f"h_attps{i}", "h_msk"], writes=[f"h_att{d}{i}"])
                        if d == 0 and stp == 0:
                            g.dump(kb, "d_att", attsb[0][0][:], "h_att00", BF16)
                        kb.op('pe', lambda e: e.transpose(out=kend_ps[i], in_=kend[d][:, cs], identity=g.ident[:]),
                              reads=[f"h_kend{d}", "ident"], writes=[f"h_kps{i}"])
                        kb.op('act', lambda e: e.activation(out=kendsb[d][i][:], in_=kend_ps[i], func=AF.Copy),
                              reads=[f"h_kps{i}"], writes=[f"h_ks{d}{i}"])
                        kb.op('pe', lambda e: e.matmul(o_ps[i], lhsT=Sbf[d][:], rhs=qe[d][:, cs], start=True, stop=False),
                              reads=[f"h_Sbf{d}", f"h_qe{d}"], writes=[f"h_ops{i}"])
                        kb.op('pe', lambda e: e.matmul(o_ps[i], lhsT=vt[:, ci, :], rhs=attsb[d][i][:], start=False, stop=True),
                              reads=["h_vt", f"h_att{d}{i}"], writes=[f"h_ops{i}"])
                        kb.op('act', lambda e: e.activation(out=oT[d][:, cs], in_=o_ps[i], func=AF.Copy),
                              reads=[f"h_ops{i}"], writes=[f"h_oT{d}"])
                        kb.op('pe', lambda e: e.matmul(sup_ps[i], lhsT=kendsb[d][i][:], rhs=vt[:, ci, :], start=True, stop=True),
                              reads=[f"h_ks{d}{i}", "h_vt"], writes=[f"h_sps{i}"])
                        kb.op('dve', lambda e: e.scalar_tensor_tensor(out=S32[d][:], in0=S32[d][:], scalar=edec[d][:, ci:ci + 1],
                                                                     in1=sup_ps[i], op0=ALU.mult, op1=ALU.add),
                              reads=[f"h_S32{d}", f"h_edec{d}", f"h_sps{i}"], writes=[f"h_S32{d}"])
                        kb.op('act', lambda e: e.activation(out=Sbf[d][:], in_=S32[d][:], func=AF.Copy),
                              reads=[f"h_S32{d}"], writes=[f"h_Sbf{d}"])
                        if d == 0 and stp == 0:
                            g.dump(kb, "d_S0", S32[0][:], "h_S320")
                            g.dump(kb, "d_ks0", kendsb[0][0][:], "h_ks00", BF16)
                            g.dump(kb, "d_vt", vt[:], "h_vt", BF16)
                        if d == 0 and stp == 1:
                            g.dump(kb, "d_S1", S32[0][:], "h_S320")
                        if d == 0 and stp == 2:
                            g.dump(kb, "d_o01", oT[0][:, 0:192], "h_oT0")
                kb.op('pool', lambda e: e.tensor_tensor(out=oT[0][:], in0=oT[0][:], in1=oT[1][:], op=ALU.add),
                      reads=["h_oT0", "h_oT1"], writes=["h_oT0"])
                kb.dma(g.oT[h * 128:(h + 1) * 128, c0:c0 + W], oT[0][:], reads=["h_oT0"], writes=["oT"])


GELU_C = 1.5957691216057308
STAIR = [(0, 1, 16), (1, 1, 8), (2, 1, 5), (3, 1, 4), (4, 1, 3), (5, 3, 2), (8, 8, 1)]
NCAND = sum(ni * ln for _, ni, ln in STAIR)
VPE = True


def gelu_tanh(kb, out, x, tmp, kx, kt, ko, eng2='pool'):
    kb.op('act', lambda e: e.activation(out=out, in_=x, func=AF.Gelu_apprx_tanh), reads=[kx], writes=[ko])


def stage_e3(kb, g):
    with kb.stage() as st:
        W = TS
        wblk32 = kb.sb("r_w32", [128, 16, 128], F32)
        wblk = kb.sb("r_w", [128, 16, 128], BF16)
        sm = kb.sb("r_sm", [128, 44], F32)
        spc = kb.sb("r_spc", [128, 2, 2, 4], F32)
        kb.dma(wblk32[:], g.rg_blk[:], writes=["r_w32"])
        kb.op('pool', lambda e: e.tensor_copy(out=wblk[:], in_=wblk32[:]), reads=["r_w32"], writes=["r_w"])
        kb.dma(sm[:], g.rg_sm[:], writes=["r_sm"])
        smv = sm[:]
        cw = smv[:, 0:16].rearrange("p (j c) -> p j c", j=4)
        cb = smv[:, 16:20]
        ba = smv[:, 20:28].rearrange("p (d c) -> p d c", d=2)
        bx = smv[:, 28:36].rearrange("p (d c) -> p d c", d=2)
        lam = smv[:, 36:44].rearrange("p (d c) -> p d c", d=2)
        kb.op('act', lambda e: e.activation(out=spc[:, 0], in_=lam, func=AF.Exp, scale=-1.0), reads=["r_sm"], writes=["r_spc"])
        kb.op('act', lambda e: e.activation(out=spc[:, 0], in_=spc[:, 0], func=AF.Ln, bias=1.0), reads=["r_spc"], writes=["r_spc"])
        kb.op('dve', lambda e: e.tensor_scalar(out=spc[:, 1], in0=spc[:, 0], scalar1=-16.0, scalar2=None, op0=ALU.mult),
              reads=["r_spc"], writes=["r_spc"])
        kb.op('dve', lambda e: e.tensor_scalar(out=spc[:, 0], in0=spc[:, 0], scalar1=-8.0, scalar2=None, op0=ALU.mult),
              reads=["r_spc"], writes=["r_spc"])

        x = kb.sb("r_x", [128, W], F32)
        gt = kb.sb("r_gt", [128, W], F32)
        xc = kb.sb("r_xc", [128, W], F32)
        xcb = kb.sb("r_xcb", [128, W], BF16)
        r = kb.sb("r_r", [128, W], F32)
        ig = kb.sb("r_i", [128, W], F32)
        aa = kb.sb("r_a", [128, W], F32)
        u = kb.sb("r_u", [128, W], F32)
        hf = kb.sb("r_hf", [128, W], F32)
        hb = kb.sb("r_hb", [128, W], F32)
        tmp = kb.sb("r_tmp", [128, W], F32)
        yb = kb.sb("r_yb", [128, W], BF16)
        pss = [kb.ps(f"r_ps{i}", [128, 512], F32) for i in range(4)]
        nps = 0
        segs = [(0, TC), (TC, TS)]
        tblocks = [(i * 512, min(512, W - i * 512)) for i in range((W + 511) // 512)]
        for b in range(NB):
            for ct in range(4):
                c0 = b * TS
                kb.dma(x[:], g.fm[(16 + ct) * 128:(17 + ct) * 128, c0:c0 + W], reads=["fm"], writes=["r_x"])
                kb.dma(gt[:], g.fm[(20 + ct) * 128:(21 + ct) * 128, c0:c0 + W], reads=["fm"], writes=["r_gt"])
                kb.op('dve', lambda e: e.tensor_scalar(out=xc[:], in0=x[:], scalar1=cw[:, 2, ct:ct + 1], scalar2=cb[:, ct:ct + 1],
                                                       op0=ALU.mult, op1=ALU.add), reads=["r_x", "r_sm"], writes=["r_xc"])
                for j in (0, 1, 3):
                    sft = j - 2
                    for (s0, s1) in segs:
                        lo = max(s0, s0 - sft)
                        hi = min(s1, s1 - sft)
                        kb.op('dve', lambda e: e.scalar_tensor_tensor(out=xc[:, lo:hi], in0=x[:, lo + sft:hi + sft],
                                                                     scalar=cw[:, j, ct:ct + 1], in1=xc[:, lo:hi],
                                                                     op0=ALU.mult, op1=ALU.add),
                              reads=["r_x", "r_sm", "r_xc"], writes=["r_xc"])
                kb.op('pool', lambda e: e.tensor_copy(out=xcb[:], in_=xc[:]), reads=["r_xc"], writes=["r_xcb"])
                for d in range(2):
                    for ty, dst, bias, kd in ((0, r, ba, "r_r"), (1, ig, bx, "r_i")):
                        wi = (ct * 2 + ty) * 2 + d
                        for (t0, tn) in tblocks:
                            p = nps % 4
                            nps += 1
                            kb.op('pe', lambda e: e.matmul(pss[p][:, :tn], lhsT=wblk[:, wi, :], rhs=xcb[:, t0:t0 + tn], start=True, stop=True),
                                  reads=["r_w", "r_xcb"], writes=[f"r_ps{p}"])
                            kb.op('act', lambda e: e.activation(out=dst[:, t0:t0 + tn], in_=pss[p][:, :tn], func=AF.Sigmoid,
                                                                bias=bias[:, d, ct:ct + 1]),
                                  reads=[f"r_ps{p}", "r_sm"], writes=[kd])
                    kb.op('act', lambda e: e.activation(out=aa[:], in_=r[:], func=AF.Exp, scale=spc[:, 0, d, ct:ct + 1]),
                          reads=["r_r", "r_spc"], writes=["r_a"])
                    kb.op('act', lambda e: e.activation(out=tmp[:], in_=r[:], func=AF.Exp, scale=spc[:, 1, d, ct:ct + 1]),
                          reads=["r_r", "r_spc"], writes=["r_tmp"])
                    kb.op('dve', lambda e: e.tensor_scalar(out=tmp[:], in0=tmp[:], scalar1=-1.0, scalar2=1.0, op0=ALU.mult, op1=ALU.add),
                          reads=["r_tmp"], writes=["r_tmp"])
                    kb.op('act', lambda e: e.activation(out=tmp[:], in_=tmp[:], func=AF.Sqrt), reads=["r_tmp"], writes=["r_tmp"])
                    kb.op('pool', lambda e: e.tensor_tensor(out=u[:], in0=ig[:], in1=xc[:], op=ALU.mult), reads=["r_i", "r_xc"], writes=["r_u"])
                    kb.op('dve', lambda e: e.tensor_tensor(out=u[:], in0=u[:], in1=tmp[:], op=ALU.mult), reads=["r_u", "r_tmp"], writes=["r_u"])
                    if d == 0:
                        kb.op('dve', lambda e: e.tensor_tensor_scan(out=hf[:], data0=aa[:], data1=u[:], initial=0.0,
                                                                    op0=ALU.mult, op1=ALU.add), reads=["r_a", "r_u"], writes=["r_hf"])
                    else:
                        def rev(tl, s0, s1):
                            v = tl[:, s0:s1]
                            return bc(v, [v.ap[0], [-1, s1 - s0]], offset_add=(s1 - s0 - 1))
                        kb.op('dve', lambda e: e.tensor_tensor_scan(out=rev(hb, 0, TC), data0=rev(aa, 0, TC), data1=rev(u, 0, TC),
                                                                    initial=0.0, op0=ALU.mult, op1=ALU.add),
                              reads=["r_a", "r_u"], writes=["r_hb"])
                        kb.op('dve', lambda e: e.tensor_tensor_scan(out=rev(hb, TC, TS), data0=rev(aa, TC, TS), data1=rev(u, TC, TS),
                                                                    initial=hb[:, 0:1], op0=ALU.mult, op1=ALU.add),
                              reads=["r_a", "r_u", "r_hb"], writes=["r_hb"])
                kb.op('pool', lambda e: e.tensor_tensor(out=hf[:], in0=hf[:], in1=hb[:], op=ALU.add), reads=["r_hf", "r_hb"], writes=["r_hf"])
                gelu_tanh(kb, u[:], gt[:], tmp[:], "r_gt", "r_tmp", "r_u")
                kb.op('dve', lambda e: e.tensor_tensor(out=yb[:], in0=u[:], in1=hf[:], op=ALU.mult), reads=["r_u", "r_hf"], writes=["r_yb"])
                kb.dma(g.ybT[ct * 128:(ct + 1) * 128, c0:c0 + W], yb[:], reads=["r_yb"], writes=["ybT"])


def load_gate_rows(kb, g, l, idx, tag):
    mg = kb.sb(f"{tag}_mg", [128, 3, D], F32)
    for c in range(3):
        src = AP(g.mrow.tensor, g.mrow.offset + (l * 3 + c) * 6144 + idx * D, [[0, 128], [1, D]])
        kb.dma(mg[:, c, :], src, reads=["mrow"], writes=[f"{tag}_mg"])
    return mg


class OutProj:
    def __init__(self, kb, g, l, w_dram, gate_idx, tag):
        self.kb, self.g, self.tag = kb, g, tag
        self.w = kb.sb(f"{tag}_w", [128, 8, D], BF16)
        load_weight_bf(kb, self.w, w_dram, D, D, f"{tag}_w")
        self.mg = load_gate_rows(kb, g, l, gate_idx, tag)
        self.ps = [kb.ps(f"{tag}_ps{i}", [128, 512], F32) for i in range(2)]
        self.xt = [kb.sb(f"{tag}_xt{i}", [128, D], F32) for i in range(2)]
        self.n = 0

    def run(self, yT, ky, blk, src, dst, tiles=None):
        kb, tag = self.kb, self.tag
        for tt in range(4):
            t = blk * 4 + tt if tiles is None else tiles[tt]
            i = self.n % 2
            self.n += 1
            xt = self.xt[i]
            kb.dma(xt[:], src[t * 128:(t + 1) * 128, :], reads=["stream"], writes=[f"{tag}_xt{i}"])
            col = mod_col(t)
            for half in range(2):
                for k in range(8):
                    kb.op('pe', lambda e: e.matmul(self.ps[half][:], lhsT=yT[:, k, tt * 128:(tt + 1) * 128],
                                                   rhs=self.w[:, k, half * 512:(half + 1) * 512], start=(k == 0), stop=(k == 7)),
                          reads=[ky, f"{tag}_w"], writes=[f"{tag}_ps{half}"])
            tmpk = f"{tag}_tmp{i}"
            if not hasattr(self, "tmp"):
                self.tmp = [kb.sb(f"{tag}_tmp{j}", [128, D], F32) for j in range(2)]
            tmp = self.tmp[i]
            for half in range(2):
                hs = slice(half * 512, (half + 1) * 512)
                kb.op('dve', lambda e: e.tensor_tensor(out=tmp[:, hs], in0=self.ps[half][:], in1=self.mg[:, col, hs], op=ALU.mult),
                      reads=[f"{tag}_ps{half}", f"{tag}_mg"], writes=[tmpk])
            kb.op('pool', lambda e: e.tensor_tensor(out=tmp[:], in0=tmp[:], in1=xt[:], op=ALU.add),
                  reads=[tmpk, f"{tag}_xt{i}"], writes=[tmpk])
            kb.dma(dst[t * 128:(t + 1) * 128, :], tmp[:], reads=[tmpk], writes=["stream_out"])


def stage_e4(kb, g, l, src, dst):
    with kb.stage() as st:
        op = OutProj(kb, g, l, g.e_w_out, 2, "e4")
        ones = kb.sb("e4_ones", [128, 128], F32)
        kb.op('pool', lambda e: e.memset(ones[:], 1.0 / 128.0), writes=["e4_ones"])
        gain = kb.sb("e4_gain", [128, 1], F32)
        kb.dma(gain[:], g.hg_gain[:], writes=["e4_gain"])
        o_ = [kb.sb(f"e4_o{i}", [128, 4, 512], F32) for i in range(2)]
        gg_ = [kb.sb(f"e4_g{i}", [128, 4, 512], F32) for i in range(2)]
        sq_ = [kb.sb(f"e4_sq{i}", [128, 4, 512], F32) for i in range(2)]
        rs_ = [kb.sb(f"e4_rs{i}", [128, 512], F32) for i in range(2)]
        yT = [kb.sb(f"e4_yT{i}", [128, 8, 512], BF16) for i in range(2)]
        sps = [kb.ps(f"e4_sps{i}", [128, 512], F32) for i in range(2)]

        def merge(blk):
            bi = blk % 2
            o, gg, sq, rs = o_[bi], gg_[bi], sq_[bi], rs_[bi]
            KO, KG, KSQ, KRS = f"e4_o{bi}", f"e4_g{bi}", f"e4_sq{bi}", f"e4_rs{bi}"
            cs = slice(blk * 512, (blk + 1) * 512)
            kb.dma(o[:], g.oT[:, cs].rearrange("(h p) t -> p h t", p=128), reads=["oT"], writes=[KO])
            kb.dma(gg[:], g.fm[12 * 128:16 * 128, cs].rearrange("(h p) t -> p h t", p=128), reads=["fm"], writes=[KG])
            kb.dma(yT[bi][:, 4:8, :], g.ybT[:, cs].rearrange("(h p) t -> p h t", p=128), reads=["ybT"], writes=[f"e4_yT{bi}"])
            kb.op('act', lambda e: e.activation(out=sq[:], in_=o[:], func=AF.Square), reads=[KO], writes=[KSQ])
            for h in range(4):
                p = h % 2
                kb.op('pe', lambda e: e.matmul(sps[p][:], lhsT=ones[:], rhs=sq[:, h, :], start=True, stop=True),
                      reads=["e4_ones", KSQ], writes=[f"e4_sps{p}"])
                kb.op('dve', lambda e: e.tensor_scalar(out=rs[:], in0=sps[p][:], scalar1=EPS, scalar2=None, op0=ALU.add),
                      reads=[f"e4_sps{p}"], writes=[KRS])
                kb.op('act', lambda e: e.activation(out=rs[:], in_=rs[:], func=AF.Sqrt), reads=[KRS], writes=[KRS])
                kb.op('dve', lambda e: e.reciprocal(out=rs[:], in_=rs[:]), reads=[KRS], writes=[KRS])
                kb.op('dve', lambda e: e.tensor_tensor(out=rs[:], in0=rs[:], in1=o[:, h, :], op=ALU.mult),
                      reads=[KRS, KO], writes=[KRS])
                kb.op('dve', lambda e: e.scalar_tensor_tensor(out=yT[bi][:, h, :], in0=rs[:], scalar=gain[:, 0:1], in1=gg[:, h, :],
                                                             op0=ALU.mult, op1=ALU.mult),
                      reads=[KRS, "e4_gain", KG], writes=[f"e4_yT{bi}"])

        NBLK = NT // 512
        merge(0)
        for blk in range(NBLK):
            if blk + 1 < NBLK:
                merge(blk + 1)
            op.run(yT[blk % 2], f"e4_yT{blk % 2}", blk, src, dst)


class UVConv:
    R = 2

    def __init__(self, kb, g, l):
        self.kb, self.g, self.l = kb, g, l
        R = self.R
        self.ld = [[kb.sb(f"uv_ld{j}{i}", [128, R, D], F32) for i in range(2)] for j in range(2)]
        self.ob = [kb.sb(f"uv_ob{i}", [128, R, 2 * D], BF16) for i in range(2)]
        self.tabs = (g.p_u0, g.p_v0) if l == 0 else (g.p_u1, g.p_v1)
        self.dst = g.uv0 if l == 0 else g.uv1
        self.nit = 128 // R
        self.loaded = 0
        self.done = 0
        self._load()

    def _load(self):
        if self.loaded >= self.nit:
            return
        kb, R = self.kb, self.R
        i = self.loaded % 2
        r0 = self.loaded * R
        for j in range(2):
            srcp = AP(self.tabs[j].tensor, self.tabs[j].offset + r0 * D, [[128 * D, 128], [D, R], [1, D]])
            kb.dma(self.ld[j][i][:], srcp, writes=[f"uv_ld{j}{i}"])
        self.loaded += 1

    def step(self):
        if self.done >= self.nit:
            return
        kb, R = self.kb, self.R
        self._load()
        i = self.done % 2
        r0 = self.done * R
        for j in range(2):
            kb.op('pool', lambda e: e.tensor_copy(out=self.ob[i][:, :, j * D:(j + 1) * D], in_=self.ld[j][i][:]),
                  reads=[f"uv_ld{j}{i}"], writes=[f"uv_ob{i}"])
        dstp = AP(self.dst.tensor, self.dst.offset + r0 * 2 * D, [[128 * 2 * D, 128], [2 * D, R], [1, 2 * D]])
        kb.dma(dstp, self.ob[i][:], reads=[f"uv_ob{i}"], writes=["uv"])
        self.done += 1

    def finish(self):
        while self.done < self.nit:
            self.step()


def bg_step(g, n=1):
    bg = getattr(g, "bg", None)
    if bg is not None:
        for _ in range(n):
            bg.step()


def stage_peer(kb, g, l, src, dst, tiles, dst_row):
    GS = 8
    NGRP = 128 // GS
    with kb.stage() as st:
        wq = kb.sb("p_wq", [128, 8, 2048], BF16)
        with kb.stage() as st2:
            load_weight_bf(kb, wq, g.p_wq[l], D, 2048, "p_wq", pieces=256)
        uvt = g.uv0 if l == 0 else g.uv1
        keysT = kb.sb("p_keys", [128, 16, 128], F32)
        kb.dma(keysT[:], g.keysT[l], writes=["p_keys"])
        bq = kb.sb("p_bq", [128, 16], F32)
        kb.dma(bq[:], g.bqc[l], writes=["p_bq"])
        nt = NormT(kb, g, "pn", nx=2)
        hT = [kb.sb(f"p_hT{i}", [128, 8, 128], BF16) for i in range(2)]
        qT = kb.sb("p_qT", [128, 16, 128], F32)
        sc2 = kb.sb("p_sc2", [128, 128], F32)
        sv = kb.sb("p_sv", [128, 16, 16], F32)
        si = kb.sb("p_si", [128, 16, 16], U32)
        sif = kb.sb("p_sif", [128, 16, 16], F32)
        cand = kb.sb("p_cand", [128, 8, NCAND], F32)
        cand2 = kb.sb("p_cand2", [128, NCAND], F32)
        cidx = kb.sb("p_cidx", [128, 8, NCAND], F32)
        junk = kb.sb("p_junk", [128, NCAND], F32)
        best = kb.sb("p_best", [128, 8, 16], F32)
        eif = kb.sb("p_eif", [128, 128], F32)
        eidx = [kb.sb(f"p_eidx{i}", [128, 128], I32) for i in range(2)]
        gate = [kb.sb(f"p_gate{i}", [128, 8, 16], F32) for i in range(2)]
        gsum = kb.sb("p_gsum", [128, 8], F32)
        dots = [kb.sb(f"p_dots{i}", [128, 128], F32) for i in range(2)]
        actv = kb.sb("p_act", [128, 128], F32)
        gtmp = kb.sb("p_gtmp", [128, 128], F32)
        abrow = kb.sb("p_abrow", [128, 2, D], F32)
        m5 = [kb.sb(f"p_m5{i}", [128, D], F32) for i in range(2)]
        grow = kb.sb("p_grow", [128, D], F32)
        xtok = [kb.sb(f"p_xtok{i}", [128, D], F32) for i in range(2)]
        bigj = kb.sb("p_bigj", [128, D], BF16)
        acc = kb.sb("p_acc", [128, D], F32)
        NG = 22
        gb = [kb.sb(f"p_gb{i}", [128, 2 * D], BF16) for i in range(NG)]
        NDG = 8
        dg = [kb.sb(f"p_dg{i}", [128, 128], BF16) for i in range(NDG)]
        qps = kb.ps("p_qps", [128, 512], F32)
        scps = [kb.ps(f"p_scps{i}", [128, 512], F32) for i in range(2)]
        vps = [kb.ps(f"p_vps{i}", [128, 512], F32) for i in range(2)]
        src_g = AP(g.gffn.tensor, g.gffn.offset + l * D, [[0, 128], [1, D]])
        kb.dma(grow[:], src_g, writes=["p_grow"])
        state = {"col": None, "epoch": -1, "ngb": 0, "ndg": 0}
        info = {}
        slot_buf = {}

        def routing(n):
            t = tiles[n]
            pi = n % 2
            col = mod_col(t)
            if col != state["col"]:
                state["col"] = col
                state["epoch"] += 1
                ep = state["epoch"] % 2
                for j, idx in enumerate((4, 3)):
                    srcm = AP(g.mrow.tensor, g.mrow.offset + (l * 3 + col) * 6144 + idx * D, [[0, 128], [1, D]])
                    kb.dma(abrow[:, j, :], srcm, reads=["mrow"], writes=["p_abrow"])
                srcm = AP(g.mrow.tensor, g.mrow.offset + (l * 3 + col) * 6144 + 5 * D, [[0, 128], [1, D]])
                kb.dma(m5[ep][:], srcm, reads=["mrow"], writes=[f"p_m5{ep}"])
                kb.op('dve', lambda e: e.scalar_tensor_tensor(out=abrow[:, 0, :], in0=abrow[:, 0, :], scalar=1.0, in1=grow[:],
                                                             op0=ALU.add, op1=ALU.mult), reads=["p_abrow", "p_grow"], writes=["p_abrow"])
            ep = state["epoch"] % 2
            xt, kx, ss, ks = nt.run(src[t * 128:(t + 1) * 128, :], l, 1, col, hT[pi][:], f"p_hT{pi}")
            info[n] = (xt, kx, ep)
            kxt = f"p_xtok{pi}"
            kb.op('dve', lambda e: e.scalar_tensor_tensor(out=xtok[pi][:], in0=xt[:], scalar=ss[:, 1:2], in1=abrow[:, 0, :],
                                                         op0=ALU.mult, op1=ALU.mult), reads=[kx, ks, "p_abrow"], writes=[kxt])
            kb.op('dve', lambda e: e.tensor_tensor(out=xtok[pi][:], in0=xtok[pi][:], in1=abrow[:, 1, :], op=ALU.add),
                  reads=[kxt, "p_abrow"], writes=[kxt])
            yield
            for hp4 in range(4):
                for j in range(4):
                    hp = hp4 * 4 + j
                    for k in range(8):
                        kb.op('pe', lambda e: e.matmul(qps[:, j * 128:(j + 1) * 128], lhsT=wq[:, k, hp * 128:(hp + 1) * 128],
                                                       rhs=hT[pi][:, k, :], start=(k == 0), stop=(k == 7)),
                              reads=["p_wq", f"p_hT{pi}"], writes=["p_qps"])
                for j in range(4):
                    hp = hp4 * 4 + j
                    kb.op('act', lambda e: e.activation(out=qT[:, hp, :], in_=qps[:, j * 128:(j + 1) * 128], func=AF.Identity,
                                                        bias=bq[:, hp:hp + 1]), reads=["p_qps", "p_bq"], writes=["p_qT"])
                yield
            for hp4 in range(4):
                sp = hp4 % 2
                for j in range(4):
                    hp = hp4 * 4 + j
                    kb.op('pe', lambda e: e.matmul(scps[sp][:, j * 128:(j + 1) * 128], lhsT=qT[:, hp, :], rhs=keysT[:, hp, :],
                                                   start=True, stop=True), reads=["p_qT", "p_keys"], writes=[f"p_scps{sp}"])
                for j in range(4):
                    hp = hp4 * 4 + j
                    scv = scps[sp][:, j * 128:(j + 1) * 128]
                    ksp = f"p_scps{sp}"
                    kb.op('dve', lambda e: e.max(out=sv[:, hp, 0:8], in_=scv), reads=[ksp], writes=["p_sv"])
                    kb.op('dve', lambda e: e.max_index(out=si[:, hp, 0:8], in_max=sv[:, hp, 0:8], in_values=scv),
                          reads=[ksp, "p_sv"], writes=["p_si"])
                    kb.op('dve', lambda e: e.match_replace(out=sc2[:], in_to_replace=sv[:, hp, 0:8], in_values=scv, imm_value=-1e30),
                          reads=[ksp, "p_sv"], writes=["p_sc2"])
                    kb.op('dve', lambda e: e.max(out=sv[:, hp, 8:16], in_=sc2[:]), reads=["p_sc2"], writes=["p_sv"])
                    kb.op('dve', lambda e: e.max_index(out=si[:, hp, 8:16], in_max=sv[:, hp, 8:16], in_values=sc2[:]),
                          reads=["p_sc2", "p_sv"], writes=["p_si"])
                    if j % 2 == 1:
                        yield
            kb.op('dve', lambda e: e.tensor_copy(out=sif[:], in_=si[:]), reads=["p_si"], writes=["p_sif"])
            sfa = sif[:].rearrange("p (h two) k -> p h two k", two=2)[:, :, 0, :]
            kb.op('dve', lambda e: e.tensor_scalar(out=sfa, in0=sfa, scalar1=128.0, scalar2=None, op0=ALU.mult), reads=["p_sif"], writes=["p_sif"])
            yield
            sv4 = sv[:].rearrange("p (h two) k -> p h two k", two=2)
            sf4 = sif[:].rearrange("p (h two) k -> p h two k", two=2)
            off = 0
            for (i0, ni, ln) in STAIR:
                cseg = cand[:, :, off:off + ni * ln].rearrange("p h (i j) -> p h i j", i=ni)
                xseg = cidx[:, :, off:off + ni * ln].rearrange("p h (i j) -> p h i j", i=ni)
                a0 = sv4[:, :, 0, i0:i0 + ni]
                a1 = sv4[:, :, 1, 0:ln]
                f0 = sf4[:, :, 0, i0:i0 + ni]
                f1 = sf4[:, :, 1, 0:ln]
                a0b = bc(a0, [a0.ap[0], a0.ap[1], a0.ap[2], [0, ln]])
                a1b = bc(a1, [a1.ap[0], a1.ap[1], [0, ni], a1.ap[2]])
                f0b = bc(f0, [f0.ap[0], f0.ap[1], f0.ap[2], [0, ln]])
                f1b = bc(f1, [f1.ap[0], f1.ap[1], [0, ni], f1.ap[2]])
                kb.op('dve', lambda e: e.tensor_tensor(out=cseg, in0=a0b, in1=a1b, op=ALU.add), reads=["p_sv"], writes=["p_cand"])
                kb.op('dve', lambda e: e.tensor_tensor(out=xseg, in0=f0b, in1=f1b, op=ALU.add), reads=["p_sif"], writes=["p_cidx"])
                off += ni * ln
            yield
            for h in range(8):
                kb.op('dve', lambda e: e.max(out=best[:, h, 0:8], in_=cand[:, h, :]), reads=["p_cand"], writes=["p_best"])
                kb.op('dve', lambda e: e.match_replace(out=cand2[:], in_to_replace=best[:, h, 0:8], in_values=cand[:, h, :], imm_value=-1e30),
                      reads=["p_cand", "p_best"], writes=["p_cand2"])
                kb.op('dve', lambda e: e.max(out=best[:, h, 8:16], in_=cand2[:]), reads=["p_cand2"], writes=["p_best"])
                for k in range(16):
                    kb.op('dve', lambda e: e.scalar_tensor_tensor(out=junk[:], in0=cand[:, h, :], scalar=best[:, h, k:k + 1], in1=cidx[:, h, :],
                                                                 op0=ALU.is_equal, op1=ALU.mult, accum_out=eif[:, h * 16 + k:h * 16 + k + 1]),
                          reads=["p_cand", "p_best", "p_cidx"] + (["p_eif"] if (h == 0 and k == 0) else []),
                          writes=(["p_eif"] if (k == 15 or (h == 0 and k == 0)) else []))
                yield
            kb.op('dve', lambda e: e.tensor_scalar(out=eif[:], in0=eif[:], scalar1=16383.0, scalar2=0.0, op0=ALU.min, op1=ALU.max),
                  reads=["p_eif"], writes=["p_eif"])
            kb.op('dve', lambda e: e.tensor_copy(out=eidx[pi][:], in_=eif[:]), reads=["p_eif"], writes=[f"p_eidx{pi}"])
            gt_ = gate[pi]
            kg = f"p_gate{pi}"
            bm = best[:, :, 0:1]
            kb.op('dve', lambda e: e.tensor_tensor(out=gt_[:], in0=best[:], in1=bc(bm, [bm.ap[0], bm.ap[1], [0, 16]]), op=ALU.subtract),
                  reads=["p_best"], writes=[kg])
            kb.op('act', lambda e: e.activation(out=gt_[:], in_=gt_[:], func=AF.Exp), reads=[kg], writes=[kg])
            kb.op('dve', lambda e: e.tensor_reduce(out=gsum[:], in_=gt_[:], axis=AX.X, op=ALU.add), reads=[kg], writes=["p_gsum"])
            kb.op('dve', lambda e: e.reciprocal(out=gsum[:], in_=gsum[:]), reads=["p_gsum"], writes=["p_gsum"])
            gs = gsum[:]
            kb.op('dve', lambda e: e.tensor_tensor(out=gt_[:], in0=gt_[:], in1=bc(gs, [gs.ap[0], gs.ap[1], [0, 16]]), op=ALU.mult),
                  reads=[kg, "p_gsum"], writes=[kg])
            yield

        NYIELD = 1 + 4 + 8 + 1 + 1 + 8 + 1

        def ggroup(n, grp):
            pi = n % 2
            for j in range(GS):
                k = grp * GS + j
                bi = state["ngb"] % NG
                state["ngb"] += 1
                slot_buf[(n, k)] = bi
                kb.dma(None, None, reads=[f"p_eidx{pi}"], writes=[f"p_gb{bi}"], q='pool',
                       fn=lambda e: e.indirect_dma_start(out=gb[bi][:], out_offset=None, in_=uvt,
                                                         in_offset=bass.IndirectOffsetOnAxis(ap=eidx[pi][:, k:k + 1], axis=0)))

        def dgroup(n, grp):
            pi = n % 2
            kd = f"p_dots{pi}_{grp}"
            for j in range(GS):
                k = grp * GS + j
                bi = slot_buf[(n, k)]
                kb.op('dve', lambda e: e.scalar_tensor_tensor(out=bigj[:], in0=gb[bi][:, 0:D], scalar=1.0, in1=xtok[pi][:], op0=ALU.mult, op1=ALU.mult,
                                                             accum_out=dots[pi][:, k:k + 1]),
                      reads=[f"p_gb{bi}", f"p_xtok{pi}"] + ([kd] if j == 0 else []), writes=([kd] if j in (0, GS - 1) else []))

        def vgroup(n, grp):
            pi = n % 2
            kd = f"p_dots{pi}_{grp}"
            cs = slice(grp * GS, (grp + 1) * GS)
            ka = f"p_act_{grp % 2}"
            kt = f"p_gtmp_{grp % 2}"
            gelu_tanh(kb, actv[:, cs], dots[pi][:, cs], gtmp[:, cs], kd, kt, ka, eng2='dve')
            gfl = gate[pi][:].rearrange("p h k -> p (h k)")
            kb.op('dve', lambda e: e.tensor_tensor(out=actv[:, cs], in0=actv[:, cs], in1=gfl[:, cs], op=ALU.mult),
                  reads=[ka, f"p_gate{pi}"], writes=[ka])
            for j in range(GS):
                k = grp * GS + j
                bi = slot_buf.pop((n, k))
                di = state["ndg"] % NDG
                state["ndg"] += 1
                kb.op('act', lambda e: e.activation(out=dg[di][:], in_=g.ident[:], func=AF.Identity, scale=actv[:, k:k + 1]),
                      reads=["ident", ka], writes=[f"p_dg{di}"])
                for half in range(2):
                    kb.op('pe', lambda e: e.matmul(vps[half][:], lhsT=dg[di][:], rhs=gb[bi][:, D + half * 512:D + (half + 1) * 512],
                                                   start=(k == 0), stop=(k == 127)),
                          reads=[f"p_dg{di}", f"p_gb{bi}"], writes=[f"p_vps{half}"])

        def vfin(n):
            t = tiles[n]
            xt, kx, ep = info.pop(n)
            for half in range(2):
                hs = slice(half * 512, (half + 1) * 512)
                kb.op('dve', lambda e: e.tensor_tensor(out=acc[:, hs], in0=vps[half][:], in1=m5[ep][:, hs], op=ALU.mult),
                      reads=[f"p_vps{half}", f"p_m5{ep}"], writes=["p_acc"])
            kb.op('dve', lambda e: e.tensor_tensor(out=acc[:], in0=acc[:], in1=xt[:], op=ALU.add), reads=["p_acc", kx], writes=["p_acc"])
            r0 = dst_row(t)
            g.out_toks.append(kb.dma(dst[r0:r0 + 128, :], acc[:], reads=["p_acc"], writes=["stream_out"]))

        NTL = len(tiles)
        for _ in routing(0):
            pass
        ggroup(0, 0)
        dgroup(0, 0)
        per = -(-NYIELD // (NGRP - 2))
        for n in range(NTL):
            rgen = routing(n + 1) if n + 1 < NTL else None
            for grp in range(NGRP):
                nxt = None
                if grp + 1 < NGRP:
                    nxt = (n, grp + 1)
                else:
                    if rgen is not None:
                        for _ in rgen:
                            pass
                        rgen = None
                    if n + 1 < NTL:
                        nxt = (n + 1, 0)
                if nxt is not None:
                    ggroup(*nxt)
                vgroup(n, grp)
                if nxt is not None:
                    dgroup(*nxt)
                if rgen is not None:
                    for _ in range(per):
                        if next(rgen, "done") == "done":
                            rgen = None
                            break
            vfin(n)


LT = TL // 128
ST_ = TS // 128


def stage_o1(kb, g, l, src):
    with kb.stage() as st:
        wbf = kb.sb("o1_w", [128, 8, 1536], BF16)
        load_weight_bf(kb, wbf, g.o_w_in, D, 1536, "o1_w")
        nt = NormT(kb, g, "o1n")
        hT = [kb.sb(f"o1_hT{i}", [128, 8, 128], BF16) for i in range(2)]
        gq = kb.sb("o1_gq", [128, 16, 64], F32)
        for hd in range(16):
            srcg = g.q_gain if hd < 12 else g.k_gain
            kb.dma(gq[:, hd, :], bc(srcg, [[0, 128], [1, 64]]), writes=["o1_gq"])
        cs_t = [kb.sb(f"o1_cs{i}", [128, 2, 32], F32) for i in range(2)]
        ps_sh = kb.ps("o1_pssh", [128, 512], F32)
        pss2 = [[kb.ps(f"o1_ps{i}_{j}", [128, 512], F32) for j in range(2)] + [ps_sh] for i in range(2)]
        tps1 = kb.ps("o1_tp", [128, 1024], BF16)
        tps = [tps1, tps1]
        qk_ = [kb.sb(f"o1_qk{i}", [128, 16, 64], F32) for i in range(2)]
        sq_ = [kb.sb(f"o1_sq{i}", [128, 16, 64], F32) for i in range(2)]
        ss_ = [kb.sb(f"o1_ss{i}", [128, 16], F32) for i in range(2)]
        t1_ = [kb.sb(f"o1_t1{i}", [128, 16, 32], F32) for i in range(2)]
        t2_ = [kb.sb(f"o1_t2{i}", [128, 16, 32], F32) for i in range(2)]
        qr_ = [kb.sb(f"o1_qr{i}", [128, 16, 64], BF16) for i in range(2)]
        qrT = [kb.sb(f"o1_qrT{i}", [64, 16, 128], BF16) for i in range(2)]
        vz = [kb.sb(f"o1_vz{i}", [128, 512], BF16) for i in range(2)]
        def phase_a(t):
            i = t % 2
            pss = pss2[i]
            PK = f"o1_ps{i}_"
            PKN = [PK + "0", PK + "1", "o1_pssh"]
            qk = qk_[i]
            KQ = f"o1_qk{i}"
            nt.run(src[t * 128:(t + 1) * 128, :], l, 0, mod_col(t), hT[i][:], f"o1_hT{i}")
            bg_step(g)
            kb.dma(cs_t[i][:], g.rope[(t % ST_) * 128:(t % ST_ + 1) * 128], writes=[f"o1_cs{i}"])
            for bnk in range(3):
                for k in range(8):
                    kb.op('pe', lambda e: e.matmul(pss[bnk][:], lhsT=hT[i][:, k, :], rhs=wbf[:, k, bnk * 512:(bnk + 1) * 512],
                                                   start=(k == 0), stop=(k == 7)), reads=[f"o1_hT{i}", "o1_w"], writes=[PKN[bnk]])
            qkf = qk[:].rearrange("p h d -> p (h d)")
            kb.op('act', lambda e: e.activation(out=qkf[:, 0:512], in_=pss[0][:], func=AF.Copy), reads=[PK + "0"], writes=[KQ])
            kb.op('act', lambda e: e.activation(out=qkf[:, 512:1024], in_=pss[1][:], func=AF.Copy), reads=[PK + "1"], writes=[KQ])
            kb.op('act', lambda e: e.activation(out=vz[i][:], in_=pss[2][:], func=AF.Copy), reads=["o1_pssh"], writes=[f"o1_vz{i}"])
            kb.dma(g.vtok2[t * 128:(t + 1) * 128, :], vz[i][:, 0:256], reads=[f"o1_vz{i}"], writes=["vtok2"])
            kb.dma(g.ztok[t * 128:(t + 1) * 128, :], vz[i][:, 256:512], reads=[f"o1_vz{i}"], writes=["ztok"])

        def phase_b(t):
            i = t % 2
            qk, sq, ss, t1, t2, qr = qk_[i], sq_[i], ss_[i], t1_[i], t2_[i], qr_[i]
            KQ, KS, KSS, KT1, KT2, KQR = f"o1_qk{i}", f"o1_sq{i}", f"o1_ss{i}", f"o1_t1{i}", f"o1_t2{i}", f"o1_qr{i}"
            kb.op('dve', lambda e: e.tensor_tensor(out=sq[:], in0=qk[:], in1=qk[:], op=ALU.mult), reads=[KQ], writes=[KS])
            kb.op('dve', lambda e: e.tensor_reduce(out=ss[:], in_=sq[:], axis=AX.X, op=ALU.add), reads=[KS], writes=[KSS])
            kb.op('dve', lambda e: e.tensor_scalar(out=ss[:], in0=ss[:], scalar1=1.0 / 64.0, scalar2=EPS, op0=ALU.mult, op1=ALU.add),
                  reads=[KSS], writes=[KSS])
            kb.op('act', lambda e: e.activation(out=ss[:], in_=ss[:], func=AF.Sqrt), reads=[KSS], writes=[KSS])
            kb.op('dve', lambda e: e.reciprocal(out=ss[:], in_=ss[:]), reads=[KSS], writes=[KSS])
            ssv = ss[:]
            kb.op('dve', lambda e: e.tensor_tensor(out=qk[:], in0=qk[:], in1=bc(ssv, [ssv.ap[0], ssv.ap[1], [0, 64]]), op=ALU.mult),
                  reads=[KQ, KSS], writes=[KQ])
            kb.op('dve', lambda e: e.tensor_tensor(out=qk[:], in0=qk[:], in1=gq[:], op=ALU.mult), reads=[KQ, "o1_gq"], writes=[KQ])
            cosv = cs_t[i][:, 0, :]
            sinv = cs_t[i][:, 1, :]
            cb_ = bc(cosv, [cosv.ap[0], [0, 16], [1, 32]])
            sb_ = bc(sinv, [sinv.ap[0], [0, 16], [1, 32]])
            x1 = qk[:, :, 0:32]
            x2 = qk[:, :, 32:64]
            kc = f"o1_cs{i}"
            kb.op('dve', lambda e: e.tensor_tensor(out=t1[:], in0=x1, in1=cb_, op=ALU.mult), reads=[KQ, kc], writes=[KT1])
            kb.op('dve', lambda e: e.tensor_tensor(out=t2[:], in0=x2, in1=sb_, op=ALU.mult), reads=[KQ, kc], writes=[KT2])
            kb.op('dve', lambda e: e.tensor_tensor(out=qr[:, :, 0:32], in0=t1[:], in1=t2[:], op=ALU.subtract),
                  reads=[KT1, KT2], writes=[KQR])
            kb.op('dve', lambda e: e.tensor_tensor(out=t1[:], in0=x2, in1=cb_, op=ALU.mult), reads=[KQ, kc, KQR], writes=[KT1])
            kb.op('dve', lambda e: e.tensor_tensor(out=t2[:], in0=x1, in1=sb_, op=ALU.mult), reads=[KQ, kc, KQR], writes=[KT2])
            kb.op('dve', lambda e: e.tensor_tensor(out=qr[:, :, 32:64], in0=t1[:], in1=t2[:], op=ALU.add),
                  reads=[KT1, KT2], writes=[KQR])
            for bnk in range(2):
                for hd in range(bnk * 8, bnk * 8 + 8):
                    kb.op('pe', lambda e: e.transpose(out=tps1[0:64, (hd % 8) * 128:(hd % 8 + 1) * 128], in_=qr[:, hd, :], identity=g.ident[:]),
                          reads=[KQR, "ident"], writes=["o1_tp"])
                tv = tps1[0:64, :].rearrange("p (h t) -> p h t", h=8)
                if bnk == 0:
                    kb.op('act', lambda e: e.activation(out=qrT[i][:, 0:8, :], in_=tv, func=AF.Copy), reads=["o1_tp"], writes=[f"o1_qrT{i}"])
                else:
                    kb.op('dve', lambda e: e.tensor_copy(out=qrT[i][:, 8:16, :], in_=tv), reads=["o1_tp"], writes=[f"o1_qrT{i}"])
            dstT = AP(g.qkT.tensor, g.qkT.offset + t * 128, [[NT, 64], [64 * NT, 16], [1, 128]])
            kb.dma(dstT, qrT[i][:], reads=[f"o1_qrT{i}"], writes=["qkT"])

        phase_a(0)
        for t in range(NTILE):
            if t + 1 < NTILE:
                phase_a(t + 1)
            phase_b(t)


def stage_o2(kb, g):
    with kb.stage() as st:
        kT = kb.sb("a_kT", [64, TS], BF16)
        qT = kb.sb("a_qT", [64, 3, TL], BF16)
        V = kb.sb("a_V", [128, ST_, 64], BF16)
        aT = kb.sb("a_aT", [64, 3, TL], BF16)
        ones = kb.sb("a_ones", [128, 64], BF16)
        kb.op('pool', lambda e: e.memset(ones[:], 1.0), writes=["a_ones"])
        am32 = kb.sb("a_m32", [128, 2, 128], F32)
        am = kb.sb("a_m", [128, 2, 128], BF16)
        kb.dma(am32[:], g.amask[:], writes=["a_m32"])
        kb.op('dve', lambda e: e.tensor_copy(out=am[:], in_=am32[:]), reads=["a_m32"], writes=["a_m"])
        es = kb.sb("a_es", [64, 12], F32)
        kb.dma(es[:], bc(g.sinks, [[0, 64], [1, 12]]), writes=["a_es"])
        kb.op('act', lambda e: e.activation(out=es[:], in_=es[:], func=AF.Exp), reads=["a_es"], writes=["a_es"])
        P = [kb.sb(f"a_P{i}", [128, 3, 128], BF16) for i in range(4)]
        den = kb.sb("a_den", [64, 3, 128], F32)
        s_ps = [kb.ps(f"a_sps{i}", [128, 512], F32) for i in range(3)]
        o_ps = [kb.ps(f"a_ops{i}", [128, 512], F32) for i in range(2)]
        d_ps = [kb.ps(f"a_dps{i}", [128, 512], F32) for i in range(2)]
        nS = 0
        nP = 0
        nG = 0
        for b in range(NB):
            c0 = b * TS
            for kvh in range(4):
                kb.dma(kT[:], g.qkT[12 + kvh, :, c0:c0 + TS], reads=["qkT"], writes=["a_kT"])
                for gi in range(3):
                    kb.dma(qT[:, gi, :], g.qkT[kvh * 3 + gi, :, c0 + TC:c0 + TS], reads=["qkT"], writes=["a_qT"])
                vsrc = AP(g.vtok2.tensor, g.vtok2.offset + c0 * 256 + kvh * 64, [[256, 128], [128 * 256, ST_], [1, 64]])
                kb.dma(V[:], vsrc, reads=["vtok2"], writes=["a_V"])
                for n in range(LT):
                    if n % 4 == 1:
                        bg_step(g)
                    og = nG % 2
                    nG += 1
                    kts = [(0, None), (1, None)]
                    for m in (n - 1, n, n + 1):
                        if 0 <= m < LT:
                            kts.append((2 + m, 0 if m == n - 1 else (1 if m == n + 1 else None)))
                    for ki, (kt, mk) in enumerate(kts):
                        sp = nS % 3
                        nS += 1
                        pp = nP % 4
                        nP += 1
                        kb.op('pe', lambda e: e.matmul(s_ps[sp][:, 0:384], lhsT=kT[:, kt * 128:(kt + 1) * 128],
                                                       rhs=qT[:, :, n * 128:(n + 1) * 128], start=True, stop=True),
                              reads=["a_kT", "a_qT"], writes=[f"a_sps{sp}"])
                        kb.op('act', lambda e: e.activation(out=P[pp][:].rearrange("p g t -> p (g t)"), in_=s_ps[sp][:, 0:384],
                                                            func=AF.Exp, scale=0.125), reads=[f"a_sps{sp}"], writes=[f"a_P{pp}"])
                        if mk is not None:
                            mv = am[:, mk, :]
                            kb.op('dve', lambda e: e.tensor_tensor(out=P[pp][:], in0=P[pp][:], in1=bc(mv, [mv.ap[0], [0, 3], [1, 128]]),
                                                                    op=ALU.mult), reads=[f"a_P{pp}", "a_m"], writes=[f"a_P{pp}"])
                        first = ki == 0
                        lastk = ki == len(kts) - 1
                        Pf = P[pp][:].rearrange("p g t -> p (g t)")
                        kb.op('pe', lambda e: e.matmul(o_ps[og][0:64, 0:384], lhsT=V[:, kt, :], rhs=Pf, start=first, stop=lastk),
                              reads=["a_V", f"a_P{pp}"], writes=[f"a_ops{og}"])
                        kb.op('pe', lambda e: e.matmul(d_ps[og][0:64, 0:384], lhsT=ones[:], rhs=Pf, start=first, stop=lastk),
                              reads=["a_ones", f"a_P{pp}"], writes=[f"a_dps{og}"])
                    esv = es[:, kvh * 3:kvh * 3 + 3]
                    kb.op('dve', lambda e: e.tensor_tensor(out=den[:], in0=d_ps[og][0:64, 0:384].rearrange("p (g t) -> p g t", g=3),
                                                           in1=bc(esv, [esv.ap[0], [1, 3], [0, 128]]), op=ALU.add),
                          reads=[f"a_dps{og}", "a_es"], writes=["a_den"])
                    kb.op('dve', lambda e: e.reciprocal(out=den[:], in_=den[:]), reads=["a_den"], writes=["a_den"])
                    kb.op('dve', lambda e: e.tensor_tensor(out=aT[:, :, n * 128:(n + 1) * 128],
                                                           in0=o_ps[og][0:64, 0:384].rearrange("p (g t) -> p g t", g=3), in1=den[:], op=ALU.mult),
                          reads=[f"a_ops{og}", "a_den"], writes=["a_aT"])
                for gi in range(3):
                    hq = kvh * 3 + gi
                    kb.dma(g.ymT[hq * 64:(hq + 1) * 64, c0 + TC:c0 + TS], aT[:, gi, :], reads=["a_aT"], writes=["ymT"])


def stage_o3(kb, g):
    with kb.stage() as st:
        Z = kb.sb("f_Z", [128, 16, 256], BF16)
        tb = [[kb.sb(f"f_tb{j}{i}", [128, 16, 512], BF16) for i in range(2)] for j in range(2)]
        c64 = kb.sb("f_c64", [128, 2, 128], F32)
        c64b = kb.sb("f_c64b", [128, 2, 128], BF16)
        kb.dma(c64[:], g.dft64[:], writes=["f_c64"])
        kb.op('dve', lambda e: e.tensor_copy(out=c64b[:], in_=c64[:]), reads=["f_c64"], writes=["f_c64b"])
        PQ = [kb.sb(f"f_PQ{j}", [128, 512], BF16) for j in range(2)]
        yv = [kb.sb(f"f_y{i}", [128, 512], BF16) for i in range(2)]
        pq_ps = [kb.ps(f"f_pqps{j}", [128, 512], F32) for j in range(2)]
        y_ps = [kb.ps(f"f_yps{i}", [128, 512], F32) for i in range(2)]
        ny = 0
        nb_ = 0
        for b in range(NB):
            c0 = b * TS + TC
            zsrc = AP(g.ztok.tensor, g.ztok.offset + c0 * 256, [[256, 128], [128 * 256, 16], [1, 256]])
            kb.dma(Z[:], zsrc, reads=["ztok"], writes=["f_Z"])
            for tblk in range(4):
                bi = nb_ % 2
                nb_ += 1
                for j in range(2):
                    kb.dma(tb[j][bi][:], g.dftT[j, :, tblk * 512:(tblk + 1) * 512].rearrange("(sc p) t -> p sc t", p=128),
                           writes=[f"f_tb{j}{bi}"])
                for cc in range(2):
                    for j in range(2):
                        for sc in range(16):
                            kb.op('pe', lambda e: e.matmul(pq_ps[j][:], lhsT=Z[:, sc, cc * 128:(cc + 1) * 128], rhs=tb[j][bi][:, sc, :],
                                                           start=(sc == 0), stop=(sc == 15)),
                                  reads=["f_Z", f"f_tb{j}{bi}"], writes=[f"f_pqps{j}"])
                        kb.op('act' if j == 0 else 'dve',
                              (lambda e: e.activation(out=PQ[0][:], in_=pq_ps[0][:], func=AF.Copy)) if j == 0 else
                              (lambda e: e.tensor_copy(out=PQ[1][:], in_=pq_ps[1][:])),
                              reads=[f"f_pqps{j}"], writes=[f"f_PQ{j}"])
                    yi = ny % 2
                    ny += 1
                    kb.op('pe', lambda e: e.matmul(y_ps[yi][:], lhsT=c64b[:, 0, :], rhs=PQ[0][:], start=True, stop=False),
                          reads=["f_c64b", "f_PQ0"], writes=[f"f_yps{yi}"])
                    kb.op('pe', lambda e: e.matmul(y_ps[yi][:], lhsT=c64b[:, 1, :], rhs=PQ[1][:], start=False, stop=True),
                          reads=["f_c64b", "f_PQ1"], writes=[f"f_yps{yi}"])
                    kb.op('act', lambda e: e.activation(out=yv[yi][:], in_=y_ps[yi][:], func=AF.Copy), reads=[f"f_yps{yi}"], writes=[f"f_y{yi}"])
                    kb.dma(g.ymT[768 + cc * 128:768 + (cc + 1) * 128, c0 + tblk * 512:c0 + (tblk + 1) * 512], yv[yi][:],
                           reads=[f"f_y{yi}"], writes=["ymT"])


def stage_o4(kb, g, l, src, dst):
    with kb.stage() as st:
        op = OutProj(kb, g, l, g.o_w_out, 2, "o4")
        yT = [kb.sb(f"o4_yT{i}", [128, 8, 512], BF16) for i in range(2)]
        n = 0
        for b in range(NB):
            for q4 in range(LT // 4):
                t0 = b * ST_ + 2 + q4 * 4
                i = n % 2
                n += 1
                kb.dma(yT[i][:], g.ymT[:, t0 * 128:t0 * 128 + 512].rearrange("(k p) t -> p k t", p=128), reads=["ymT"], writes=[f"o4_yT{i}"])
                op.run(yT[i], f"o4_yT{i}", None, src, dst, tiles=[t0 + j for j in range(4)])


IN_SPECS = {
    "s_in": ([NT, D], F32),
    "cT": ([128, 8, 3], F32),
    "w_mod": ([2, D, 6 * D], F32),
    "b_mod": ([2, 6 * D], F32),
    "gcol": ([128, 2, 2, 8], F32),
    "lbl": ([128, 2, 2, 4], F32),
    "identf": ([128, 128], F32),
    "e_w_in": ([D, 3584], F32),
    "hmask": ([64, 2, 64], F32),
    "rg_blk": ([128, 16, 128], F32),
    "rg_sm": ([128, 44], F32),
    "e_w_out": ([D, D], F32),
    "hg_gain": ([128, 1], F32),
    "p_wq": ([2, D, 2048], F32),
    "keysT": ([2, 128, 16, 128], F32),
    "bqc": ([2, 128, 16], F32),
    "gffn": ([2, D], F32),
    "p_u0": ([16384, D], F32),
    "p_u1": ([16384, D], F32),
    "p_v0": ([16384, D], F32),
    "p_v1": ([16384, D], F32),
    "o_w_in": ([D, 1536], F32),
    "o_w_out": ([D, D], F32),
    "q_gain": ([64], F32),
    "k_gain": ([64], F32),
    "sinks": ([12], F32),
    "rope": ([TS, 2, 32], F32),
    "amask": ([128, 2, 128], F32),
    "dft64": ([128, 2, 128], F32),
    "dftT": ([2, TL, TL], BF16),
}


def build(upto="all", dbg=(), ptiles=None, ptiles1=None):
    nc = bass.Bass("TRN2", target_bir_lowering=False)
    kb = KB(nc)
    g = Ctx()
    g.ptiles = ptiles
    g.ptiles1 = ptiles1
    g.dbg = set(dbg)
    for name, (shape, dt) in IN_SPECS.items():
        setattr(g, name, nc.dram_tensor(name, list(shape), dt, kind="ExternalInput").ap())

    def scratch(name, shape, dt):
        kind = "ExternalOutput" if name in g.dbg else "Internal"
        t = nc.dram_tensor(name, list(shape), dt, kind=kind).ap()
        setattr(g, name, t)
        return t

    scratch("mrow", [2, 3, 6 * D], F32)
    g.out = nc.dram_tensor("out", [NB * TL, D], F32, kind="ExternalOutput").ap()

    scratch("fm", [3072, NT], F32)
    scratch("vtok", [NT, 512], BF16)
    g.stream = g.s_in
    scratch("uv0", [16384, 2 * D], BF16)
    scratch("uv1", [16384, 2 * D], BF16)
    scratch("oT", [512, NT], F32)
    scratch("ybT", [512, NT], BF16)
    scratch("s1", [NT, D], F32)
    scratch("s2", [NT, D], F32)
    scratch("qkT", [16, 64, NT], BF16)
    scratch("vtok2", [NT, 256], BF16)
    scratch("ztok", [NT, 256], BF16)
    scratch("ymT", [D, NT], BF16)
    scratch("s3", [NT, D], F32)
    alloc_globals(kb, g)
    g.bg = None
    if upto.startswith("odd"):
        with kb.stage() as outer0:
            stage_mod(kb, g)
            stage_prep(kb, g)
    else:
      with kb.stage() as outer0:
        g.bg = UVConv(kb, g, 0)
        stage_mod(kb, g)
        stage_prep(kb, g)
        stage_e1(kb, g)
        if upto != "e1":
            stage_e2(kb, g)
        if upto not in ("e1", "e2"):
            stage_e3(kb, g)
        if upto not in ("e1", "e2", "e3"):
            stage_e4(kb, g, 0, g.s_in, g.s1)
        if upto not in ("e1", "e2", "e3", "e4"):
            g.bg.finish()
        g.bg = None
    if upto in ("e1", "e2", "e3", "e4"):
        return finish(kb, nc)
    if True:
        pass
    g.out_toks = []
    ptiles = list(range(NTILE)) if g.ptiles is None else g.ptiles
    if not upto.startswith("odd"):
        stage_peer(kb, g, 0, g.s1, g.s2, ptiles, lambda t: t * 128)
    if upto == "p0":
        return finish(kb, nc)
    s2 = g.s_in if upto.startswith("odd") else g.s2
    with kb.stage() as outer1:
        g.bg = UVConv(kb, g, 1)
        stage_o1(kb, g, 1, s2)
        if upto != "odd1":
            stage_o2(kb, g)
        if upto not in ("odd1", "odd2"):
            stage_o3(kb, g)
        if upto not in ("odd1", "odd2", "odd3"):
            stage_o4(kb, g, 1, s2, g.s3)
        if not upto.startswith("odd"):
            g.bg.finish()
        g.bg = None
    if upto.startswith("odd"):
        return finish(kb, nc)
    lat_tiles = [b * ST_ + 2 + j for b in range(NB) for j in range(LT)]
    g.out_toks = []
    stage_peer(kb, g, 1, g.s3, g.out, lat_tiles if g.ptiles1 is None else g.ptiles1,
               lambda t: (t // ST_) * TL + (t % ST_ - 2) * 128)
    return finish(kb, nc)


def finish(kb, nc):
    kb.barrier()
    print(f"[build] inst={kb.n_inst} waits={kb.n_wait} dmas={kb.ndma}")
    kb.close()
    return nc


def colform(v):
    v = np.asarray(v, np.float32)
    return np.ascontiguousarray(v.reshape(-1, 128).T)


_SHARED = {}


def prep_core(inp, core):
    b0 = core * NB
    key = id(inp)
    if key not in _SHARED:
        _SHARED.clear()
        _SHARED[key] = _prep_shared(inp)
    m = dict(_SHARED[key])
    s = np.concatenate([np.concatenate([inp["ctx"][b0 + i], inp["x"][b0 + i]], axis=0) for i in range(NB)], axis=0)
    m["s_in"] = np.ascontiguousarray(s, np.float32)
    cv = np.stack([inp["c"][b0], inp["c"][b0 + 1], inp["c_ctx"]], axis=1)
    m["cT"] = np.ascontiguousarray(cv.reshape(8, 128, 3).transpose(1, 0, 2), np.float32)
    return m


def _prep_shared(inp):
    m = {}
    m["w_mod"] = np.ascontiguousarray(inp["w_mod"], np.float32)
    m["b_mod"] = np.ascontiguousarray(inp["b_mod"], np.float32)
    gc = np.zeros((128, 2, 2, 8), np.float32)
    for l in range(2):
        gc[:, l, 0, :] = colform(inp["g_mix"][l])
        gc[:, l, 1, :] = colform(inp["g_ffn"][l])
    m["gcol"] = gc
    lbl = np.zeros((128, 2, 2, 4), np.float32)
    for d in range(2):
        for j in range(2):
            lbl[:, d, j, :] = colform(inp["hg_lb_logits"][d, j])
    m["lbl"] = lbl
    m["identf"] = np.eye(128, dtype=np.float32)
    hm = np.zeros((64, 2, 64), np.float32)
    ii = np.arange(64)
    hm[:, 0, :] = (ii[:, None] <= ii[None, :])
    hm[:, 1, :] = (ii[:, None] >= ii[None, :])
    m["hmask"] = hm
    blk = np.zeros((128, 16, 128), np.float32)
    for ct in range(4):
        for ty, wname in enumerate(("rg_wa", "rg_wx")):
            for d in range(2):
                wi = (ct * 2 + ty) * 2 + d
                for half in range(2):
                    blk[half * 64:(half + 1) * 64, wi, half * 64:(half + 1) * 64] = inp[wname][0, d, 2 * ct + half]
    m["rg_blk"] = blk
    sm = np.zeros((128, 44), np.float32)
    for j in range(4):
        sm[:, j * 4:(j + 1) * 4] = colform(inp["rg_conv_w"][0, j])
    sm[:, 16:20] = colform(inp["rg_conv_b"][0])
    for d in range(2):
        sm[:, 20 + d * 4:24 + d * 4] = colform(inp["rg_ba"][0, d])
        sm[:, 28 + d * 4:32 + d * 4] = colform(inp["rg_bx"][0, d])
        sm[:, 36 + d * 4:40 + d * 4] = colform(inp["rg_lambda"][0, d])
    m["rg_sm"] = sm
    m["e_w_out"] = np.ascontiguousarray(inp["e_w_out"][0], np.float32)
    m["p_wq"] = np.ascontiguousarray(inp["p_wq"], np.float32)
    m["keysT"] = np.ascontiguousarray(inp["p_keys"].reshape(2, 16, 128, 128).transpose(0, 3, 1, 2), np.float32)
    m["bqc"] = np.ascontiguousarray(inp["p_bq"].reshape(2, 16, 128).transpose(0, 2, 1), np.float32)
    m["gffn"] = np.ascontiguousarray(inp["g_ffn"], np.float32)
    for l in range(2):
        m[f"p_u{l}"] = np.ascontiguousarray(inp["p_u"][l], np.float32)
        m[f"p_v{l}"] = np.ascontiguousarray(inp["p_v"][l], np.float32)
    m["o_w_in"] = np.ascontiguousarray(inp["o_w_in"][0], np.float32)
    m["o_w_out"] = np.ascontiguousarray(inp["o_w_out"][0], np.float32)
    m["q_gain"] = np.ascontiguousarray(inp["q_gain"][0], np.float32)
    m["k_gain"] = np.ascontiguousarray(inp["k_gain"][0], np.float32)
    m["sinks"] = np.ascontiguousarray(inp["sinks"][0], np.float32)
    m.update(const_tables())
    m["hg_gain"] = np.ascontiguousarray(inp["hg_gain"][0].reshape(128, 1), np.float32)
    m["e_w_in"] = np.ascontiguousarray(inp["e_w_in"][0], np.float32)
    return m


_CONST = {}


def const_tables():
    if _CONST:
        return _CONST
    import ml_dtypes
    rope = np.zeros((TS, 2, 32), np.float32)
    rope[:TC, 0, :] = 1.0
    pos = np.arange(TL)
    row = (pos // 64).astype(np.float64)
    colp = (pos % 64).astype(np.float64)
    inv = 10000.0 ** (-np.arange(16, dtype=np.float64) / 16)
    ang = np.concatenate([row[:, None] * inv, colp[:, None] * inv], axis=-1)
    rope[TC:, 0, :] = np.cos(ang)
    rope[TC:, 1, :] = np.sin(ang)
    _CONST["rope"] = rope
    ii = np.arange(128)
    am = np.zeros((128, 2, 128), np.float32)
    am[:, 0, :] = (ii[:, None] >= ii[None, :])
    am[:, 1, :] = (ii[:, None] <= ii[None, :])
    _CONST["amask"] = am
    k64 = np.arange(64)
    a64 = 2 * np.pi * np.outer(k64, k64) / 64
    c64 = np.cos(a64) / 8.0
    s64 = np.sin(a64) / 8.0
    d64 = np.zeros((128, 2, 128), np.float32)
    for hlf in range(2):
        d64[hlf * 64:(hlf + 1) * 64, 0, hlf * 64:(hlf + 1) * 64] = c64
        d64[hlf * 64:(hlf + 1) * 64, 1, hlf * 64:(hlf + 1) * 64] = -s64
    _CONST["dft64"] = d64
    kT = np.arange(TL)
    aT = 2 * np.pi * ((np.outer(kT, kT) % TL).astype(np.float64)) / TL
    dT = np.stack([np.cos(aT), np.sin(aT)], 0) / np.sqrt(TL)
    _CONST["dftT"] = dT.astype(ml_dtypes.bfloat16)
    return _CONST


_PROG = {}


def kernel(**inputs):
    inp = {k: np.asarray(v) for k, v in inputs.items()}
    if "nc" not in _PROG:
        _PROG["nc"] = build("all")
    nc = _PROG["nc"]
    maps = [prep_core(inp, c) for c in range(8)]
    res = run_bass_kernel_spmd(nc, maps, core_ids=list(range(8)))
    out = np.concatenate([np.asarray(r["out"]).reshape(NB, TL, D) for r in res.results], axis=0)
    return np.ascontiguousarray(out, dtype=np.float32)
```
